# Optimizing a Trainium2 kernel written in Bass

```python
import math
import jax, jax.numpy as jnp
from jax import lax
import numpy as np

D_MODEL = 1024
BATCH = 4
SEQ = 8192
DEPTH = 1

GRID_W = 64
CTX_LEN = 256
N_HEADS = 8
N_KV_HEADS = 2
HEAD_DIM = 128
Q_GROUP = N_HEADS // N_KV_HEADS
ATTN_WIDTH = N_HEADS * HEAD_DIM
KV_WIDTH = N_KV_HEADS * HEAD_DIM
ROPE_AXIS_DIM = HEAD_DIM // 2
ROPE_THETA = 10000.0
Q_BLOCK = 128
D_HYENA = D_MODEL // 2
HYENA_ORDER = 2
FILTER_EMB_DIM = 33
FILTER_HIDDEN = 64
FILTER_INNER = 2
DECAY_TARGET = 1e-2
FAST_DECAY_PCT = 0.3
SLOW_DECAY_PCT = 1.5
N_BRANCHES = 2
D_FF = 2816
CONV_W = 3
RMS_EPS = 1e-6
PROJ_WIDTH = ATTN_WIDTH + 2 * KV_WIDTH + (HYENA_ORDER + 1) * D_HYENA + N_BRANCHES * D_MODEL
N_MOD = 6

kernel_name = 'hybrid_gqa_hyena_convffn_prefix_dit'


def rmsnorm(x, w):
    xf = x.astype(jnp.float32)
    y = xf * lax.rsqrt(jnp.mean(xf * xf, axis=-1, keepdims=True) + RMS_EPS)
    return (y * w.astype(jnp.float32)).astype(x.dtype)


def modulation(cvec, w_mod, b_mod):
    m = jax.nn.silu(cvec) @ w_mod + b_mod
    return jnp.split(m[:, None, :], N_MOD, axis=-1)


def modulate(x, w, shift, scale):
    return rmsnorm(x, w) * (1.0 + scale) + shift


def dwconv3(u, w, b):
    up = jnp.pad(u, ((0, 0), (1, 1), (0, 0)))
    return up[:, :-2] * w[0] + up[:, 1:-1] * w[1] + up[:, 2:] * w[2] + b


def axial_rope_tables(L):
    rows = L // GRID_W
    row = jnp.repeat(jnp.arange(rows), GRID_W).astype(jnp.float32)
    col = jnp.tile(jnp.arange(GRID_W), rows).astype(jnp.float32)
    freqs = ROPE_THETA ** (-jnp.arange(0, ROPE_AXIS_DIM, 2, dtype=jnp.float32) / ROPE_AXIS_DIM)
    ang = jnp.concatenate([row[:, None] * freqs, col[:, None] * freqs], axis=-1)
    return jnp.cos(ang), jnp.sin(ang)


def apply_rope(x, cos, sin):
    xr = x.reshape(x.shape[:-1] + (HEAD_DIM // 2, 2))
    x1, x2 = xr[..., 0], xr[..., 1]
    cs = cos[None, :, None, :].astype(x.dtype)
    sn = sin[None, :, None, :].astype(x.dtype)
    return jnp.stack([x1 * cs - x2 * sn, x1 * sn + x2 * cs], axis=-1).reshape(x.shape)


def split_proj(p):
    B, L = p.shape[:2]
    q, k, v, hy, gates = jnp.split(
        p, [ATTN_WIDTH, ATTN_WIDTH + KV_WIDTH, ATTN_WIDTH + 2 * KV_WIDTH,
            ATTN_WIDTH + 2 * KV_WIDTH + (HYENA_ORDER + 1) * D_HYENA], axis=-1)
    q = q.reshape(B, L, N_HEADS, HEAD_DIM)
    k = k.reshape(B, L, N_KV_HEADS, HEAD_DIM)
    v = v.reshape(B, L, N_KV_HEADS, HEAD_DIM)
    return q, k, v, hy, gates


def attend(q, k_all, v_all):
    B, Lq = q.shape[:2]
    nblk = Lq // Q_BLOCK
    qb = q.reshape(B, nblk, Q_BLOCK, N_KV_HEADS, Q_GROUP, HEAD_DIM).transpose(1, 0, 2, 3, 4, 5)
    scale = HEAD_DIM ** -0.5

    def one_block(qi):
        s = jnp.einsum('bqhgd,bkhd->bhgqk', qi, k_all, preferred_element_type=jnp.float32) * scale
        p = jax.nn.softmax(s, axis=-1).astype(v_all.dtype)
        return jnp.einsum('bhgqk,bkhd->bqhgd', p, v_all)

    o = lax.map(one_block, qb)
    return o.transpose(1, 0, 2, 3, 4, 5).reshape(B, Lq, ATTN_WIDTH)


def hyena_filters(L, w1, b1, w_inner, b_inner, freq, w_out):
    t_idx = jnp.arange(L, dtype=jnp.float32)
    t = t_idx / max(L - 1, 1)
    bands = (FILTER_EMB_DIM - 1) // 2
    f = jnp.linspace(1e-4, bands - 1, bands, dtype=jnp.float32)
    wpos = 2.0 * math.pi * t_idx / L
    z = jnp.concatenate([t[:, None], jnp.cos(wpos[:, None] * f), -jnp.sin(wpos[:, None] * f)], axis=-1)
    h = jnp.sin(freq * (z @ w1 + b1))
    for j in range(FILTER_INNER):
        h = jnp.sin(freq * (h @ w_inner[j] + b_inner[j]))
    h = h @ w_out
    min_decay = math.log(DECAY_TARGET) / SLOW_DECAY_PCT
    max_decay = math.log(DECAY_TARGET) / FAST_DECAY_PCT
    deltas = jnp.linspace(min_decay, max_decay, D_HYENA, dtype=jnp.float32)
    decay = jnp.exp(-t[:, None] * jnp.abs(deltas))
    h = (h.reshape(L, 2, HYENA_ORDER, D_HYENA) * decay[:, None, None, :]).astype(jnp.float32)
    h_fwd, h_bwd = h[:, 0], h[:, 1]
    k = jnp.concatenate([h_fwd[:1] + h_bwd[:1], h_fwd[1:],
                         jnp.zeros((1, HYENA_ORDER, D_HYENA), jnp.float32), h_bwd[:0:-1]], axis=0)
    return jnp.fft.rfft(k, axis=0)


def fft_long_conv(u, K):
    L = u.shape[1]
    U = jnp.fft.rfft(u.astype(jnp.float32), n=2 * L, axis=1)
    y = jnp.fft.irfft(U * K[None], n=2 * L, axis=1)[:, :L]
    return y.astype(u.dtype)


def hyena(streams, conv_w, conv_b, w1, b1, w_inner, b_inner, freq, w_out, skip):
    L = streams.shape[1]
    u = dwconv3(streams, conv_w, conv_b)
    x1, x2, v = jnp.split(u, HYENA_ORDER + 1, axis=-1)
    K = hyena_filters(L, w1, b1, w_inner, b_inner, freq, w_out)
    z = v
    for o, gate in enumerate((x1, x2)):
        z = gate * (fft_long_conv(z, K[:, o]) + skip[o] * z)
    return z


def merge(attn_o, hy_o, gates, w_attn_out, w_hy_out, w_o, b_o):
    g_attn, g_hy = jnp.split(gates, N_BRANCHES, axis=-1)
    mixed = jax.nn.sigmoid(g_attn) * (attn_o @ w_attn_out) + jax.nn.sigmoid(g_hy) * (hy_o @ w_hy_out)
    return mixed @ w_o + b_o


def conv_ffn(h, w_up, b_up, conv_w, conv_b, w_down, b_down):
    u = dwconv3(h @ w_up + b_up, conv_w, conv_b)
    a, g = jnp.split(u, 2, axis=-1)
    return (jax.nn.gelu(a) * g) @ w_down + b_down


def setup_inputs(seed: int = 0) -> dict:
    key = jax.random.key(seed)
    ks = jax.random.split(key, 32)

    def nrm(k, shape, scale):
        return jax.random.normal(k, shape, jnp.float32) * scale

    def gain(k, shape):
        return 1.0 + nrm(k, shape, 0.02)

    L = DEPTH
    return {
        'x': nrm(ks[0], (BATCH, SEQ, D_MODEL), 1.0),
        'c': nrm(ks[1], (BATCH, D_MODEL), 1.0),
        'ctx': nrm(ks[2], (BATCH, CTX_LEN, D_MODEL), 1.0),
        'c_ctx': nrm(ks[3], (D_MODEL,), 1.0),
        'w_mod': nrm(ks[4], (L, D_MODEL, N_MOD * D_MODEL), 0.5 * D_MODEL ** -0.5),
        'b_mod': nrm(ks[5], (L, N_MOD * D_MODEL), 0.02),
        'norm1_w': gain(ks[6], (L, D_MODEL)),
        'norm2_w': gain(ks[7], (L, D_MODEL)),
        'w_in': nrm(ks[8], (L, D_MODEL, PROJ_WIDTH), D_MODEL ** -0.5),
        'b_in': nrm(ks[9], (L, PROJ_WIDTH), 0.02),
        'q_norm_w': gain(ks[10], (L, HEAD_DIM)),
        'k_norm_w': gain(ks[11], (L, HEAD_DIM)),
        'hy_conv_w': nrm(ks[12], (L, CONV_W, (HYENA_ORDER + 1) * D_HYENA), CONV_W ** -0.5),
        'hy_conv_b': nrm(ks[13], (L, (HYENA_ORDER + 1) * D_HYENA), 0.02),
        'filt_w1': nrm(ks[14], (L, FILTER_EMB_DIM, FILTER_HIDDEN), FILTER_EMB_DIM ** -0.5),
        'filt_b1': nrm(ks[15], (L, FILTER_HIDDEN), 0.02),
        'filt_w_inner': nrm(ks[16], (L, FILTER_INNER, FILTER_HIDDEN, FILTER_HIDDEN), FILTER_HIDDEN ** -0.5),
        'filt_b_inner': nrm(ks[17], (L, FILTER_INNER, FILTER_HIDDEN), 0.02),
        'filt_freq': gain(ks[18], (L, FILTER_HIDDEN)),
        'filt_w_out': nrm(ks[19], (L, FILTER_HIDDEN, 2 * HYENA_ORDER * D_HYENA), 0.1 * FILTER_HIDDEN ** -0.5),
        'hy_skip': nrm(ks[20], (L, HYENA_ORDER, D_HYENA), 0.1),
        'w_attn_out': nrm(ks[21], (L, ATTN_WIDTH, D_MODEL), ATTN_WIDTH ** -0.5),
        'w_hy_out': nrm(ks[22], (L, D_HYENA, D_MODEL), D_HYENA ** -0.5),
        'w_o': nrm(ks[23], (L, D_MODEL, D_MODEL), D_MODEL ** -0.5),
        'b_o': nrm(ks[24], (L, D_MODEL), 0.02),
        'w_up': nrm(ks[25], (L, D_MODEL, 2 * D_FF), D_MODEL ** -0.5),
        'b_up': nrm(ks[26], (L, 2 * D_FF), 0.02),
        'ffn_conv_w': nrm(ks[27], (L, CONV_W, 2 * D_FF), CONV_W ** -0.5),
        'ffn_conv_b': nrm(ks[28], (L, 2 * D_FF), 0.02),
        'w_down': nrm(ks[29], (L, D_FF, D_MODEL), D_FF ** -0.5),
        'b_down': nrm(ks[30], (L, D_MODEL), 0.02),
        'final_norm_w': gain(ks[31], (D_MODEL,)),
    }


def reference(x, c, ctx, c_ctx, w_mod, b_mod, norm1_w, norm2_w, w_in, b_in, q_norm_w, k_norm_w,
              hy_conv_w, hy_conv_b, filt_w1, filt_b1, filt_w_inner, filt_b_inner, filt_freq,
              filt_w_out, hy_skip, w_attn_out, w_hy_out, w_o, b_o, w_up, b_up, ffn_conv_w,
              ffn_conv_b, w_down, b_down, final_norm_w):
    cos, sin = axial_rope_tables(x.shape[1])
    xc = ctx
    for i in range(DEPTH):
        update_ctx = i < DEPTH - 1
        sh1, sc1, g1, sh2, sc2, g2 = modulation(c, w_mod[i], b_mod[i])
        csh1, csc1, cg1, csh2, csc2, cg2 = modulation(c_ctx[None], w_mod[i], b_mod[i])

        def hyena_i(streams):
            return hyena(streams, hy_conv_w[i], hy_conv_b[i], filt_w1[i], filt_b1[i], filt_w_inner[i],
                         filt_b_inner[i], filt_freq[i], filt_w_out[i], hy_skip[i])

        def merge_i(attn_o, hy_o, gates):
            return merge(attn_o, hy_o, gates, w_attn_out[i], w_hy_out[i], w_o[i], b_o[i])

        def ffn_i(h):
            return conv_ffn(h, w_up[i], b_up[i], ffn_conv_w[i], ffn_conv_b[i], w_down[i], b_down[i])

        h = modulate(x, norm1_w[i], sh1, sc1)
        hc = modulate(xc, norm1_w[i], csh1, csc1)
        q, k, v, hy, gates = split_proj(h @ w_in[i] + b_in[i])
        qc, kc, vc, hyc, gatesc = split_proj(hc @ w_in[i] + b_in[i])
        q = apply_rope(rmsnorm(q, q_norm_w[i]), cos, sin)
        k = apply_rope(rmsnorm(k, k_norm_w[i]), cos, sin)
        kc = rmsnorm(kc, k_norm_w[i])
        k_all = jnp.concatenate([kc, k], axis=1)
        v_all = jnp.concatenate([vc, v], axis=1)
        attn_o = attend(q, k_all, v_all)
        hy_o = hyena_i(hy)
        x_new = x + g1 * merge_i(attn_o, hy_o, gates)
        x_new = x_new + g2 * ffn_i(modulate(x_new, norm2_w[i], sh2, sc2))

        if update_ctx:
            qc = rmsnorm(qc, q_norm_w[i])
            attn_c = attend(qc, kc, vc)
            hy_c = hyena_i(hyc)
            xc = xc + cg1 * merge_i(attn_c, hy_c, gatesc)
            xc = xc + cg2 * ffn_i(modulate(xc, norm2_w[i], csh2, csc2))
        x = x_new
    return rmsnorm(x, final_norm_w)
```

```python
import math
from contextlib import ExitStack
import numpy as np
import ml_dtypes
import concourse.bass as bass
import concourse.mybir as mybir
from concourse.bass_utils import run_bass_kernel_spmd

F32 = mybir.dt.float32
BF16 = mybir.dt.bfloat16
AF = mybir.ActivationFunctionType
ALU = mybir.AluOpType
AX = mybir.AxisListType

D = 1024
T = 8192
NCTX = 256
NKEY = T + NCTX
NKT = NKEY // 128
NQT = 33
NQ = NQT * 128
DFF = 2816
EPS = 1e-6


class Buf:
    def __init__(self, t, name):
        self.t = t
        self.name = name
        self.writer = {}
        self.readers = {}
        self.dsem = None
        self.dcount = 0

    def __getitem__(self, idx):
        return self.t[idx]


def _merge(d, toks):
    for k_, v in toks.items():
        if d.get(k_, 0) < v:
            d[k_] = v


class KB:
    def __init__(self, nc, stack):
        self.nc = nc
        self.gstack = stack
        self.stack = stack
        self.eng = {"pe": nc.tensor, "act": nc.scalar, "dve": nc.vector, "pool": nc.gpsimd, "sp": nc.sync}
        self.sems = {}
        self.cnt = {}
        for e in ("pe", "act", "dve", "pool"):
            self.sems[e] = stack.enter_context(nc.semaphore("s_" + e))
            self.cnt[e] = 0
        self.waited = {e: {} for e in self.eng}
        self.nbuf = 0
        self.ninst = 0
        self.dsem_live = {}

    def sbuf(self, shape, dtype, name):
        self.nbuf += 1
        name = f"{name}_{self.nbuf}"
        t = self.stack.enter_context(self.nc.sbuf_tensor(name, list(shape), dtype))
        return Buf(t, name)

    def view(self, buf, name):
        self.nbuf += 1
        return Buf(buf.t, f"{name}_{self.nbuf}")

    def psum(self, shape, dtype, name):
        self.nbuf += 1
        name = f"{name}_{self.nbuf}"
        t = self.stack.enter_context(self.nc.psum_tensor(name, list(shape), dtype))
        return Buf(t, name)

    def dram(self, name, shape, dtype, kind=None):
        if kind is None:
            t = self.nc.dram_tensor(name, list(shape), dtype)
        else:
            t = self.nc.dram_tensor(name, list(shape), dtype, kind=kind)
        return Buf(t.ap(), name)

    def _wait(self, e, deps):
        w = self.waited[e]
        for k_, v in deps.items():
            if w.get(k_, 0) >= v:
                continue
            self.eng[e].wait_ge(self.sems[k_], v)
            w[k_] = v
            self.ninst += 1

    def op(self, e, fn, reads=(), writes=()):
        deps = {}
        for b in reads:
            _merge(deps, b.writer)
        for b in writes:
            for src in (b.writer, b.readers):
                for k_, v in src.items():
                    if deps.get(k_, 0) < v:
                        deps[k_] = v
        if e == "pe":
            deps.pop("pe", None)
        self._wait(e, deps)
        inst = fn(self.eng[e])
        self.cnt[e] += 1
        inst.then_inc(self.sems[e], 1)
        self.ninst += 1
        tok = {e: self.cnt[e]}
        for b in reads:
            _merge(b.readers, tok)
        for b in writes:
            b.writer = dict(tok)
            b.readers = {}
        return tok

    def dma(self, q, sb, dr, parts, load=True):
        if sb.dsem is None:
            key = "d_" + sb.name
            self.sems[key] = self.stack.enter_context(self.nc.semaphore(key))
            sb.dsem = key
            self.dsem_live[key] = sb
        deps = {}
        if load:
            _merge(deps, dr.writer)
            _merge(deps, sb.writer)
            _merge(deps, sb.readers)
        else:
            _merge(deps, sb.writer)
            _merge(deps, dr.readers)
        if sb.dcount:
            _merge(deps, {sb.dsem: sb.dcount})
        self._wait(q, deps)
        for (o, i) in parts:
            self.eng[q].dma_start(out=o, in_=i).then_inc(self.sems[sb.dsem], 16)
            sb.dcount += 16
            self.ninst += 1
        tok = {sb.dsem: sb.dcount}
        if load:
            sb.writer = dict(tok)
            sb.readers = {}
            _merge(dr.readers, tok)
        else:
            _merge(sb.readers, tok)
            _merge(dr.writer, tok)
        return tok

    def barrier(self):
        allc = {e: c for e, c in self.cnt.items() if c}
        for key, sb in self.dsem_live.items():
            if sb.dcount:
                allc[key] = sb.dcount
        for e in self.eng:
            self._wait(e, dict(allc))

    def end_phase(self, drams=()):
        self.barrier()
        for d in drams:
            d.writer = {}
            d.readers = {}
        self.dsem_live = {}

    def mm(self, out, lhsT, rhs, start, stop, reads, writes):
        return self.op("pe", lambda e: e.matmul(out, lhsT, rhs, start=start, stop=stop), reads, writes)

    def tr(self, out, in_, ident_ap, reads, writes):
        return self.op("pe", lambda e: e.transpose(out, in_, ident_ap), reads, writes)

    def act(self, out, in_, func, reads, writes, **kw):
        return self.op("act", lambda e: e.activation(out=out, in_=in_, func=func, **kw), reads, writes)

    def tt(self, eng, out, in0, in1, op, reads, writes):
        return self.op(eng, lambda e: e.tensor_tensor(out=out, in0=in0, in1=in1, op=op), reads, writes)

    def ts(self, eng, out, in0, s1, s2, op0, op1, reads, writes):
        return self.op(eng, lambda e: e.tensor_scalar(out=out, in0=in0, scalar1=s1, scalar2=s2, op0=op0, op1=op1),
                       reads, writes)

    def stt(self, out, in0, scalar, in1, op0, op1, reads, writes):
        return self.op("dve", lambda e: e.scalar_tensor_tensor(out=out, in0=in0, scalar=scalar, in1=in1,
                                                                op0=op0, op1=op1), reads, writes)

    def copy(self, eng, out, in_, reads, writes):
        if eng == "act":
            return self.act(out, in_, AF.Identity, reads, writes)
        return self.op(eng, lambda e: e.tensor_copy(out=out, in_=in_), reads, writes)


class G:
    pass


VEC_COLS = {}
_off = 0
for _n, _w in [("norm1_w", 8), ("norm2_w", 8), ("b_in", 40), ("hcw0", 12), ("hcw1", 12), ("hcw2", 12),
               ("hcb", 12), ("b_up", 44), ("fcw0", 44), ("fcw1", 44), ("fcw2", 44), ("fcb", 44), ("b_mod", 48)]:
    VEC_COLS[_n] = (_off, _w)
    _off += _w
NV = _off


def vcol(name, j=0):
    return VEC_COLS[name][0] + j


def phase_0(k, g):
    nc = k.nc
    k.stack = k.gstack
    g.ident = k.sbuf([128, 128], BF16, "ident")
    g.vec = k.sbuf([128, NV], F32, "vec")
    g.mods = k.sbuf([128, 6, 8], F32, "mods")
    g.g1row = k.sbuf([128, 1024], F32, "g1row")
    g.g2row = k.sbuf([128, 1024], F32, "g2row")
    k.dma("sp", g.ident, g.d_ident, [(g.ident[:, :], g.d_ident[:, :])])
    k.dma("sp", g.vec, g.d_vec, [(g.vec[:, :], g.d_vec[:, :])])
    with ExitStack() as ps:
        k.stack = ps
        cv = k.sbuf([128, 8, 2], F32, "cv")
        sc = k.sbuf([128, 8, 2], BF16, "sc")
        screp = k.sbuf([128, 8, 128], BF16, "screp")
        wm = [k.sbuf([128, 8, 1024], BF16, f"wm{i}") for i in range(2)]
        mT = k.sbuf([128, 48, 2], F32, "mT")
        brow = k.sbuf([128, 1024], F32, "brow")
        pm = [k.psum([128, 512], F32, f"pm{i}") for i in range(2)]
        pr = [k.psum([128, 512], F32, f"pr{i}") for i in range(2)]
        k.dma("sp", cv, g.d_cvecT, [(cv[:, :, :], g.d_cvecT[:, :, :])])
        k.act(sc[:, :, :], cv[:, :, :], AF.Silu, [cv], [sc])
        k.copy("dve", screp[:, :, :], sc[:, :, 0:1].to_broadcast([128, 8, 128]), [sc], [screp])
        wv = g.d_w_mod.t.rearrange("(kc p) n -> p kc n", p=128)
        for piece in range(6):
            w = wm[piece % 2]
            k.dma("pool", w, g.d_w_mod, [(w[:, :, :], wv[:, :, piece * 1024:(piece + 1) * 1024])])
            if piece in (0, 1, 3, 4):
                p_ = pm[piece % 2]
                for j in range(8):
                    for kc in range(8):
                        k.mm(p_[:, j * 2:j * 2 + 2], w[:, kc, j * 128:(j + 1) * 128], sc[:, kc, :],
                             kc == 0, kc == 7, [w, sc], [p_])
                bm = g.vec[:, vcol("b_mod", piece * 8):vcol("b_mod", piece * 8) + 8]
                k.tt("dve", mT[:, piece * 8:(piece + 1) * 8, :],
                     p_[:, 0:16].rearrange("p (j two) -> p j two", two=2),
                     bm.unsqueeze(2).to_broadcast([128, 8, 2]), ALU.add, [p_, g.vec], [mT])
            else:
                dst = g.g1row if piece == 2 else g.g2row
                k.dma("sp", brow, g.d_b_mod, [(brow[:, :], g.d_b_mod[piece * 1024:(piece + 1) * 1024].partition_broadcast(128))])
                for half in range(2):
                    p_ = pr[half]
                    for kc in range(8):
                        k.mm(p_[:, :], screp[:, kc, :], w[:, kc, half * 512:(half + 1) * 512],
                             kc == 0, kc == 7, [w, screp], [p_])
                    k.tt("dve", dst[:, half * 512:(half + 1) * 512], p_[:, :], brow[:, half * 512:(half + 1) * 512],
                         ALU.add, [p_, brow], [dst])
        n1 = g.vec[:, vcol("norm1_w"):vcol("norm1_w") + 8]
        n2 = g.vec[:, vcol("norm2_w"):vcol("norm2_w") + 8]
        k.stt(g.mods[:, 0, :], mT[:, 8:16, 0], 1.0, n1, ALU.add, ALU.mult, [mT, g.vec], [g.mods])
        k.copy("dve", g.mods[:, 1, :], mT[:, 0:8, 0], [mT], [g.mods])
        k.stt(g.mods[:, 2, :], mT[:, 8:16, 1], 1.0, n1, ALU.add, ALU.mult, [mT, g.vec], [g.mods])
        k.copy("dve", g.mods[:, 3, :], mT[:, 0:8, 1], [mT], [g.mods])
        k.stt(g.mods[:, 4, :], mT[:, 32:40, 0], 1.0, n2, ALU.add, ALU.mult, [mT, g.vec], [g.mods])
        k.copy("dve", g.mods[:, 5, :], mT[:, 24:32, 0], [mT], [g.mods])
        k.end_phase()
    k.stack = k.gstack


def norm_tile(k, g, xt, xn, ss, rs, junk):
    k.act(junk[:, :], xt[:, :], AF.Square, [xt], [junk, ss], accum_out=ss[:, 0:1])
    k.act(rs[:, :], ss[:, :], AF.Sqrt, [ss], [rs], scale=1.0 / D, bias=g.epsc[:, 0:1])
    k.op("dve", lambda e: e.reciprocal(out=rs[:, :], in_=rs[:, :]), [rs], [rs])
    k.act(xn[:, :], xt[:, :], AF.Identity, [xt, rs], [xn], scale=rs[:, 0:1])


def phase_A(k, g):
    with ExitStack() as ps:
        k.stack = ps
        win = k.sbuf([128, 8, 5120], BF16, "win")
        wv = g.d_w_in.t.rearrange("(kc p) n -> p kc n", p=128)
        for lo, hi in [(1024, 2048), (2048, 3072), (0, 1024), (3072, 4096), (4096, 5120)]:
            k.dma("pool", win, g.d_w_in, [(win[:, :, lo:hi], wv[:, :, lo:hi])])
        brow = k.sbuf([128, 1536], F32, "brow")
        wrow = k.sbuf([128, 1280], F32, "wrow")
        k.dma("sp", brow, g.d_b_in, [(brow[:, :], g.d_b_in[0:1536].partition_broadcast(128))])
        k.dma("sp", wrow, g.d_qkw, [(wrow[:, :], g.d_qkw[0:1280].partition_broadcast(128))])
        NX = 3
        xts = [k.sbuf([128, 1024], F32, f"xt{i}") for i in range(NX)]
        xns = [k.sbuf([128, 1024], BF16, f"xn{i}") for i in range(2)]
        junk = k.sbuf([128, 1024], BF16, "junk")
        sss = [k.sbuf([128, 1], F32, f"ss{i}") for i in range(2)]
        rss = [k.sbuf([128, 1], F32, f"rs{i}") for i in range(2)]
        hTbig = k.sbuf([128, 2, 8, 512], BF16, "hT")
        hTt = [[k.view(hTbig, f"hTt{s}_{j}") for j in range(4)] for s in range(2)]
        ropes = [k.sbuf([128, 2, 64], F32, f"rope{i}") for i in range(5)]
        qks = [k.sbuf([128, 1280], F32, f"qk{i}") for i in range(2)]
        sq = k.sbuf([128, 1280], F32, "sq")
        ss10 = k.sbuf([128, 10], F32, "ss10")
        rs10 = k.sbuf([128, 10], F32, "rs10")
        tmpa = k.sbuf([128, 10, 64], F32, "tmpa")
        tmpb = k.sbuf([128, 10, 64], F32, "tmpb")
        qrs = [k.sbuf([128, 1280], BF16, f"qr{i}") for i in range(2)]
        vts = [k.sbuf([128, 2, 129], BF16, f"vt{i}") for i in range(2)]
        qTs = [k.sbuf([128, 8, 128], BF16, f"qT{i}") for i in range(2)]
        kTs = [k.sbuf([128, 2, 128], BF16, f"kT{i}") for i in range(2)]
        fms = [k.sbuf([128, 512], BF16, f"fm{i}") for i in range(4)]
        ptr = k.psum([128, 8, 128], BF16, "ptr")
        pq = [k.psum([128, 512], F32, f"pq{i}") for i in range(2)]
        pkv = k.psum([128, 512], F32, "pkv")
        pfm = [k.psum([128, 512], F32, f"pfm{i}") for i in range(2)]
        pqT = k.psum([128, 8, 128], BF16, "pqT")
        pkT = k.psum([128, 2, 128], BF16, "pkT")
        for v in vts:
            k.op("dve", lambda e, v=v: e.memset(v[:, :, 128:129], 1.0), [], [v])

        NT = NKT
        fm_count = [0]

        def is_q(ti):
            return 2 <= ti < 2 + NQT

        def stage_load(ti):
            xt = xts[ti % NX]
            if ti < 2:
                src, dbuf = g.d_ctx.t[ti * 128:(ti + 1) * 128, :], g.d_ctx
            else:
                src, dbuf = g.d_xl.t[(ti - 2) * 128:(ti - 1) * 128, :], g.d_xl
            k.dma("sp", xt, dbuf, [(xt[:, :], src)])
            if ti >= 2:
                rp = ropes[ti % 5]
                k.dma("sp", rp, g.d_rope, [(rp[:, :, :], g.d_rope.t[(ti - 2) * 128:(ti - 1) * 128, :, :])])

        def slot(ti):
            if ti < 2:
                return 1, ti
            t = ti - 2
            return (t // 4) % 2, t % 4

        def stage_norm(ti):
            xt = xts[ti % NX]
            xn = xns[ti % 2]
            norm_tile(k, g, xt, xn, sss[ti % 2], rss[ti % 2], junk)

        def stage_front(ti):
            xn = xns[ti % 2]
            for kc in range(8):
                k.tr(ptr[:, kc, :], xn[:, kc * 128:(kc + 1) * 128], g.ident[:, :], [xn, g.ident], [ptr])
            s, j = slot(ti)
            hb = hTt[s][j]
            mi = 2 if ti < 2 else 0
            for kc in range(8):
                k.act(hTbig[:, s, kc, j * 128:(j + 1) * 128], ptr[:, kc, :], AF.Identity, [ptr, g.mods], [hb],
                      scale=g.mods[:, mi, kc:kc + 1], bias=g.mods[:, mi + 1, kc:kc + 1])

        def stage_front2(ti):
            s, j = slot(ti)
            hb = hTt[s][j]
            if is_q(ti):
                for half in range(2):
                    for kc in range(8):
                        k.mm(pq[half][:, :], hTbig[:, s, kc, j * 128:(j + 1) * 128],
                             win[:, kc, half * 512:(half + 1) * 512], kc == 0, kc == 7, [hb, win], [pq[half]])
            for kc in range(8):
                k.mm(pkv[:, :], hTbig[:, s, kc, j * 128:(j + 1) * 128], win[:, kc, 1024:1536],
                     kc == 0, kc == 7, [hb, win], [pkv])
            qk = qks[ti % 2]
            vt = vts[ti % 2]
            if is_q(ti):
                for half in range(2):
                    k.tt("dve", qk[:, half * 512:(half + 1) * 512], pq[half][:, :], brow[:, half * 512:(half + 1) * 512],
                         ALU.add, [pq[half], brow], [qk])
            k.tt("dve", qk[:, 1024:1280], pkv[:, 0:256], brow[:, 1024:1280], ALU.add, [pkv, brow], [qk])
            k.tt("dve", vt[:, :, 0:128], pkv[:, 256:512].rearrange("p (g d) -> p g d", d=128),
                 brow[:, 1280:1536].rearrange("p (g d) -> p g d", d=128), ALU.add, [pkv, brow], [vt])
            k.dma("pool", vt, g.Vd, [(g.Vd.t[ti, :, :], vt[:, :, :].rearrange("p g c -> p (g c)"))], load=False)

        def stage_back(ti):
            qk = qks[ti % 2]
            qr = qrs[ti % 2]
            lo = 0 if is_q(ti) else 1024
            nh = (1280 - lo) // 128
            h0 = lo // 128
            k.tt("dve", sq[:, lo:1280], qk[:, lo:1280], qk[:, lo:1280], ALU.mult, [qk], [sq])
            k.op("dve", lambda e: e.tensor_reduce(out=ss10[:, h0:10], in_=sq[:, lo:1280].rearrange("p (h d) -> p h d", d=128),
                                                  op=ALU.add, axis=AX.X), [sq], [ss10])
            k.act(rs10[:, h0:10], ss10[:, h0:10], AF.Sqrt, [ss10], [rs10], scale=1.0 / 128, bias=g.epsc[:, 0:1])
            k.op("dve", lambda e: e.reciprocal(out=rs10[:, h0:10], in_=rs10[:, h0:10]), [rs10], [rs10])
            qv = qk[:, lo:1280].rearrange("p (h d) -> p h d", d=128)
            k.tt("dve", qv, qv, rs10[:, h0:10].unsqueeze(2).to_broadcast([128, nh, 128]), ALU.mult, [qk, rs10], [qk])
            if ti < 2:
                k.tt("dve", qr[:, lo:1280], qk[:, lo:1280], wrow[:, lo:1280], ALU.mult, [qk, wrow], [qr])
            else:
                k.tt("dve", qk[:, lo:1280], qk[:, lo:1280], wrow[:, lo:1280], ALU.mult, [qk, wrow], [qk])
                rp = ropes[ti % 5]
                q4 = qk[:, lo:1280].rearrange("p (h i two) -> p h i two", i=64, two=2)
                o4 = qr[:, lo:1280].rearrange("p (h i two) -> p h i two", i=64, two=2)
                x1, x2 = q4[:, :, :, 0], q4[:, :, :, 1]
                cs = rp[:, 0, :].unsqueeze(1).to_broadcast([128, nh, 64])
                sn = rp[:, 1, :].unsqueeze(1).to_broadcast([128, nh, 64])
                ta, tb = tmpa[:, h0:10, :], tmpb[:, h0:10, :]
                k.tt("dve", ta, x1, cs, ALU.mult, [qk, rp], [tmpa])
                k.tt("dve", tb, x2, sn, ALU.mult, [qk, rp], [tmpb])
                k.tt("dve", o4[:, :, :, 0], ta, tb, ALU.subtract, [tmpa, tmpb], [qr])
                k.tt("dve", ta, x1, sn, ALU.mult, [qk, rp], [tmpa])
                k.tt("dve", tb, x2, cs, ALU.mult, [qk, rp], [tmpb])
                k.tt("dve", o4[:, :, :, 1], ta, tb, ALU.add, [tmpa, tmpb], [qr])

        def stage_back2(ti):
            qr = qrs[ti % 2]
            kT = kTs[ti % 2]
            for gi in range(2):
                k.tr(pkT[:, gi, :], qr[:, 1024 + gi * 128:1024 + (gi + 1) * 128], g.ident[:, :], [qr, g.ident], [pkT])
            k.copy("act", kT[:, :, :], pkT[:, :, :], [pkT], [kT])
            k.dma("pool", kT, g.Kd, [(g.Kd.t[:, :, ti * 128:(ti + 1) * 128], kT[:, :, :])], load=False)
            if is_q(ti):
                qT = qTs[ti % 2]
                for h in range(8):
                    k.tr(pqT[:, h, :], qr[:, h * 128:(h + 1) * 128], g.ident[:, :], [qr, g.ident], [pqT])
                k.copy("act", qT[:, :, :], pqT[:, :, :], [pqT], [qT])
                k.dma("pool", qT, g.Qd, [(g.Qd.t[ti - 2, :, :], qT[:, :, :].rearrange("p h q -> p (h q)"))], load=False)

        def stage_fm(st):
            s = st % 2
            hbs = hTt[s]
            nq = max(0, min(4, NQT - st * 4))
            for j in range(12):
                pf = pfm[fm_count[0] % 2]
                fb = fms[fm_count[0] % 4]
                fm_count[0] += 1
                for kc in range(8):
                    k.mm(pf[:, :], win[:, kc, 1536 + j * 128:1536 + (j + 1) * 128], hTbig[:, s, kc, :],
                         kc == 0, kc == 7, hbs + [win], [pf])
                bc = vcol("b_in", 12 + j)
                k.act(fb[:, :], pf[:, :], AF.Identity, [pf, g.vec], [fb], bias=g.vec[:, bc:bc + 1])
                k.dma("pool", fb, g.HSd, [(g.HSd.t[j * 128:(j + 1) * 128, st * 512:(st + 1) * 512], fb[:, :])], load=False)
                yield
            if nq:
                n = nq * 128
                for j in range(16):
                    pf = pfm[fm_count[0] % 2]
                    fb = fms[fm_count[0] % 4]
                    fm_count[0] += 1
                    for kc in range(8):
                        k.mm(pf[:, 0:n], win[:, kc, 3072 + j * 128:3072 + (j + 1) * 128], hTbig[:, s, kc, 0:n],
                             kc == 0, kc == 7, hbs[:nq] + [win], [pf])
                    bc = vcol("b_in", 24 + j)
                    k.act(fb[:, 0:n], pf[:, 0:n], AF.Sigmoid, [pf, g.vec], [fb], bias=g.vec[:, bc:bc + 1])
                    k.dma("pool", fb, g.SGd, [(g.SGd.t[j * 128:(j + 1) * 128, st * 512:st * 512 + n], fb[:, 0:n])], load=False)
                    yield

        fm_gens = []

        def pump_fm(n):
            while n > 0 and fm_gens:
                try:
                    next(fm_gens[0])
                    n -= 1
                except StopIteration:
                    fm_gens.pop(0)

        stage_load(0)
        for it in range(NT + 4):
            if it + 1 < NT:
                stage_load(it + 1)
            if it < NT:
                stage_norm(it)
            if 3 <= it <= NT + 2:
                stage_back(it - 3)
            if 1 <= it <= NT:
                stage_front(it - 1)
            if 2 <= it <= NT + 1:
                stage_front2(it - 2)
                t = it - 2 - 2
                if t >= 0 and t % 4 == 3:
                    fm_gens.append(stage_fm(t // 4))
            pump_fm(8)
            if 3 <= it <= NT + 2:
                stage_back2(it - 3)
        pump_fm(10 ** 6)
        k.end_phase()
    k.stack = k.gstack


WARM_N = 0
TWO_PI = 2.0 * math.pi
MAGIC = 12582912.0
PI_LO = 3.1415925


def phase_B0(k, g):
    prev = k.stack
    with ExitStack() as ps:
        k.stack = ps
        us = [k.sbuf([128, T + 2], BF16, f"u{i}") for i in range(2)]
        accs = [k.sbuf([128, T], F32, f"acc{i}") for i in range(2)]
        outs = [k.sbuf([128, T], BF16, f"o{i}") for i in range(2)]
        dgs = [k.sbuf([128, 3, 128], BF16, f"dg{i}") for i in range(2)]
        pcs = [k.psum([128, 512], F32, f"pc{i}") for i in range(4)]
        for u in us:
            k.op("dve", lambda e, u=u: e.memset(u[:, 0:1], 0.0), [], [u])
            k.op("dve", lambda e, u=u: e.memset(u[:, T + 1:T + 2], 0.0), [], [u])
        n = 0
        for j in range(12):
            u = us[j % 2]
            o = outs[j % 2]
            acc = accs[j % 2]
            dg = dgs[j % 2]
            k.dma("sp", u, g.HSd, [(u[:, 1:T + 1], g.HSd.t[j * 128:(j + 1) * 128, :])])
            for tap, nm in enumerate(("hcw0", "hcw1", "hcw2")):
                wc = g.vec[:, vcol(nm, j):vcol(nm, j) + 1]
                k.act(dg[:, tap, :], g.ident[:, :], AF.Identity, [g.ident, g.vec], [dg], scale=wc)
            cb = g.vec[:, vcol("hcb", j):vcol("hcb", j) + 1]
            for m in range(T // 512):
                p_ = pcs[n % 4]
                n += 1
                for tap in range(3):
                    k.mm(p_[:, :], dg[:, tap, :], u[:, m * 512 + tap:m * 512 + tap + 512], tap == 0, tap == 2, [dg, u], [p_])
                k.act(acc[:, m * 512:(m + 1) * 512], p_[:, :], AF.Identity, [p_, g.vec], [acc], bias=cb)
            ov = o[:, :].rearrange("p (n2 n1) -> p n2 n1", n1=64)
            av = acc[:, :].rearrange("p (n1 n2) -> p n2 n1", n2=128)
            k.copy("dve", ov[:, 0:96, :], av[:, 0:96, :], [acc], [o])
            k.copy("pool", ov[:, 96:128, :], av[:, 96:128, :], [acc], [o])
            k.dma("sp", o, g.HCd, [(g.HCd.t[j * 128:(j + 1) * 128, :], o[:, :])], load=False)
        k.end_phase()
    k.stack = prev


def phase_BM(k, g):
    prev = k.stack
    with ExitStack() as ps:
        k.stack = ps
        w1 = k.sbuf([33, 64], F32, "fw1")
        wi = k.sbuf([64, 2, 64], F32, "fwi")
        fv = k.sbuf([64, 8], F32, "fv")
        wo = k.sbuf([64, 2048], F32, "fwo")
        k.dma("sp", w1, g.d_filt_w1, [(w1[:, :], g.d_filt_w1[:, :])])
        k.dma("sp", wi, g.d_filt_wi, [(wi[:, :, :], g.d_filt_wi[:, :, :])])
        k.dma("sp", fv, g.d_filt_vec, [(fv[:, :], g.d_filt_vec[:, :])])
        k.dma("sp", wo, g.d_filt_w_out, [(wo[:, :], g.d_filt_w_out[:, :])])
        k.ts("dve", fv[:, 4:7], fv[:, 1:4], fv[:, 0:1], None, ALU.mult, ALU.bypass, [fv], [fv])
        for o in range(2):
            f_ = wo[:, o * 512:(o + 1) * 512].rearrange("p (cb c) -> p cb c", c=64)
            b_ = wo[:, 1024 + o * 512:1024 + (o + 1) * 512].rearrange("p (cb c) -> p cb c", c=64)
            k.tt("dve", g.WSD[:, o, :, 0, :], f_, b_, ALU.add, [wo], [g.WSD])
            k.tt("dve", g.WSD[:, o, :, 1, :], f_, b_, ALU.subtract, [wo], [g.WSD])
        zs = [k.sbuf([33, 512], F32, f"z{i}") for i in range(4)]
        pres = [k.sbuf([64, 512], F32, f"pre{i}") for i in range(2)]
        rrs = [k.sbuf([64, 512], F32, f"rr{i}") for i in range(2)]
        hss = [[k.sbuf([64, 512], F32, f"h{p}_{i}") for i in range(2)] for p in range(2)]
        pps = [[k.psum([64, 512], F32, f"pp{p}_{i}") for i in range(2)] for p in range(2)]
        for cp in range(8):
            for par in range(2):
                ch = 2 * cp + par
                z = zs[ch % 4]
                k.dma("sp", z, g.d_zT, [(z[:, :], g.d_zT[:, ch * 512:(ch + 1) * 512])])
            for layer in range(3):
                for par in range(2):
                    ch = 2 * cp + par
                    z = zs[ch % 4]
                    pre, rr, hs, pp = pres[par], rrs[par], hss[par], pps[par]
                    p_ = pp[layer % 2]
                    if layer == 0:
                        k.mm(p_[:, :], w1[:, :], z[:, :], True, True, [w1, z], [p_])
                    else:
                        hin = hs[(layer - 1) % 2]
                        k.mm(p_[:, :], wi[:, layer - 1, :], hin[:, :], True, True, [wi, hin], [p_])
                    k.ts("dve", pre[:, :], p_[:, :], fv[:, 0:1], fv[:, 4 + layer:5 + layer], ALU.mult, ALU.add, [p_, fv], [pre])
                    k.ts("dve", rr[:, :], pre[:, :], 1.0 / TWO_PI, MAGIC, ALU.mult, ALU.add, [pre], [rr])
                    k.ts("dve", rr[:, :], rr[:, :], -MAGIC, None, ALU.add, ALU.bypass, [rr], [rr])
                    k.stt(pre[:, :], rr[:, :], -TWO_PI, pre[:, :], ALU.mult, ALU.add, [rr, pre], [pre])
                    k.ts("dve", pre[:, :], pre[:, :], -PI_LO, PI_LO, ALU.max, ALU.min, [pre], [pre])
                    if layer < 2:
                        ho = hs[layer % 2]
                        k.act(ho[:, :], pre[:, :], AF.Sin, [pre], [ho])
                    else:
                        k.act(g.HF[:, ch * 512:(ch + 1) * 512], pre[:, :], AF.Sin, [pre], [g.HF])
        k.end_phase()
    k.stack = prev


def phase_B1(k, g):
    prev = k.stack
    with ExitStack() as ps:
        k.stack = ps
        F2D = k.sbuf([128, 128, 384], BF16, "F2D")
        C1 = k.sbuf([64, 256], BF16, "C1")
        k.dma("sp", C1, g.d_C1, [(C1[:, :], g.d_C1[:, :])])
        k.dma("sp", F2D, g.d_F2D, [(F2D[:, q4 * 16:(q4 + 1) * 16, :], g.d_F2D.t[:, q4 * 16:(q4 + 1) * 16, :]) for q4 in range(8)])
        hTf = k.sbuf([64, 2, 128, 64], BF16, "hTf")
        Af = k.sbuf([128, 64, 256], BF16, "Af")
        dcs = [k.sbuf([64, 8, 64], F32, f"dc{i}") for i in range(2)]
        stg = [k.sbuf([128, 8, 64], BF16, f"stg{i}") for i in range(4)]
        pf = [k.psum([64, 4, 128], F32, f"pf{i}") for i in range(2)]
        pa = [k.psum([128, 2, 256], F32, f"pa{i}") for i in range(2)]
        px = [k.psum([128, 8, 64], F32, f"px{i}") for i in range(2)]
        cnt = {"pf": 0, "pa": 0, "px": 0, "dc": 0, "stg": 0, "ev": 0}

        def evac(out, in_, reads, writes):
            e = "act" if cnt["ev"] % 2 == 0 else "dve"
            cnt["ev"] += 1
            k.copy(e, out, in_, reads, writes)

        for o in range(2):
            for cb in range(8):
                c0 = cb * 64
                for dg in range(16):
                    dc = dcs[cnt["dc"] % 2]
                    cnt["dc"] += 1
                    k.dma("sp", dc, g.d_decay, [(dc[:, :, :], g.d_decay.t[:, dg * 8:(dg + 1) * 8, c0:c0 + 64])])
                    for half in range(2):
                        p_ = pf[cnt["pf"] % 2]
                        cnt["pf"] += 1
                        for i in range(4):
                            n2 = dg * 8 + half * 4 + i
                            k.mm(p_[:, i, :], g.HF[:, n2 * 64:(n2 + 1) * 64],
                                 g.WSD[:, o, cb, :, :].rearrange("p s c -> p (s c)"), True, True, [g.HF, g.WSD], [p_])
                        n0 = dg * 8 + half * 4
                        k.tt("dve", hTf[:, :, n0:n0 + 4, :],
                             p_[:, :, :].rearrange("p n (s c) -> p s n c", s=2),
                             dc[:, half * 4:half * 4 + 4, :].unsqueeze(1).to_broadcast([64, 2, 4, 64]),
                             ALU.mult, [p_, dc], [hTf])
                for sd in range(2):
                    for cp in range(32):
                        p_ = pa[cnt["pa"] % 2]
                        cnt["pa"] += 1
                        for i in range(2):
                            k.mm(p_[:, i, :], hTf[:, sd, :, 2 * cp + i], C1[:, :], True, True, [hTf, C1], [p_])
                        evac(Af[:, 2 * cp:2 * cp + 2, :], p_[:, :, :], [p_], [Af])
                    dst = g.KRd if sd == 0 else g.KId
                    for kg in range(16):
                        p_ = px[cnt["px"] % 2]
                        cnt["px"] += 1
                        for i in range(8):
                            k1 = kg * 8 + i
                            ar = Af[:, :, k1]
                            ai = Af[:, :, 128 + k1]
                            if sd == 0:
                                la, lb = F2D[:, k1, 128:256], F2D[:, k1, 0:128]
                            else:
                                la, lb = F2D[:, k1, 256:384], F2D[:, k1, 128:256]
                            k.mm(p_[:, i, :], la, ar, True, False, [F2D, Af], [p_])
                            k.mm(p_[:, i, :], lb, ai, False, True, [F2D, Af], [p_])
                        sb_ = stg[cnt["stg"] % 4]
                        cnt["stg"] += 1
                        evac(sb_[:, :, :], p_[:, :, :], [p_], [sb_])
                        k.dma("pool", sb_, dst, [(dst.t[o, cb, :, kg * 8:(kg + 1) * 8, :], sb_[:, :, :])], load=False)
        k.end_phase()
    k.stack = prev


def phase_B2(k, g):
    prev = k.stack
    with ExitStack() as ps:
        k.stack = ps
        F2T = k.sbuf([128, 128, 192], BF16, "F2T")
        C1 = k.sbuf([64, 256], BF16, "C1")
        R12 = k.sbuf([128, 2, 256], BF16, "R12")
        skp = k.sbuf([64, 2, 8], F32, "skp")
        k.dma("sp", C1, g.d_C1, [(C1[:, :], g.d_C1[:, :])])
        k.dma("sp", R12, g.d_R12, [(R12[:, :, :], g.d_R12[:, :, :])])
        k.dma("sp", skp, g.d_skipT, [(skp[:, :, :], g.d_skipT[:, :, :])])
        for q4 in range(4):
            k.dma("sp", F2T, g.d_F2T, [(F2T[:, q4 * 32:(q4 + 1) * 32, :], g.d_F2T.t[:, q4 * 32:(q4 + 1) * 32, :])])
        R1 = k.sbuf([64, T], BF16, "R1")
        BIG1 = k.sbuf([128, 2 * T], BF16, "BIG1")
        BIG2 = k.sbuf([128, 2 * T], BF16, "BIG2")
        uT = BIG1[0:64, 0:T].rearrange("p (n c) -> p n c", c=64)
        P1 = BIG1[:, 0:T].rearrange("p (k c) -> p k c", c=64)
        P2 = BIG1[:, T:2 * T].rearrange("p (k c) -> p k c", c=64)
        A = BIG2[:, :].rearrange("p (c r) -> p c r", r=256)
        B = BIG2[:, :].rearrange("p (c r n) -> p c r n", r=2, n=128)
        HY = k.sbuf([64, NQT, 128], BF16, "HY")
        krs = [k.sbuf([128, 8, 64], BF16, f"kr{i}") for i in range(2)]
        kis = [k.sbuf([128, 8, 64], BF16, f"ki{i}") for i in range(2)]
        i2s = [k.sbuf([128, 8, 128], BF16, f"i2{i}") for i in range(2)]
        xgs = [k.sbuf([64, 8, 64], BF16, f"xg{i}") for i in range(2)]
        tmp = k.sbuf([64, 8, 64], F32, "tmp")
        ptb = [k.psum([64, 16, 64], BF16, f"ptb{i}") for i in range(2)]
        pg = [k.psum([128, 512], F32, f"pg{i}") for i in range(6)]
        cnt = {"pg": 0, "ev": 0, "ld": 0}

        def bank():
            p_ = pg[cnt["pg"] % 6]
            cnt["pg"] += 1
            return p_

        def evac(out, in_, reads, writes):
            e = "act" if cnt["ev"] % 2 == 0 else "dve"
            cnt["ev"] += 1
            k.copy(e, out, in_, reads, writes)

        for cb in range(8):
            c0 = cb * 64
            k.dma("sp", R1, g.HCd, [(R1[:, :], g.HCd.t[1024 + c0:1024 + c0 + 64, :])])
            for o in range(2):
                NN = 64 if o == 0 else NQT
                for ng in range(8):
                    p_ = ptb[ng % 2]
                    for i in range(16):
                        n2 = ng * 16 + i
                        k.tr(p_[:, i, :], R1[:, n2 * 64:(n2 + 1) * 64], g.ident[0:64, 0:64], [R1, g.ident], [p_])
                    evac(uT[:, ng * 16:(ng + 1) * 16, :], p_[:, :, :], [p_], [BIG1])
                for cp in range(32):
                    p_ = bank()
                    pv = p_[:, :].rearrange("p (i r) -> p i r", r=256)
                    for i in range(2):
                        k.mm(pv[:, i, :], uT[:, :, 2 * cp + i], C1[:, :], True, True, [BIG1, C1], [p_])
                    evac(A[:, 2 * cp:2 * cp + 2, :], pv, [p_], [BIG2])
                for kg in range(16):
                    kr = krs[kg % 2]
                    ki = kis[kg % 2]
                    k.dma("sp", kr, g.KRd, [(kr[:, :, :], g.KRd.t[o, cb, :, kg * 8:(kg + 1) * 8, :])])
                    k.dma("sp", ki, g.KId, [(ki[:, :, :], g.KId.t[o, cb, :, kg * 8:(kg + 1) * 8, :])])
                    p_ = bank()
                    pv = p_[:, :].rearrange("p (i c) -> p i c", c=64)
                    for i in range(8):
                        k1 = kg * 8 + i
                        k.mm(pv[:, i, :], F2T[:, k1, 64:192], A[:, :, k1], True, False, [F2T, BIG2], [p_])
                        k.mm(pv[:, i, :], F2T[:, k1, 0:128], A[:, :, 128 + k1], False, True, [F2T, BIG2], [p_])
                    k.tt("dve", P1[:, kg * 8:(kg + 1) * 8, :], pv, kr[:, :, :], ALU.mult, [p_, kr], [BIG1])
                    k.tt("dve", P2[:, kg * 8:(kg + 1) * 8, :], pv, ki[:, :, :], ALU.mult, [p_, ki], [BIG1])
                for cp in range(32):
                    p_ = bank()
                    pv = p_[:, :].rearrange("p (i r) -> p i r", r=256)
                    for i in range(2):
                        c = 2 * cp + i
                        k.mm(pv[:, i, :], P1[:, :, c], R12[:, 0, :], True, False, [BIG1, R12], [p_])
                        k.mm(pv[:, i, :], P2[:, :, c], R12[:, 1, :], False, True, [BIG1, R12], [p_])
                    evac(B[:, 2 * cp:2 * cp + 2, :, :], p_[:, :].rearrange("p (i r n) -> p i r n", r=2, n=128), [p_], [BIG2])
                xrow = (0 if o == 0 else 512) + c0
                for ng in range(16):
                    i2 = i2s[ng % 2]
                    xg = xgs[ng % 2]
                    k.dma("sp", i2, g.d_I2T, [(i2[:, :, :], g.d_I2T.t[:, ng * 8:(ng + 1) * 8, :])])
                    k.dma("sp", xg, g.HCd, [(xg[:, :, :].rearrange("p a b -> p (a b)"), g.HCd.t[xrow:xrow + 64, ng * 512:(ng + 1) * 512])])
                    p_ = bank()
                    pv = p_[0:64, :].rearrange("p (i n) -> p i n", n=64)
                    for i in range(8):
                        n2 = ng * 8 + i
                        k.mm(pv[:, i, 0:NN], B[:, :, 0, n2], i2[:, i, 0:NN], True, False, [BIG2, i2], [p_])
                        k.mm(pv[:, i, 0:NN], B[:, :, 1, n2], i2[:, i, 64:64 + NN], False, True, [BIG2, i2], [p_])
                    zin = R1[:, ng * 512:(ng + 1) * 512].rearrange("p (a b) -> p a b", b=64)
                    k.stt(tmp[:, :, 0:NN], zin[:, :, 0:NN], skp[:, o, cb:cb + 1], pv[:, :, 0:NN], ALU.mult, ALU.add,
                          [R1, skp, p_], [tmp])
                    if o == 0:
                        k.tt("dve", zin, tmp[:, :, :], xg[:, :, :], ALU.mult, [tmp, xg], [R1])
                    else:
                        k.tt("dve", HY[:, :, ng * 8:(ng + 1) * 8].rearrange("p n a -> p a n"), tmp[:, :, 0:NN], xg[:, :, 0:NN],
                             ALU.mult, [tmp, xg], [HY])
            k.dma("pool", HY, g.HYd, [(g.HYd.t[c0:c0 + 64, :], HY[:, :, :].rearrange("p n a -> p (n a)"))], load=False)
        k.end_phase()
    k.stack = prev


def phase_B(k, g):
    prev = k.stack
    with ExitStack() as bs:
        k.stack = bs
        g.HF = k.sbuf([64, T], BF16, "HF")
        g.WSD = k.sbuf([64, 2, 8, 2, 64], BF16, "WSD")
        phase_BM(k, g)
        phase_B1(k, g)
        k.end_phase()
    k.stack = prev
    phase_B0(k, g)
    phase_B2(k, g)


def phase_C(k, g):
    prev = k.stack
    with ExitStack() as ps_:
        k.stack = ps_
        KT = k.sbuf([128, 2, NKEY], BF16, "KT")
        Vs = k.sbuf([128, NKT, 258], BF16, "Vs")
        wao = k.sbuf([128, 8, 1024], BF16, "wao")
        who = k.sbuf([128, 4, 1024], BF16, "who")
        wo = k.sbuf([128, 8, 1024], BF16, "wo")
        bg1 = k.sbuf([128, 1024], F32, "bg1")
        k.dma("sp", KT, g.Kd, [(KT[:, g_, :], g.Kd.t[:, g_, :]) for g_ in range(2)])
        k.dma("sp", Vs, g.Vd, [(Vs[:, t0:t0 + 22, :], g.Vd.t[t0:t0 + 22, :, :].rearrange("t p c -> p t c")) for t0 in (0, 22, 44)])
        k.dma("pool", wao, g.d_w_attn_out, [(wao[:, :, :], g.d_w_attn_out.t.rearrange("(kc p) n -> p kc n", p=128))])
        k.dma("pool", who, g.d_w_hy_out, [(who[:, :, :], g.d_w_hy_out.t.rearrange("(kc p) n -> p kc n", p=128))])
        k.dma("pool", wo, g.d_w_o, [(wo[:, :, :], g.d_w_o.t.rearrange("(kc p) n -> p kc n", p=128))])
        k.dma("sp", bg1, g.d_b_o, [(bg1[:, :], g.d_b_o[0:1024].partition_broadcast(128))])
        k.tt("dve", bg1[:, :], bg1[:, :], g.g1row[:, :], ALU.mult, [bg1, g.g1row], [bg1])
        Qts = [k.sbuf([128, 8, 128], BF16, f"Qt{i}") for i in range(2)]
        PTs = [k.sbuf([128, 512], BF16, f"PT{i}") for i in range(8)]
        recs = [k.sbuf([128, 512], F32, f"rec{i}") for i in range(2)]
        ssums = [k.sbuf([128, 512], F32, f"ssum{i}") for i in range(2)]
        ss2 = k.sbuf([128, 2], F32, "ss2")
        aoTs = [k.sbuf([128, 8, 128], BF16, f"aoT{i}") for i in range(2)]
        hyT = [k.sbuf([128, 4, 128], BF16, f"hyT{i}") for i in range(2)]
        sgs = [k.sbuf([128, 16, 128], BF16, f"sg{i}") for i in range(2)]
        xqs = [k.sbuf([128, 1024], F32, f"xq{i}") for i in range(2)]
        t1 = k.sbuf([128, 512], F32, "t1")
        t2 = k.sbuf([128, 512], F32, "t2")
        mixT = k.sbuf([128, 8, 128], BF16, "mixT")
        xnew = [k.sbuf([128, 1024], F32, f"xnew{i}") for i in range(2)]
        xn2 = k.sbuf([128, 1024], F32, "xn2")
        junk = k.sbuf([128, 1024], BF16, "junkc")
        ss = k.sbuf([128, 1], F32, "ssc")
        rs = k.sbuf([128, 1], F32, "rsc")
        h2T = [k.sbuf([128, 8, 128], BF16, f"h2T{i}") for i in range(2)]
        psb = [k.psum([128, 512], F32, f"ps{i}") for i in range(3)]
        pacc = [k.psum([128, 512], F32, f"pacc{i}") for i in range(2)]
        psum_s = k.psum([128, 512], F32, "psum_s")
        pm = [k.psum([128, 512], F32, f"pm{i}") for i in range(2)]
        cnt = {"s": 0}
        scale = 128.0 ** -0.5
        H2v = g.H2d.t.rearrange("kc p t -> p kc t")
        HYv = g.HYd.t.rearrange("(j p) t -> p j t", p=128)
        SGv = g.SGd.t.rearrange("(j p) t -> p j t", p=128)

        def loadQ(qi):
            Qt = Qts[qi % 2]
            k.dma("sp", Qt, g.Qd, [(Qt[:, :, :].rearrange("p h q -> p (h q)"), g.Qd.t[qi, :, :])])

        def loadM(qi):
            k.dma("sp", hyT[qi % 2], g.HYd, [(hyT[qi % 2][:, :, :], HYv[:, :, qi * 128:(qi + 1) * 128])])
            k.dma("sp", sgs[qi % 2], g.SGd, [(sgs[qi % 2][:, :, :], SGv[:, :, qi * 128:(qi + 1) * 128])])
            k.dma("sp", xqs[qi % 2], g.d_xl, [(xqs[qi % 2][:, :], g.d_xl.t[qi * 128:(qi + 1) * 128, :])])

        pend = {"q": [], "prevPT": None, "npair": 0}
        onesb = k.sbuf([128, 128], BF16, "onesb")
        k.op("dve", lambda e: e.memset(onesb[:, :], 1.0), [], [onesb])
        pairs = [k.sbuf([128, 512], BF16, f"pair{i}") for i in range(6)]

        def emit_pv(p):
            qi, g_, kt, PT = p
            pa_ = pacc[g_]
            k.mm(pa_[:, :], Vs[:, kt, g_ * 129:g_ * 129 + 128], PT[:, :], kt == 0, kt == NKT - 1, [PT, Vs], [pa_])
            if kt % 2 == 0:
                pend["prevPT"] = PT
            else:
                PTa = pend["prevPT"]
                pb = pairs[pend["npair"] % 6]
                pend["npair"] += 1
                k.tt("dve", pb[:, :], PTa[:, :], PT[:, :], ALU.add, [PTa, PT], [pb])
                if kt == NKT - 1:
                    sumq.append((pb, kt))
                elif kt % 4 == 1:
                    pend["prevpair"] = pb
                else:
                    pa2 = pend["prevpair"]
                    k.tt("dve", pb[:, :], pa2[:, :], pb[:, :], ALU.add, [pa2, pb], [pb])
                    sumq.append((pb, kt))
            flush_sums(keep=0 if kt == NKT - 1 else 2)
            if kt == NKT - 1:
                gens.append(norm_gen(qi, g_))

        sumq = []

        def flush_sums(keep):
            while len(sumq) > keep:
                pb, kt = sumq.pop(0)
                k.mm(psum_s[:, :], onesb[:, :], pb[:, :], kt == 3, kt == NKT - 1, [onesb, pb], [psum_s])

        def flush_pv(keep=0):
            while len(pend["q"]) > keep:
                emit_pv(pend["q"].pop(0))

        gens = []
        ncnt = {"n": 0}

        def norm_gen(qi, g_):
            sm = ssums[ncnt["n"] % 2]
            rc = recs[ncnt["n"] % 2]
            ncnt["n"] += 1
            pa_ = pacc[g_]
            aT = aoTs[qi % 2]
            k.copy("dve", sm[:, :], psum_s[:, :], [psum_s], [sm])
            yield
            for j in range(4):
                sl = slice(j * 128, (j + 1) * 128)
                k.op("dve", lambda e, sl=sl: e.reciprocal(out=rc[:, sl], in_=sm[:, sl]), [sm], [rc])
                yield
                k.tt("dve", aT[:, 4 * g_ + j, :], pa_[:, sl], rc[:, sl], ALU.mult, [pa_, rc], [aT])
                yield

        def pump():
            for gen in list(gens):
                try:
                    next(gen)
                except StopIteration:
                    gens.remove(gen)

        def drain():
            while gens:
                pump()

        def attn_group(qi, g_):
            Qt = Qts[qi % 2]
            rhsq = Qt[:, 4 * g_:4 * g_ + 4, :].rearrange("p h q -> p (h q)")
            for kt in range(NKT):
                p_ = psb[cnt["s"] % 3]
                PT = PTs[cnt["s"] % 8]
                cnt["s"] += 1
                k.mm(p_[:, :], KT[:, g_, kt * 128:(kt + 1) * 128], rhsq, True, True, [KT, Qt], [p_])
                k.act(PT[:, :], p_[:, :], AF.Exp, [p_], [PT], scale=scale)
                flush_pv(keep=1)
                pend["q"].append((qi, g_, kt, PT))
                if g_ == 0 and kt == 12 and qi >= 1:
                    gens.append(merge_gen(qi - 1, qi + 1 if qi + 1 < NQT else None))
                pump()

        def merge_gen(qi, next_load=None):
            aoT = aoTs[qi % 2]
            sg = sgs[qi % 2]
            hy = hyT[qi % 2]
            xq = xqs[qi % 2]
            xnw = xnew[qi % 2]
            for r in range(2):
                pA = pm[0][:, :].rearrange("p (j q) -> p j q", q=128)
                pH = pm[1][:, :].rearrange("p (j q) -> p j q", q=128)
                for i in range(4):
                    j = 4 * r + i
                    for kc in range(8):
                        k.mm(pA[:, i, :], wao[:, kc, j * 128:(j + 1) * 128], aoT[:, kc, :], kc == 0, kc == 7, [wao, aoT], [pm[0]])
                    yield
                for i in range(4):
                    j = 4 * r + i
                    for kc in range(4):
                        k.mm(pH[:, i, :], who[:, kc, j * 128:(j + 1) * 128], hy[:, kc, :], kc == 0, kc == 3, [who, hy], [pm[1]])
                    yield
                k.tt("dve", t1[:, :].rearrange("p (j q) -> p j q", q=128), pA, sg[:, 4 * r:4 * r + 4, :], ALU.mult, [pm[0], sg], [t1])
                yield
                k.tt("dve", t2[:, :].rearrange("p (j q) -> p j q", q=128), pH, sg[:, 8 + 4 * r:12 + 4 * r, :], ALU.mult, [pm[1], sg], [t2])
                yield
                k.tt("dve", mixT[:, 4 * r:4 * r + 4, :], t1[:, :].rearrange("p (j q) -> p j q", q=128),
                     t2[:, :].rearrange("p (j q) -> p j q", q=128), ALU.add, [t1, t2], [mixT])
                yield
            k.tt("pool", xq[:, :], xq[:, :], bg1[:, :], ALU.add, [xq, bg1], [xq])
            for half in range(2):
                for j in range(8):
                    k.mm(pm[half][:, :], mixT[:, j, :], wo[:, j, half * 512:(half + 1) * 512], j == 0, j == 7, [mixT, wo], [pm[half]])
                    if j % 2 == 1:
                        yield
            for half in range(2):
                sl = slice(half * 512, (half + 1) * 512)
                k.tt("dve", t1[:, :], pm[half][:, :], g.g1row[:, sl], ALU.mult, [pm[half], g.g1row], [t1])
                yield
                k.tt("dve", xnw[:, sl], t1[:, :], xq[:, sl], ALU.add, [t1, xq], [xnw])
                yield
            k.dma("pool", xnw, g.XNd, [(g.XNd.t[qi * 128:(qi + 1) * 128, :], xnw[:, :])], load=False)
            for half in range(2):
                sl = slice(half * 512, (half + 1) * 512)
                k.tt("pool", xn2[:, sl], xnw[:, sl], xnw[:, sl], ALU.mult, [xnw], [xn2])
                yield
                k.op("dve", lambda e, sl=sl, half=half: e.tensor_reduce(out=ss2[:, half:half + 1], in_=xn2[:, sl], op=ALU.add, axis=AX.X),
                     [xn2], [ss2])
                yield
            k.tt("dve", ss[:, :], ss2[:, 0:1], ss2[:, 1:2], ALU.add, [ss2], [ss])
            yield
            k.act(rs[:, :], ss[:, :], AF.Ln, [ss, g.epsc], [rs], scale=1.0 / D, bias=g.epsc[:, 0:1])
            k.act(rs[:, :], rs[:, :], AF.Exp, [rs], [rs], scale=-0.5)
            yield
            k.act(xn2[:, :], xnw[:, :], AF.Identity, [xnw, rs], [xn2], scale=rs[:, 0:1])
            yield
            hT_ = h2T[qi % 2]
            for r in range(2):
                pv = pm[r][:, :].rearrange("p (h q) -> p h q", q=128)
                for i in range(4):
                    kc = 4 * r + i
                    k.tr(pv[:, i, :], xn2[:, kc * 128:(kc + 1) * 128], g.identf[:, :], [xn2, g.identf], [pm[r]])
                yield
                for i in range(4):
                    kc = 4 * r + i
                    k.ts("dve", hT_[:, kc, :], pv[:, i, :], g.mods[:, 4, kc:kc + 1], g.mods[:, 5, kc:kc + 1], ALU.mult, ALU.add,
                         [pm[r], g.mods], [hT_])
                yield
            k.dma("pool", hT_, g.H2d, [(H2v[:, :, qi * 128:(qi + 1) * 128], hT_[:, :, :])], load=False)
            if next_load is not None:
                loadM(next_load)

        loadQ(0)
        loadM(0)
        loadM(1)
        for qi in range(NQT):
            if qi + 1 < NQT:
                loadQ(qi + 1)
            attn_group(qi, 0)
            attn_group(qi, 1)
            drain()
        flush_pv()
        drain()
        gens.append(merge_gen(NQT - 1))
        drain()
        k.end_phase()
    k.stack = prev


def phase_D(k, g):
    prev = k.stack
    with ExitStack() as ps_:
        k.stack = ps_
        wup = k.sbuf([128, 8, 2 * DFF], BF16, "wup")
        wdn = k.sbuf([128, 22, 1024], BF16, "wdn")
        upv = g.d_w_up.t.rearrange("(kc p) n -> p kc n", p=128)
        for q4 in range(4):
            lo, hi = q4 * 704, (q4 + 1) * 704
            k.dma("pool", wup, g.d_w_up, [(wup[:, :, lo:hi], upv[:, :, lo:hi]),
                                          (wup[:, :, DFF + lo:DFF + hi], upv[:, :, DFF + lo:DFF + hi])])
        k.dma("pool", wdn, g.d_w_down, [(wdn[:, :, :], g.d_w_down.t.rearrange("(j p) n -> p j n", p=128))])
        bg2 = k.sbuf([128, 1024], F32, "bg2")
        fnw = k.sbuf([128, 1024], F32, "fnw")
        k.dma("sp", bg2, g.d_b_down, [(bg2[:, :], g.d_b_down[0:1024].partition_broadcast(128))])
        k.dma("sp", fnw, g.d_final_norm_w, [(fnw[:, :], g.d_final_norm_w[0:1024].partition_broadcast(128))])
        k.tt("dve", bg2[:, :], bg2[:, :], g.g2row[:, :], ALU.mult, [bg2, g.g2row], [bg2])
        h2b = [k.sbuf([128, 8, 514], BF16, f"h2b{i}") for i in range(1)]
        uas = [k.sbuf([128, 514], F32, f"ua{i}") for i in range(2)]
        ugs = [k.sbuf([128, 514], F32, f"ug{i}") for i in range(2)]
        cas = [k.sbuf([128, 512], F32, f"ca{i}") for i in range(2)]
        cgs = [k.sbuf([128, 512], F32, f"cg{i}") for i in range(2)]
        actT = k.sbuf([128, 22, 512], BF16, "actT")
        xns = [k.sbuf([128, 1024], F32, f"xnd{i}") for i in range(1)]
        ys = [k.sbuf([128, 1024], F32, f"y{i}") for i in range(1)]
        t1 = k.sbuf([128, 512], F32, "t1d")
        ss = k.sbuf([128, 1], F32, "ssd")
        rs = k.sbuf([128, 1], F32, "rsd")
        pa = [k.psum([128, 512], F32, f"pa{i}") for i in range(2)]
        pg_ = [k.psum([128, 512], F32, f"pgd{i}") for i in range(2)]
        phs = [k.psum([128, 512], F32, f"ph{i}") for i in range(2)]
        cen = k.sbuf([128, 44], F32, "cen")
        k.tt("dve", cen[:, :], g.vec[:, vcol("fcw1"):vcol("fcw1") + 44], g.vec[:, vcol("b_up"):vcol("b_up") + 44], ALU.mult, [g.vec], [cen])
        k.tt("dve", cen[:, :], cen[:, :], g.vec[:, vcol("fcb"):vcol("fcb") + 44], ALU.add, [cen, g.vec], [cen])
        pd = [k.psum([128, 512], F32, f"pd{i}") for i in range(2)]
        H2v = g.H2d.t.rearrange("kc p t -> p kc t")
        NB = 8
        tcount = [0]
        for bi in range(NB):
            T0 = bi * 512
            hb = h2b[0]
            if bi == 0:
                k.dma("sp", hb, g.H2d, [(hb[:, :, 1:514], H2v[:, :, 0:513])])
                k.op("pool", lambda e, hb=hb: e.memset(hb[:, :, 0:1], 0.0), [], [hb])
            else:
                k.dma("sp", hb, g.H2d, [(hb[:, :, :], H2v[:, :, T0 - 1:T0 + 513])])
            for j in range(22):
                p_a, p_g = pa[j % 2], pg_[j % 2]
                ua, ug = uas[j % 2], ugs[j % 2]
                ca, cg = cas[j % 2], cgs[j % 2]
                for kc in range(8):
                    k.mm(p_a[:, :], wup[:, kc, j * 128:(j + 1) * 128], hb[:, kc, 1:513], kc == 0, kc == 7, [wup, hb], [p_a])
                for kc in range(8):
                    k.mm(p_g[:, :], wup[:, kc, DFF + j * 128:DFF + (j + 1) * 128], hb[:, kc, 1:513], kc == 0, kc == 7, [wup, hb], [p_g])
                ph = phs[j % 2]
                po = (j % 2) * 8
                for kc in range(8):
                    k.mm(ph[:, po:po + 2], wup[:, kc, j * 128:(j + 1) * 128], hb[:, kc, 0:514:513], kc == 0, kc == 7, [wup, hb], [ph])
                for kc in range(8):
                    k.mm(ph[:, po + 2:po + 4], wup[:, kc, DFF + j * 128:DFF + (j + 1) * 128], hb[:, kc, 0:514:513], kc == 0, kc == 7, [wup, hb], [ph])
                ba = g.vec[:, vcol("b_up", j):vcol("b_up", j) + 1]
                bg_ = g.vec[:, vcol("b_up", 22 + j):vcol("b_up", 22 + j) + 1]
                k.act(ua[:, 1:513], p_a[:, :], AF.Identity, [p_a, g.vec], [ua], bias=ba)
                k.ts("dve", ua[:, 0:514:513], ph[:, po:po + 2], ba, None, ALU.add, ALU.bypass, [ph, g.vec], [ua])
                k.act(ug[:, 1:513], p_g[:, :], AF.Identity, [p_g, g.vec], [ug], bias=bg_)
                k.ts("dve", ug[:, 0:514:513], ph[:, po + 2:po + 4], bg_, None, ALU.add, ALU.bypass, [ph, g.vec], [ug])
                if bi == 0:
                    k.op("dve", lambda e, ua=ua: e.memset(ua[:, 0:1], 0.0), [], [ua])
                    k.op("dve", lambda e, ug=ug: e.memset(ug[:, 0:1], 0.0), [], [ug])
                for (u_, c_, jj, pp_) in ((ua, ca, j, p_a), (ug, cg, 22 + j, p_g)):
                    w0 = g.vec[:, vcol("fcw0", jj):vcol("fcw0", jj) + 1]
                    w1 = g.vec[:, vcol("fcw1", jj):vcol("fcw1", jj) + 1]
                    w2 = g.vec[:, vcol("fcw2", jj):vcol("fcw2", jj) + 1]
                    k.act(c_[:, :], pp_[:, :], AF.Identity, [pp_, g.vec, cen], [c_], scale=w1, bias=cen[:, jj:jj + 1])
                    k.stt(c_[:, :], u_[:, 0:512], w0, c_[:, :], ALU.mult, ALU.add, [u_, g.vec, c_], [c_])
                    k.stt(c_[:, :], u_[:, 2:514], w2, c_[:, :], ALU.mult, ALU.add, [u_, g.vec, c_], [c_])
                k.act(ca[:, :], ca[:, :], AF.Gelu_apprx_tanh, [ca], [ca])
                k.tt("dve", actT[:, j, :], ca[:, :], cg[:, :], ALU.mult, [ca, cg], [actT])
            for tt_ in range(4):
                qi = bi * 4 + tt_
                xn_ = xns[0]
                y = ys[0]
                tcount[0] += 1
                k.dma("sp", xn_, g.XNd, [(xn_[:, :], g.XNd.t[qi * 128:(qi + 1) * 128, :])])
                k.tt("pool", xn_[:, :], xn_[:, :], bg2[:, :], ALU.add, [xn_, bg2], [xn_])
                for half in range(2):
                    for j in range(22):
                        k.mm(pd[half][:, :], actT[:, j, tt_ * 128:(tt_ + 1) * 128], wdn[:, j, half * 512:(half + 1) * 512],
                             j == 0, j == 21, [actT, wdn], [pd[half]])
                for half in range(2):
                    sl = slice(half * 512, (half + 1) * 512)
                    k.tt("dve", t1[:, :], pd[half][:, :], g.g2row[:, sl], ALU.mult, [pd[half], g.g2row], [t1])
                    k.tt("dve", y[:, sl], t1[:, :], xn_[:, sl], ALU.add, [t1, xn_], [y])
                k.act(t1[:, :].bitcast(BF16), y[:, :], AF.Square, [y], [t1, ss], accum_out=ss[:, 0:1])
                k.act(rs[:, :], ss[:, :], AF.Sqrt, [ss, g.epsc], [rs], scale=1.0 / D, bias=g.epsc[:, 0:1])
                k.op("dve", lambda e: e.reciprocal(out=rs[:, :], in_=rs[:, :]), [rs], [rs])
                k.act(xn_[:, :], y[:, :], AF.Identity, [y, rs], [xn_], scale=rs[:, 0:1])
                k.tt("pool", y[:, :], xn_[:, :], fnw[:, :], ALU.mult, [xn_, fnw], [y])
                k.dma("sp", y, g.out, [(g.out.t[qi * 128:(qi + 1) * 128, :], y[:, :])], load=False)
        k.end_phase()
    k.stack = prev


INPUT_SPECS = [
    ("xl", [T, D], F32), ("ctx", [NCTX, D], F32), ("cvecT", [128, 8, 2], F32),
    ("w_mod", [D, 6 * D], F32), ("b_mod", [6 * D], F32), ("vec", [128, NV], F32),
    ("w_in", [D, 5120], F32), ("b_in", [5120], F32), ("qkw", [1280], F32),
    ("rope", [T, 2, 64], F32), ("ident", [128, 128], BF16),
    ("zT", [33, T], F32), ("filt_w1", [33, 64], F32), ("filt_wi", [64, 2, 64], F32), ("filt_vec", [64, 8], F32),
    ("filt_w_out", [64, 2048], F32), ("skipT", [64, 2, 8], F32), ("decay", [64, 128, 512], F32),
    ("C1", [64, 256], BF16), ("F2T", [128, 128, 192], BF16), ("F2D", [128, 128, 384], BF16), ("R12", [128, 2, 256], BF16), ("I2T", [128, 128, 128], BF16),
    ("identf", [128, 128], F32), ("w_attn_out", [D, D], F32), ("w_hy_out", [512, D], F32), ("w_o", [D, D], F32),
    ("b_o", [D], F32), ("w_up", [D, 2 * DFF], F32), ("w_down", [DFF, D], F32), ("b_down", [D], F32),
    ("final_norm_w", [D], F32),
]


def build(debug=None):
    nc = bass.Bass("TRN2", target_bir_lowering=False)
    with ExitStack() as st:
        k = KB(nc, st)
        g = G()
        for name, shape, dt in INPUT_SPECS:
            setattr(g, "d_" + name, k.dram(name, shape, dt, kind="ExternalInput"))
        dk = "ExternalOutput" if debug == "A" else None
        dkb = "ExternalOutput" if debug == "B" else None
        g.HCd = k.dram("HCd", [1536, T], BF16, kind=dkb)
        g.KRd = k.dram("KRd", [2, 8, 128, 128, 64], BF16, kind=dkb)
        g.KId = k.dram("KId", [2, 8, 128, 128, 64], BF16, kind=dkb)
        g.HYd = k.dram("HYd", [512, NQ], BF16, kind=dkb)
        dkc = "ExternalOutput" if debug == "C" else None
        g.XNd = k.dram("XNd", [NQ, D], F32, kind=dkc)
        g.H2d = k.dram("H2d", [8, 128, NQ], BF16, kind=dkc)
        g.out = k.dram("out", [4096, D], F32, kind="ExternalOutput")
        g.identf = k.sbuf([128, 128], F32, "identf")
        k.dma("sp", g.identf, g.d_identf, [(g.identf[:, :], g.d_identf[:, :])])
        g.Kd = k.dram("Kd", [128, 2, NKEY], BF16, kind=dk)
        g.Vd = k.dram("Vd", [NKT, 128, 258], BF16, kind=dk)
        g.Qd = k.dram("Qd", [NQT, 128, 1024], BF16, kind=dk)
        g.HSd = k.dram("HSd", [1536, T], BF16, kind=dk)
        g.SGd = k.dram("SGd", [2048, NQ], BF16, kind=dk)
        g.epsc = k.sbuf([128, 1], F32, "epsc")
        k.op("dve", lambda e: e.memset(g.epsc[:, :], EPS), [], [g.epsc])
        phase_0(k, g)
        phase_A(k, g)
        phase_B(k, g)
        phase_C(k, g)
        phase_D(k, g)
        k.barrier()
        print("ninst", k.ninst, {e: c for e, c in k.cnt.items()})
    return nc


def _fm(v):
    v = np.asarray(v, np.float32)
    return np.ascontiguousarray(v.reshape(-1, 128).T)


def rope_tables():
    rows = T // 64
    row = np.repeat(np.arange(rows), 64).astype(np.float32)
    col = np.tile(np.arange(64), rows).astype(np.float32)
    freqs = (10000.0 ** (-np.arange(0, 64, 2, dtype=np.float32) / 64)).astype(np.float32)
    ang = np.concatenate([row[:, None] * freqs, col[:, None] * freqs], axis=-1).astype(np.float32)
    return np.stack([np.cos(ang), np.sin(ang)], axis=1).astype(np.float32)


_CONST = {}


def const_tables():
    if _CONST:
        return _CONST
    bf = ml_dtypes.bfloat16
    N = 2 * T
    n1 = np.arange(64, dtype=np.float64)
    n2 = np.arange(128, dtype=np.float64)
    k1 = np.arange(128, dtype=np.float64)
    k2 = np.arange(64, dtype=np.float64)
    phi = 2 * np.pi * np.outer(n1, k1 + 0.5) / 128
    _CONST["C1"] = np.concatenate([np.cos(phi), -np.sin(phi)], 1).astype(bf)
    psi = 2 * np.pi * (n2[:, None, None] * (k1[None, :, None] + 0.5) / N + n2[:, None, None] * k2[None, None, :] / 128)
    Hr, Hi = np.cos(psi), -np.sin(psi)
    _CONST["F2T"] = np.ascontiguousarray(np.concatenate([-Hi, Hr, Hi], 2).astype(bf))
    _CONST["F2D"] = np.ascontiguousarray(np.concatenate([-Hi, -Hi, Hr, Hr, Hi, Hi], 2).astype(bf))
    th = 2 * np.pi * np.outer(k2, n2) / 128
    Cr, Ci = np.cos(th), np.sin(th)
    S1 = np.concatenate([Cr, -Ci], 0); S2 = np.concatenate([-Ci, -Cr], 0); S3 = np.concatenate([Ci, Cr], 0)
    _CONST["R12"] = np.ascontiguousarray(np.stack([np.concatenate([S1, S3], 1), np.concatenate([S2, S1], 1)], 1).astype(bf))
    gg = 2 * np.pi * (n2[None, :, None] * (k1[:, None, None] + 0.5) / N + n1[None, None, :] * (k1[:, None, None] + 0.5) / 128)
    _CONST["I2T"] = np.ascontiguousarray(np.concatenate([np.cos(gg) * 2 / N, -np.sin(gg) * 2 / N], 2).astype(bf))
    L = T
    t_idx = np.arange(L, dtype=np.float32)
    tt = t_idx / np.float32(L - 1)
    f = np.linspace(1e-4, 15, 16, dtype=np.float32)
    wpos = (np.float32(2.0 * math.pi) * t_idx / np.float32(L)).astype(np.float32)
    z = np.concatenate([tt[:, None], np.cos(wpos[:, None] * f), -np.sin(wpos[:, None] * f)], -1).astype(np.float32)
    order = (np.arange(64)[None, :] * 128 + np.arange(128)[:, None]).reshape(-1)
    _CONST["zT"] = np.ascontiguousarray(z[order].T)
    min_decay = math.log(1e-2) / 1.5
    max_decay = math.log(1e-2) / 0.3
    deltas = np.linspace(min_decay, max_decay, 512, dtype=np.float32)
    dec = np.exp(-tt[:, None] * np.abs(deltas)).astype(np.float32)
    _CONST["decay"] = np.ascontiguousarray(dec.reshape(64, 128, 512))
    return _CONST


def host_inputs(inp, core):
    b, hf = core // 2, core % 2
    rev = hf == 1
    f32 = lambda a: np.ascontiguousarray(np.asarray(a, np.float32))
    x = np.asarray(inp["x"][b], np.float32)
    rope = rope_tables()
    hcw = np.asarray(inp["hy_conv_w"][0], np.float32)
    fcw = np.asarray(inp["ffn_conv_w"][0], np.float32)
    if rev:
        x = x[::-1]
        rope = rope[::-1]
        hcw = hcw[::-1]
        fcw = fcw[::-1]
    cvec = np.stack([np.asarray(inp["c"][b], np.float32), np.asarray(inp["c_ctx"], np.float32)], 0)
    cvecT = np.ascontiguousarray(cvec.reshape(2, 8, 128).transpose(2, 1, 0))
    vec = np.zeros((128, NV), np.float32)

    def put(name, v):
        o, w = VEC_COLS[name]
        vec[:, o:o + w] = _fm(v)
    put("norm1_w", inp["norm1_w"][0]); put("norm2_w", inp["norm2_w"][0]); put("b_in", inp["b_in"][0])
    put("hcw0", hcw[0]); put("hcw1", hcw[1]); put("hcw2", hcw[2]); put("hcb", inp["hy_conv_b"][0])
    put("b_up", inp["b_up"][0]); put("fcw0", fcw[0]); put("fcw1", fcw[1]); put("fcw2", fcw[2])
    put("fcb", inp["ffn_conv_b"][0]); put("b_mod", inp["b_mod"][0])
    qkw = np.concatenate([np.tile(np.asarray(inp["q_norm_w"][0], np.float32), 8),
                          np.tile(np.asarray(inp["k_norm_w"][0], np.float32), 2)])
    m = {
        "xl": f32(x), "ctx": f32(inp["ctx"][b]), "cvecT": f32(cvecT),
        "w_mod": f32(inp["w_mod"][0]), "b_mod": f32(inp["b_mod"][0]), "vec": vec,
        "w_in": f32(inp["w_in"][0]), "b_in": f32(inp["b_in"][0]), "qkw": f32(qkw),
        "rope": f32(rope), "ident": np.eye(128, dtype=np.float32).astype(ml_dtypes.bfloat16),
    }
    m["identf"] = np.eye(128, dtype=np.float32)
    for nm_ in ("w_attn_out", "w_hy_out", "w_o", "b_o", "w_up", "w_down", "b_down"):
        m[nm_] = f32(inp[nm_][0])
    m["final_norm_w"] = f32(inp["final_norm_w"])
    ct = const_tables()
    for nm in ("zT", "decay", "C1", "F2T", "F2D", "R12", "I2T"):
        m[nm] = ct[nm]
    wout = np.asarray(inp["filt_w_out"][0], np.float32)
    if rev:
        wout = np.concatenate([wout[:, 1024:], wout[:, :1024]], 1)
    fvec = np.zeros((64, 8), np.float32)
    fvec[:, 0] = inp["filt_freq"][0]; fvec[:, 1] = inp["filt_b1"][0]
    fvec[:, 2] = inp["filt_b_inner"][0][0]; fvec[:, 3] = inp["filt_b_inner"][0][1]
    m["filt_w1"] = f32(inp["filt_w1"][0])
    m["filt_wi"] = f32(np.asarray(inp["filt_w_inner"][0], np.float32).transpose(1, 0, 2))
    m["filt_vec"] = fvec
    m["filt_w_out"] = f32(wout)
    m["skipT"] = f32(np.asarray(inp["hy_skip"][0], np.float32).reshape(2, 8, 64).transpose(2, 0, 1))
    return m


def kernel(**inputs):
    nc = build()
    in_maps = [host_inputs(inputs, c) for c in range(8)]
    res = run_bass_kernel_spmd(nc, in_maps, core_ids=list(range(8)))
    out = np.zeros((4, T, D), np.float32)
    for c in range(8):
        b, hf = c // 2, c % 2
        y = np.asarray(res.results[c]["out"], np.float32)
        if hf == 0:
            out[b, :4096] = y
        else:
            out[b, 4096:] = y[::-1]
    return out
```

```python
import math
from contextlib import ExitStack
import numpy as np
import ml_dtypes
import concourse.bass as bass
import concourse.mybir as mybir
from concourse.bass_utils import run_bass_kernel_spmd

F32 = mybir.dt.float32
BF16 = mybir.dt.bfloat16
AF = mybir.ActivationFunctionType
ALU = mybir.AluOpType
AX = mybir.AxisListType

D = 1024
T = 8192
NCTX = 256
NKEY = T + NCTX
NKT = NKEY // 128
NQT = 33
NQ = NQT * 128
DFF = 2816
EPS = 1e-6


class Buf:
    def __init__(self, t, name):
        self.t = t
        self.name = name
        self.writer = {}
        self.readers = {}
        self.dsem = None
        self.dcount = 0

    def __getitem__(self, idx):
        return self.t[idx]


def _merge(d, toks):
    for k_, v in toks.items():
        if d.get(k_, 0) < v:
            d[k_] = v


class KB:
    def __init__(self, nc, stack):
        self.nc = nc
        self.gstack = stack
        self.stack = stack
        self.eng = {"pe": nc.tensor, "act": nc.scalar, "dve": nc.vector, "pool": nc.gpsimd, "sp": nc.sync}
        self.sems = {}
        self.cnt = {}
        for e in ("pe", "act", "dve", "pool"):
            self.sems[e] = stack.enter_context(nc.semaphore("s_" + e))
            self.cnt[e] = 0
        self.waited = {e: {} for e in self.eng}
        self.nbuf = 0
        self.ninst = 0
        self.dsem_live = {}

    def sbuf(self, shape, dtype, name):
        self.nbuf += 1
        name = f"{name}_{self.nbuf}"
        t = self.stack.enter_context(self.nc.sbuf_tensor(name, list(shape), dtype))
        return Buf(t, name)

    def view(self, buf, name):
        self.nbuf += 1
        return Buf(buf.t, f"{name}_{self.nbuf}")

    def psum(self, shape, dtype, name):
        self.nbuf += 1
        name = f"{name}_{self.nbuf}"
        t = self.stack.enter_context(self.nc.psum_tensor(name, list(shape), dtype))
        return Buf(t, name)

    def dram(self, name, shape, dtype, kind=None):
        if kind is None:
            t = self.nc.dram_tensor(name, list(shape), dtype)
        else:
            t = self.nc.dram_tensor(name, list(shape), dtype, kind=kind)
        return Buf(t.ap(), name)

    def _wait(self, e, deps):
        w = self.waited[e]
        for k_, v in deps.items():
            if w.get(k_, 0) >= v:
                continue
            self.eng[e].wait_ge(self.sems[k_], v)
            w[k_] = v
            self.ninst += 1

    def op(self, e, fn, reads=(), writes=()):
        deps = {}
        for b in reads:
            _merge(deps, b.writer)
        for b in writes:
            for src in (b.writer, b.readers):
                for k_, v in src.items():
                    if deps.get(k_, 0) < v:
                        deps[k_] = v
        if e == "pe":
            deps.pop("pe", None)
        self._wait(e, deps)
        inst = fn(self.eng[e])
        self.cnt[e] += 1
        inst.then_inc(self.sems[e], 1)
        self.ninst += 1
        tok = {e: self.cnt[e]}
        for b in reads:
            _merge(b.readers, tok)
        for b in writes:
            b.writer = dict(tok)
            b.readers = {}
        return tok

    def dma(self, q, sb, dr, parts, load=True):
        if sb.dsem is None:
            key = "d_" + sb.name
            self.sems[key] = self.stack.enter_context(self.nc.semaphore(key))
            sb.dsem = key
            self.dsem_live[key] = sb
        deps = {}
        if load:
            _merge(deps, dr.writer)
            _merge(deps, sb.writer)
            _merge(deps, sb.readers)
        else:
            _merge(deps, sb.writer)
            _merge(deps, dr.readers)
        if sb.dcount:
            _merge(deps, {sb.dsem: sb.dcount})
        self._wait(q, deps)
        for (o, i) in parts:
            self.eng[q].dma_start(out=o, in_=i).then_inc(self.sems[sb.dsem], 16)
            sb.dcount += 16
            self.ninst += 1
        tok = {sb.dsem: sb.dcount}
        if load:
            sb.writer = dict(tok)
            sb.readers = {}
            _merge(dr.readers, tok)
        else:
            _merge(sb.readers, tok)
            _merge(dr.writer, tok)
        return tok

    def barrier(self):
        allc = {e: c for e, c in self.cnt.items() if c}
        for key, sb in self.dsem_live.items():
            if sb.dcount:
                allc[key] = sb.dcount
        for e in self.eng:
            self._wait(e, dict(allc))

    def end_phase(self, drams=()):
        self.barrier()
        for d in drams:
            d.writer = {}
            d.readers = {}
        self.dsem_live = {}

    def mm(self, out, lhsT, rhs, start, stop, reads, writes):
        return self.op("pe", lambda e: e.matmul(out, lhsT, rhs, start=start, stop=stop), reads, writes)

    def tr(self, out, in_, ident_ap, reads, writes):
        return self.op("pe", lambda e: e.transpose(out, in_, ident_ap), reads, writes)

    def act(self, out, in_, func, reads, writes, **kw):
        return self.op("act", lambda e: e.activation(out=out, in_=in_, func=func, **kw), reads, writes)

    def tt(self, eng, out, in0, in1, op, reads, writes):
        return self.op(eng, lambda e: e.tensor_tensor(out=out, in0=in0, in1=in1, op=op), reads, writes)

    def ts(self, eng, out, in0, s1, s2, op0, op1, reads, writes):
        return self.op(eng, lambda e: e.tensor_scalar(out=out, in0=in0, scalar1=s1, scalar2=s2, op0=op0, op1=op1),
                       reads, writes)

    def stt(self, out, in0, scalar, in1, op0, op1, reads, writes):
        return self.op("dve", lambda e: e.scalar_tensor_tensor(out=out, in0=in0, scalar=scalar, in1=in1,
                                                                op0=op0, op1=op1), reads, writes)

    def copy(self, eng, out, in_, reads, writes):
        if eng == "act":
            return self.act(out, in_, AF.Identity, reads, writes)
        return self.op(eng, lambda e: e.tensor_copy(out=out, in_=in_), reads, writes)


class G:
    pass


VEC_COLS = {}
_off = 0
for _n, _w in [("norm1_w", 8), ("norm2_w", 8), ("b_in", 40), ("hcw0", 12), ("hcw1", 12), ("hcw2", 12),
               ("hcb", 12), ("b_up", 44), ("fcw0", 44), ("fcw1", 44), ("fcw2", 44), ("fcb", 44), ("b_mod", 48)]:
    VEC_COLS[_n] = (_off, _w)
    _off += _w
NV = _off


def vcol(name, j=0):
    return VEC_COLS[name][0] + j


def phase_0(k, g):
    nc = k.nc
    k.stack = k.gstack
    g.ident = k.sbuf([128, 128], BF16, "ident")
    g.vec = k.sbuf([128, NV], F32, "vec")
    g.mods = k.sbuf([128, 6, 8], F32, "mods")
    g.g1row = k.sbuf([128, 1024], F32, "g1row")
    g.g2row = k.sbuf([128, 1024], F32, "g2row")
    k.dma("sp", g.ident, g.d_ident, [(g.ident[:, :], g.d_ident[:, :])])
    k.dma("sp", g.vec, g.d_vec, [(g.vec[:, :], g.d_vec[:, :])])
    with ExitStack() as ps:
        k.stack = ps
        cv = k.sbuf([128, 8, 2], F32, "cv")
        sc = k.sbuf([128, 8, 2], BF16, "sc")
        screp = k.sbuf([128, 8, 128], BF16, "screp")
        wm = [k.sbuf([128, 8, 1024], BF16, f"wm{i}") for i in range(2)]
        mT = k.sbuf([128, 48, 2], F32, "mT")
        brow = k.sbuf([128, 1024], F32, "brow")
        pm = [k.psum([128, 512], F32, f"pm{i}") for i in range(2)]
        pr = [k.psum([128, 512], F32, f"pr{i}") for i in range(2)]
        k.dma("sp", cv, g.d_cvecT, [(cv[:, :, :], g.d_cvecT[:, :, :])])
        k.act(sc[:, :, :], cv[:, :, :], AF.Silu, [cv], [sc])
        k.copy("dve", screp[:, :, :], sc[:, :, 0:1].to_broadcast([128, 8, 128]), [sc], [screp])
        wv = g.d_w_mod.t.rearrange("(kc p) n -> p kc n", p=128)
        for piece in range(6):
            w = wm[piece % 2]
            k.dma("pool", w, g.d_w_mod, [(w[:, :, :], wv[:, :, piece * 1024:(piece + 1) * 1024])])
            if piece in (0, 1, 3, 4):
                p_ = pm[piece % 2]
                for j in range(8):
                    for kc in range(8):
                        k.mm(p_[:, j * 2:j * 2 + 2], w[:, kc, j * 128:(j + 1) * 128], sc[:, kc, :],
                             kc == 0, kc == 7, [w, sc], [p_])
                bm = g.vec[:, vcol("b_mod", piece * 8):vcol("b_mod", piece * 8) + 8]
                k.tt("dve", mT[:, piece * 8:(piece + 1) * 8, :],
                     p_[:, 0:16].rearrange("p (j two) -> p j two", two=2),
                     bm.unsqueeze(2).to_broadcast([128, 8, 2]), ALU.add, [p_, g.vec], [mT])
            else:
                dst = g.g1row if piece == 2 else g.g2row
                k.dma("sp", brow, g.d_b_mod, [(brow[:, :], g.d_b_mod[piece * 1024:(piece + 1) * 1024].partition_broadcast(128))])
                for half in range(2):
                    p_ = pr[half]
                    for kc in range(8):
                        k.mm(p_[:, :], screp[:, kc, :], w[:, kc, half * 512:(half + 1) * 512],
                             kc == 0, kc == 7, [w, screp], [p_])
                    k.tt("dve", dst[:, half * 512:(half + 1) * 512], p_[:, :], brow[:, half * 512:(half + 1) * 512],
                         ALU.add, [p_, brow], [dst])
        n1 = g.vec[:, vcol("norm1_w"):vcol("norm1_w") + 8]
        n2 = g.vec[:, vcol("norm2_w"):vcol("norm2_w") + 8]
        k.stt(g.mods[:, 0, :], mT[:, 8:16, 0], 1.0, n1, ALU.add, ALU.mult, [mT, g.vec], [g.mods])
        k.copy("dve", g.mods[:, 1, :], mT[:, 0:8, 0], [mT], [g.mods])
        k.stt(g.mods[:, 2, :], mT[:, 8:16, 1], 1.0, n1, ALU.add, ALU.mult, [mT, g.vec], [g.mods])
        k.copy("dve", g.mods[:, 3, :], mT[:, 0:8, 1], [mT], [g.mods])
        k.stt(g.mods[:, 4, :], mT[:, 32:40, 0], 1.0, n2, ALU.add, ALU.mult, [mT, g.vec], [g.mods])
        k.copy("dve", g.mods[:, 5, :], mT[:, 24:32, 0], [mT], [g.mods])
        k.end_phase()
    k.stack = k.gstack


def norm_tile(k, g, xt, xn, ss, rs, junk):
    k.act(junk[:, :], xt[:, :], AF.Square, [xt], [junk, ss], accum_out=ss[:, 0:1])
    k.act(rs[:, :], ss[:, :], AF.Sqrt, [ss], [rs], scale=1.0 / D, bias=g.epsc[:, 0:1])
    k.op("dve", lambda e: e.reciprocal(out=rs[:, :], in_=rs[:, :]), [rs], [rs])
    k.act(xn[:, :], xt[:, :], AF.Identity, [xt, rs], [xn], scale=rs[:, 0:1])


def phase_A(k, g):
    with ExitStack() as ps:
        k.stack = ps
        win = k.sbuf([128, 8, 5120], BF16, "win")
        wv = g.d_w_in.t.rearrange("(kc p) n -> p kc n", p=128)
        for lo, hi in [(1024, 2048), (2048, 3072), (0, 1024), (3072, 4096), (4096, 5120)]:
            k.dma("pool", win, g.d_w_in, [(win[:, :, lo:hi], wv[:, :, lo:hi])])
        brow = k.sbuf([128, 1536], F32, "brow")
        wrow = k.sbuf([128, 1280], F32, "wrow")
        k.dma("sp", brow, g.d_b_in, [(brow[:, :], g.d_b_in[0:1536].partition_broadcast(128))])
        k.dma("sp", wrow, g.d_qkw, [(wrow[:, :], g.d_qkw[0:1280].partition_broadcast(128))])
        NX = 3
        xts = [k.sbuf([128, 1024], F32, f"xt{i}") for i in range(NX)]
        xns = [k.sbuf([128, 1024], BF16, f"xn{i}") for i in range(2)]
        junk = k.sbuf([128, 1024], BF16, "junk")
        sss = [k.sbuf([128, 1], F32, f"ss{i}") for i in range(2)]
        rss = [k.sbuf([128, 1], F32, f"rs{i}") for i in range(2)]
        hTbig = k.sbuf([128, 2, 8, 512], BF16, "hT")
        hTt = [[k.view(hTbig, f"hTt{s}_{j}") for j in range(4)] for s in range(2)]
        ropes = [k.sbuf([128, 2, 64], F32, f"rope{i}") for i in range(5)]
        qks = [k.sbuf([128, 1280], F32, f"qk{i}") for i in range(2)]
        sq = k.sbuf([128, 1280], F32, "sq")
        ss10 = k.sbuf([128, 10], F32, "ss10")
        rs10 = k.sbuf([128, 10], F32, "rs10")
        tmpa = k.sbuf([128, 10, 64], F32, "tmpa")
        tmpb = k.sbuf([128, 10, 64], F32, "tmpb")
        qrs = [k.sbuf([128, 1280], BF16, f"qr{i}") for i in range(2)]
        vts = [k.sbuf([128, 2, 129], BF16, f"vt{i}") for i in range(2)]
        qTs = [k.sbuf([128, 8, 128], BF16, f"qT{i}") for i in range(2)]
        kTs = [k.sbuf([128, 2, 128], BF16, f"kT{i}") for i in range(2)]
        fms = [k.sbuf([128, 512], BF16, f"fm{i}") for i in range(4)]
        ptr = k.psum([128, 8, 128], BF16, "ptr")
        pq = [k.psum([128, 512], F32, f"pq{i}") for i in range(2)]
        pkv = k.psum([128, 512], F32, "pkv")
        pfm = [k.psum([128, 512], F32, f"pfm{i}") for i in range(2)]
        pqT = k.psum([128, 8, 128], BF16, "pqT")
        pkT = k.psum([128, 2, 128], BF16, "pkT")
        for v in vts:
            k.op("dve", lambda e, v=v: e.memset(v[:, :, 128:129], 1.0), [], [v])

        NT = NKT
        fm_count = [0]

        def is_q(ti):
            return 2 <= ti < 2 + NQT

        def stage_load(ti):
            xt = xts[ti % NX]
            if ti < 2:
                src, dbuf = g.d_ctx.t[ti * 128:(ti + 1) * 128, :], g.d_ctx
            else:
                src, dbuf = g.d_xl.t[(ti - 2) * 128:(ti - 1) * 128, :], g.d_xl
            k.dma("sp", xt, dbuf, [(xt[:, :], src)])
            if ti >= 2:
                rp = ropes[ti % 5]
                k.dma("sp", rp, g.d_rope, [(rp[:, :, :], g.d_rope.t[(ti - 2) * 128:(ti - 1) * 128, :, :])])

        def slot(ti):
            if ti < 2:
                return 1, ti
            t = ti - 2
            return (t // 4) % 2, t % 4

        def stage_norm(ti):
            xt = xts[ti % NX]
            xn = xns[ti % 2]
            norm_tile(k, g, xt, xn, sss[ti % 2], rss[ti % 2], junk)

        def stage_front(ti):
            xn = xns[ti % 2]
            for kc in range(8):
                k.tr(ptr[:, kc, :], xn[:, kc * 128:(kc + 1) * 128], g.ident[:, :], [xn, g.ident], [ptr])
            s, j = slot(ti)
            hb = hTt[s][j]
            mi = 2 if ti < 2 else 0
            for kc in range(8):
                k.act(hTbig[:, s, kc, j * 128:(j + 1) * 128], ptr[:, kc, :], AF.Identity, [ptr, g.mods], [hb],
                      scale=g.mods[:, mi, kc:kc + 1], bias=g.mods[:, mi + 1, kc:kc + 1])

        def stage_front2(ti):
            s, j = slot(ti)
            hb = hTt[s][j]
            if is_q(ti):
                for half in range(2):
                    for kc in range(8):
                        k.mm(pq[half][:, :], hTbig[:, s, kc, j * 128:(j + 1) * 128],
                             win[:, kc, half * 512:(half + 1) * 512], kc == 0, kc == 7, [hb, win], [pq[half]])
            for kc in range(8):
                k.mm(pkv[:, :], hTbig[:, s, kc, j * 128:(j + 1) * 128], win[:, kc, 1024:1536],
                     kc == 0, kc == 7, [hb, win], [pkv])
            qk = qks[ti % 2]
            vt = vts[ti % 2]
            if is_q(ti):
                for half in range(2):
                    k.tt("dve", qk[:, half * 512:(half + 1) * 512], pq[half][:, :], brow[:, half * 512:(half + 1) * 512],
                         ALU.add, [pq[half], brow], [qk])
            k.tt("dve", qk[:, 1024:1280], pkv[:, 0:256], brow[:, 1024:1280], ALU.add, [pkv, brow], [qk])
            k.tt("dve", vt[:, :, 0:128], pkv[:, 256:512].rearrange("p (g d) -> p g d", d=128),
                 brow[:, 1280:1536].rearrange("p (g d) -> p g d", d=128), ALU.add, [pkv, brow], [vt])
            k.dma("pool", vt, g.Vd, [(g.Vd.t[ti, :, :], vt[:, :, :].rearrange("p g c -> p (g c)"))], load=False)

        def stage_back(ti):
            qk = qks[ti % 2]
            qr = qrs[ti % 2]
            lo = 0 if is_q(ti) else 1024
            nh = (1280 - lo) // 128
            h0 = lo // 128
            k.tt("dve", sq[:, lo:1280], qk[:, lo:1280], qk[:, lo:1280], ALU.mult, [qk], [sq])
            k.op("dve", lambda e: e.tensor_reduce(out=ss10[:, h0:10], in_=sq[:, lo:1280].rearrange("p (h d) -> p h d", d=128),
                                                  op=ALU.add, axis=AX.X), [sq], [ss10])
            k.act(rs10[:, h0:10], ss10[:, h0:10], AF.Sqrt, [ss10], [rs10], scale=1.0 / 128, bias=g.epsc[:, 0:1])
            k.op("dve", lambda e: e.reciprocal(out=rs10[:, h0:10], in_=rs10[:, h0:10]), [rs10], [rs10])
            qv = qk[:, lo:1280].rearrange("p (h d) -> p h d", d=128)
            k.tt("dve", qv, qv, rs10[:, h0:10].unsqueeze(2).to_broadcast([128, nh, 128]), ALU.mult, [qk, rs10], [qk])
            if ti < 2:
                k.tt("dve", qr[:, lo:1280], qk[:, lo:1280], wrow[:, lo:1280], ALU.mult, [qk, wrow], [qr])
            else:
                k.tt("dve", qk[:, lo:1280], qk[:, lo:1280], wrow[:, lo:1280], ALU.mult, [qk, wrow], [qk])
                rp = ropes[ti % 5]
                q4 = qk[:, lo:1280].rearrange("p (h i two) -> p h i two", i=64, two=2)
                o4 = qr[:, lo:1280].rearrange("p (h i two) -> p h i two", i=64, two=2)
                x1, x2 = q4[:, :, :, 0], q4[:, :, :, 1]
                cs = rp[:, 0, :].unsqueeze(1).to_broadcast([128, nh, 64])
                sn = rp[:, 1, :].unsqueeze(1).to_broadcast([128, nh, 64])
                ta, tb = tmpa[:, h0:10, :], tmpb[:, h0:10, :]
                k.tt("dve", ta, x1, cs, ALU.mult, [qk, rp], [tmpa])
                k.tt("dve", tb, x2, sn, ALU.mult, [qk, rp], [tmpb])
                k.tt("dve", o4[:, :, :, 0], ta, tb, ALU.subtract, [tmpa, tmpb], [qr])
                k.tt("dve", ta, x1, sn, ALU.mult, [qk, rp], [tmpa])
                k.tt("dve", tb, x2, cs, ALU.mult, [qk, rp], [tmpb])
                k.tt("dve", o4[:, :, :, 1], ta, tb, ALU.add, [tmpa, tmpb], [qr])

        def stage_back2(ti):
            qr = qrs[ti % 2]
            kT = kTs[ti % 2]
            for gi in range(2):
                k.tr(pkT[:, gi, :], qr[:, 1024 + gi * 128:1024 + (gi + 1) * 128], g.ident[:, :], [qr, g.ident], [pkT])
            k.copy("act", kT[:, :, :], pkT[:, :, :], [pkT], [kT])
            k.dma("pool", kT, g.Kd, [(g.Kd.t[:, :, ti * 128:(ti + 1) * 128], kT[:, :, :])], load=False)
            if is_q(ti):
                qT = qTs[ti % 2]
                for h in range(8):
                    k.tr(pqT[:, h, :], qr[:, h * 128:(h + 1) * 128], g.ident[:, :], [qr, g.ident], [pqT])
                k.copy("act", qT[:, :, :], pqT[:, :, :], [pqT], [qT])
                k.dma("pool", qT, g.Qd, [(g.Qd.t[ti - 2, :, :], qT[:, :, :].rearrange("p h q -> p (h q)"))], load=False)

        def stage_fm(st):
            s = st % 2
            hbs = hTt[s]
            nq = max(0, min(4, NQT - st * 4))
            for j in range(12):
                pf = pfm[fm_count[0] % 2]
                fb = fms[fm_count[0] % 4]
                fm_count[0] += 1
                for kc in range(8):
                    k.mm(pf[:, :], win[:, kc, 1536 + j * 128:1536 + (j + 1) * 128], hTbig[:, s, kc, :],
                         kc == 0, kc == 7, hbs + [win], [pf])
                bc = vcol("b_in", 12 + j)
                k.act(fb[:, :], pf[:, :], AF.Identity, [pf, g.vec], [fb], bias=g.vec[:, bc:bc + 1])
                k.dma("pool", fb, g.HSd, [(g.HSd.t[j * 128:(j + 1) * 128, st * 512:(st + 1) * 512], fb[:, :])], load=False)
                yield
            if nq:
                n = nq * 128
                for j in range(16):
                    pf = pfm[fm_count[0] % 2]
                    fb = fms[fm_count[0] % 4]
                    fm_count[0] += 1
                    for kc in range(8):
                        k.mm(pf[:, 0:n], win[:, kc, 3072 + j * 128:3072 + (j + 1) * 128], hTbig[:, s, kc, 0:n],
                             kc == 0, kc == 7, hbs[:nq] + [win], [pf])
                    bc = vcol("b_in", 24 + j)
                    k.act(fb[:, 0:n], pf[:, 0:n], AF.Sigmoid, [pf, g.vec], [fb], bias=g.vec[:, bc:bc + 1])
                    k.dma("pool", fb, g.SGd, [(g.SGd.t[j * 128:(j + 1) * 128, st * 512:st * 512 + n], fb[:, 0:n])], load=False)
                    yield

        fm_gens = []

        def pump_fm(n):
            while n > 0 and fm_gens:
                try:
                    next(fm_gens[0])
                    n -= 1
                except StopIteration:
                    fm_gens.pop(0)

        stage_load(0)
        for it in range(NT + 4):
            if it + 1 < NT:
                stage_load(it + 1)
            if it < NT:
                stage_norm(it)
            if 3 <= it <= NT + 2:
                stage_back(it - 3)
            if 1 <= it <= NT:
                stage_front(it - 1)
            if 2 <= it <= NT + 1:
                stage_front2(it - 2)
                t = it - 2 - 2
                if t >= 0 and t % 4 == 3:
                    fm_gens.append(stage_fm(t // 4))
            pump_fm(8)
            if 3 <= it <= NT + 2:
                stage_back2(it - 3)
        pump_fm(10 ** 6)
        k.end_phase()
    k.stack = k.gstack


WARM_N = 0
TWO_PI = 2.0 * math.pi
MAGIC = 12582912.0
PI_LO = 3.1415925


def phase_B0(k, g):
    prev = k.stack
    with ExitStack() as ps:
        k.stack = ps
        us = [k.sbuf([128, T + 2], BF16, f"u{i}") for i in range(2)]
        accs = [k.sbuf([128, T], F32, f"acc{i}") for i in range(2)]
        outs = [k.sbuf([128, T], BF16, f"o{i}") for i in range(2)]
        dgs = [k.sbuf([128, 3, 128], BF16, f"dg{i}") for i in range(2)]
        pcs = [k.psum([128, 512], F32, f"pc{i}") for i in range(4)]
        for u in us:
            k.op("dve", lambda e, u=u: e.memset(u[:, 0:1], 0.0), [], [u])
            k.op("dve", lambda e, u=u: e.memset(u[:, T + 1:T + 2], 0.0), [], [u])
        n = 0
        o_views = [(o_, k.view(o_, "oB")) for o_ in outs]

        def load_u(j):
            u = us[j % 2]
            k.dma("sp", u, g.HSd, [(u[:, 1:T + 1], g.HSd.t[j * 128:(j + 1) * 128, :])])

        load_u(0)
        for j in range(12):
            u = us[j % 2]
            o, oB = o_views[j % 2]
            acc = accs[j % 2]
            dg = dgs[j % 2]
            if j + 1 < 12:
                load_u(j + 1)
            for tap, nm in enumerate(("hcw0", "hcw1", "hcw2")):
                wc = g.vec[:, vcol(nm, j):vcol(nm, j) + 1]
                k.act(dg[:, tap, :], g.ident[:, :], AF.Identity, [g.ident, g.vec], [dg], scale=wc)
            cb = g.vec[:, vcol("hcb", j):vcol("hcb", j) + 1]
            for m in range(T // 512):
                p_ = pcs[n % 4]
                n += 1
                for tap in range(3):
                    k.mm(p_[:, :], dg[:, tap, :], u[:, m * 512 + tap:m * 512 + tap + 512], tap == 0, tap == 2, [dg, u], [p_])
                k.act(acc[:, m * 512:(m + 1) * 512], p_[:, :], AF.Identity, [p_, g.vec], [acc], bias=cb)
            ov = o[:, :].rearrange("p (n2 n1) -> p n2 n1", n1=64)
            av = acc[:, :].rearrange("p (n1 n2) -> p n2 n1", n2=128)
            k.copy("dve", ov[:, 0:96, :], av[:, 0:96, :], [acc], [o])
            k.copy("pool", ov[:, 96:128, :], av[:, 96:128, :], [acc], [oB])
            k.dma("pool", o, g.HCd, [(g.HCd.t[j * 128:(j + 1) * 128, 0:96 * 64], o[:, 0:96 * 64])], load=False)
            k.dma("pool", oB, g.HCd, [(g.HCd.t[j * 128:(j + 1) * 128, 96 * 64:T], o[:, 96 * 64:T])], load=False)
        k.end_phase()
    k.stack = prev


def phase_BM(k, g):
    prev = k.stack
    with ExitStack() as ps:
        k.stack = ps
        w1 = k.sbuf([33, 64], F32, "fw1")
        wi = k.sbuf([64, 2, 64], F32, "fwi")
        fv = k.sbuf([64, 8], F32, "fv")
        wo = k.sbuf([64, 2048], F32, "fwo")
        k.dma("sp", w1, g.d_filt_w1, [(w1[:, :], g.d_filt_w1[:, :])])
        k.dma("sp", wi, g.d_filt_wi, [(wi[:, :, :], g.d_filt_wi[:, :, :])])
        k.dma("sp", fv, g.d_filt_vec, [(fv[:, :], g.d_filt_vec[:, :])])
        k.dma("sp", wo, g.d_filt_w_out, [(wo[:, :], g.d_filt_w_out[:, :])])
        k.ts("dve", fv[:, 4:7], fv[:, 1:4], fv[:, 0:1], None, ALU.mult, ALU.bypass, [fv], [fv])
        for o in range(2):
            f_ = wo[:, o * 512:(o + 1) * 512].rearrange("p (cb c) -> p cb c", c=64)
            b_ = wo[:, 1024 + o * 512:1024 + (o + 1) * 512].rearrange("p (cb c) -> p cb c", c=64)
            k.tt("dve", g.WSD[:, o, :, 0, :], f_, b_, ALU.add, [wo], [g.WSD])
            k.tt("dve", g.WSD[:, o, :, 1, :], f_, b_, ALU.subtract, [wo], [g.WSD])
        zs = [k.sbuf([33, 512], F32, f"z{i}") for i in range(4)]
        pres = [k.sbuf([64, 512], F32, f"pre{i}") for i in range(2)]
        rrs = [k.sbuf([64, 512], F32, f"rr{i}") for i in range(2)]
        hss = [[k.sbuf([64, 512], F32, f"h{p}_{i}") for i in range(2)] for p in range(2)]
        pps = [[k.psum([64, 512], F32, f"pp{p}_{i}") for i in range(2)] for p in range(2)]
        for cp in range(8):
            for par in range(2):
                ch = 2 * cp + par
                z = zs[ch % 4]
                k.dma("sp", z, g.d_zT, [(z[:, :], g.d_zT[:, ch * 512:(ch + 1) * 512])])
            for layer in range(3):
                for par in range(2):
                    ch = 2 * cp + par
                    z = zs[ch % 4]
                    pre, rr, hs, pp = pres[par], rrs[par], hss[par], pps[par]
                    p_ = pp[layer % 2]
                    if layer == 0:
                        k.mm(p_[:, :], w1[:, :], z[:, :], True, True, [w1, z], [p_])
                    else:
                        hin = hs[(layer - 1) % 2]
                        k.mm(p_[:, :], wi[:, layer - 1, :], hin[:, :], True, True, [wi, hin], [p_])
                    k.ts("dve", pre[:, :], p_[:, :], fv[:, 0:1], fv[:, 4 + layer:5 + layer], ALU.mult, ALU.add, [p_, fv], [pre])
                    k.ts("dve", rr[:, :], pre[:, :], 1.0 / TWO_PI, MAGIC, ALU.mult, ALU.add, [pre], [rr])
                    k.ts("dve", rr[:, :], rr[:, :], -MAGIC, None, ALU.add, ALU.bypass, [rr], [rr])
                    k.stt(pre[:, :], rr[:, :], -TWO_PI, pre[:, :], ALU.mult, ALU.add, [rr, pre], [pre])
                    k.ts("dve", pre[:, :], pre[:, :], -PI_LO, PI_LO, ALU.max, ALU.min, [pre], [pre])
                    if layer < 2:
                        ho = hs[layer % 2]
                        k.act(ho[:, :], pre[:, :], AF.Sin, [pre], [ho])
                    else:
                        k.act(g.HF[:, ch * 512:(ch + 1) * 512], pre[:, :], AF.Sin, [pre], [g.HF])
        k.end_phase()
    k.stack = prev


def phase_B1(k, g):
    prev = k.stack
    with ExitStack() as ps:
        k.stack = ps
        F2D = k.sbuf([128, 128, 384], BF16, "F2D")
        C1 = k.sbuf([64, 256], BF16, "C1")
        k.dma("sp", C1, g.d_C1, [(C1[:, :], g.d_C1[:, :])])
        k.dma("sp", F2D, g.d_F2D, [(F2D[:, q4 * 16:(q4 + 1) * 16, :], g.d_F2D.t[:, q4 * 16:(q4 + 1) * 16, :]) for q4 in range(8)])
        hTf = k.sbuf([64, 2, 128, 64], BF16, "hTf")
        Af = k.sbuf([128, 64, 256], BF16, "Af")
        dcs = [k.sbuf([64, 8, 64], F32, f"dc{i}") for i in range(2)]
        stg = [k.sbuf([128, 8, 64], BF16, f"stg{i}") for i in range(4)]
        pf = [k.psum([64, 4, 128], F32, f"pf{i}") for i in range(2)]
        pa = [k.psum([128, 2, 256], F32, f"pa{i}") for i in range(2)]
        px = [k.psum([128, 8, 64], F32, f"px{i}") for i in range(2)]
        cnt = {"pf": 0, "pa": 0, "px": 0, "dc": 0, "stg": 0, "ev": 0}

        def evac(out, in_, reads, writes):
            e = "act" if cnt["ev"] % 2 == 0 else "dve"
            cnt["ev"] += 1
            k.copy(e, out, in_, reads, writes)

        for o in range(2):
            for cb in range(8):
                c0 = cb * 64
                for dg in range(16):
                    dc = dcs[cnt["dc"] % 2]
                    cnt["dc"] += 1
                    k.dma("sp", dc, g.d_decay, [(dc[:, :, :], g.d_decay.t[:, dg * 8:(dg + 1) * 8, c0:c0 + 64])])
                    for half in range(2):
                        p_ = pf[cnt["pf"] % 2]
                        cnt["pf"] += 1
                        for i in range(4):
                            n2 = dg * 8 + half * 4 + i
                            k.mm(p_[:, i, :], g.HF[:, n2 * 64:(n2 + 1) * 64],
                                 g.WSD[:, o, cb, :, :].rearrange("p s c -> p (s c)"), True, True, [g.HF, g.WSD], [p_])
                        n0 = dg * 8 + half * 4
                        k.tt("dve", hTf[:, :, n0:n0 + 4, :],
                             p_[:, :, :].rearrange("p n (s c) -> p s n c", s=2),
                             dc[:, half * 4:half * 4 + 4, :].unsqueeze(1).to_broadcast([64, 2, 4, 64]),
                             ALU.mult, [p_, dc], [hTf])
                for sd in range(2):
                    for cp in range(32):
                        p_ = pa[cnt["pa"] % 2]
                        cnt["pa"] += 1
                        for i in range(2):
                            k.mm(p_[:, i, :], hTf[:, sd, :, 2 * cp + i], C1[:, :], True, True, [hTf, C1], [p_])
                        evac(Af[:, 2 * cp:2 * cp + 2, :], p_[:, :, :], [p_], [Af])
                    dst = g.KRd if sd == 0 else g.KId
                    for kg in range(16):
                        p_ = px[cnt["px"] % 2]
                        cnt["px"] += 1
                        for i in range(8):
                            k1 = kg * 8 + i
                            ar = Af[:, :, k1]
                            ai = Af[:, :, 128 + k1]
                            if sd == 0:
                                la, lb = F2D[:, k1, 128:256], F2D[:, k1, 0:128]
                            else:
                                la, lb = F2D[:, k1, 256:384], F2D[:, k1, 128:256]
                            k.mm(p_[:, i, :], la, ar, True, False, [F2D, Af], [p_])
                            k.mm(p_[:, i, :], lb, ai, False, True, [F2D, Af], [p_])
                        sb_ = stg[cnt["stg"] % 4]
                        cnt["stg"] += 1
                        evac(sb_[:, :, :], p_[:, :, :], [p_], [sb_])
                        k.dma("pool", sb_, dst, [(dst.t[o, cb, :, kg * 8:(kg + 1) * 8, :], sb_[:, :, :])], load=False)
        k.end_phase()
    k.stack = prev


def phase_B2(k, g):
    prev = k.stack
    with ExitStack() as ps:
        k.stack = ps
        F2T = k.sbuf([128, 128, 192], BF16, "F2T")
        C1 = k.sbuf([64, 256], BF16, "C1")
        R12 = k.sbuf([128, 2, 256], BF16, "R12")
        skp = k.sbuf([64, 2, 8], F32, "skp")
        k.dma("sp", C1, g.d_C1, [(C1[:, :], g.d_C1[:, :])])
        k.dma("sp", R12, g.d_R12, [(R12[:, :, :], g.d_R12[:, :, :])])
        k.dma("sp", skp, g.d_skipT, [(skp[:, :, :], g.d_skipT[:, :, :])])
        for q4 in range(4):
            k.dma("sp", F2T, g.d_F2T, [(F2T[:, q4 * 32:(q4 + 1) * 32, :], g.d_F2T.t[:, q4 * 32:(q4 + 1) * 32, :])])
        R1 = k.sbuf([64, T], BF16, "R1")
        BIG1 = k.sbuf([128, 2 * T], BF16, "BIG1")
        BIG2 = k.sbuf([128, 2 * T], BF16, "BIG2")
        uT = BIG1[0:64, 0:T].rearrange("p (n c) -> p n c", c=64)
        P1 = BIG1[:, 0:T].rearrange("p (k c) -> p k c", c=64)
        P2 = BIG1[:, T:2 * T].rearrange("p (k c) -> p k c", c=64)
        A = BIG2[:, :].rearrange("p (c r) -> p c r", r=256)
        B = BIG2[:, :].rearrange("p (c r n) -> p c r n", r=2, n=128)
        HY = k.sbuf([64, NQT, 128], BF16, "HY")
        krs = [k.sbuf([128, 8, 64], BF16, f"kr{i}") for i in range(2)]
        kis = [k.sbuf([128, 8, 64], BF16, f"ki{i}") for i in range(2)]
        i2s = [k.sbuf([128, 8, 128], BF16, f"i2{i}") for i in range(2)]
        xgs = [k.sbuf([64, 8, 64], BF16, f"xg{i}") for i in range(2)]
        tmp = k.sbuf([64, 8, 64], F32, "tmp")
        ptb = [k.psum([64, 16, 64], BF16, f"ptb{i}") for i in range(2)]
        pg = [k.psum([128, 512], F32, f"pg{i}") for i in range(6)]
        cnt = {"pg": 0, "ev": 0, "ld": 0}

        def bank():
            p_ = pg[cnt["pg"] % 6]
            cnt["pg"] += 1
            return p_

        def evac(out, in_, reads, writes):
            e = "act" if cnt["ev"] % 2 == 0 else "dve"
            cnt["ev"] += 1
            k.copy(e, out, in_, reads, writes)

        for cb in range(8):
            c0 = cb * 64
            k.dma("sp", R1, g.HCd, [(R1[:, :], g.HCd.t[1024 + c0:1024 + c0 + 64, :])])
            for o in range(2):
                NN = 64 if o == 0 else NQT
                for ng in range(8):
                    p_ = ptb[ng % 2]
                    for i in range(16):
                        n2 = ng * 16 + i
                        k.tr(p_[:, i, :], R1[:, n2 * 64:(n2 + 1) * 64], g.ident[0:64, 0:64], [R1, g.ident], [p_])
                    evac(uT[:, ng * 16:(ng + 1) * 16, :], p_[:, :, :], [p_], [BIG1])
                for cp in range(32):
                    p_ = bank()
                    pv = p_[:, :].rearrange("p (i r) -> p i r", r=256)
                    for i in range(2):
                        k.mm(pv[:, i, :], uT[:, :, 2 * cp + i], C1[:, :], True, True, [BIG1, C1], [p_])
                    evac(A[:, 2 * cp:2 * cp + 2, :], pv, [p_], [BIG2])
                for kg in range(16):
                    kr = krs[kg % 2]
                    ki = kis[kg % 2]
                    k.dma("sp", kr, g.KRd, [(kr[:, :, :], g.KRd.t[o, cb, :, kg * 8:(kg + 1) * 8, :])])
                    k.dma("sp", ki, g.KId, [(ki[:, :, :], g.KId.t[o, cb, :, kg * 8:(kg + 1) * 8, :])])
                    p_ = bank()
                    pv = p_[:, :].rearrange("p (i c) -> p i c", c=64)
                    for i in range(8):
                        k1 = kg * 8 + i
                        k.mm(pv[:, i, :], F2T[:, k1, 64:192], A[:, :, k1], True, False, [F2T, BIG2], [p_])
                        k.mm(pv[:, i, :], F2T[:, k1, 0:128], A[:, :, 128 + k1], False, True, [F2T, BIG2], [p_])
                    k.tt("dve", P1[:, kg * 8:(kg + 1) * 8, :], pv, kr[:, :, :], ALU.mult, [p_, kr], [BIG1])
                    k.tt("dve", P2[:, kg * 8:(kg + 1) * 8, :], pv, ki[:, :, :], ALU.mult, [p_, ki], [BIG1])
                for cp in range(32):
                    p_ = bank()
                    pv = p_[:, :].rearrange("p (i r) -> p i r", r=256)
                    for i in range(2):
                        c = 2 * cp + i
                        k.mm(pv[:, i, :], P1[:, :, c], R12[:, 0, :], True, False, [BIG1, R12], [p_])
                        k.mm(pv[:, i, :], P2[:, :, c], R12[:, 1, :], False, True, [BIG1, R12], [p_])
                    evac(B[:, 2 * cp:2 * cp + 2, :, :], p_[:, :].rearrange("p (i r n) -> p i r n", r=2, n=128), [p_], [BIG2])
                xrow = (0 if o == 0 else 512) + c0
                for ng in range(16):
                    i2 = i2s[ng % 2]
                    xg = xgs[ng % 2]
                    k.dma("sp", i2, g.d_I2T, [(i2[:, :, :], g.d_I2T.t[:, ng * 8:(ng + 1) * 8, :])])
                    k.dma("sp", xg, g.HCd, [(xg[:, :, :].rearrange("p a b -> p (a b)"), g.HCd.t[xrow:xrow + 64, ng * 512:(ng + 1) * 512])])
                    p_ = bank()
                    pv = p_[0:64, :].rearrange("p (i n) -> p i n", n=64)
                    for i in range(8):
                        n2 = ng * 8 + i
                        k.mm(pv[:, i, 0:NN], B[:, :, 0, n2], i2[:, i, 0:NN], True, False, [BIG2, i2], [p_])
                        k.mm(pv[:, i, 0:NN], B[:, :, 1, n2], i2[:, i, 64:64 + NN], False, True, [BIG2, i2], [p_])
                    zin = R1[:, ng * 512:(ng + 1) * 512].rearrange("p (a b) -> p a b", b=64)
                    k.stt(tmp[:, :, 0:NN], zin[:, :, 0:NN], skp[:, o, cb:cb + 1], pv[:, :, 0:NN], ALU.mult, ALU.add,
                          [R1, skp, p_], [tmp])
                    if o == 0:
                        k.tt("dve", zin, tmp[:, :, :], xg[:, :, :], ALU.mult, [tmp, xg], [R1])
                    else:
                        k.tt("dve", HY[:, :, ng * 8:(ng + 1) * 8].rearrange("p n a -> p a n"), tmp[:, :, 0:NN], xg[:, :, 0:NN],
                             ALU.mult, [tmp, xg], [HY])
            k.dma("pool", HY, g.HYd, [(g.HYd.t[c0:c0 + 64, :], HY[:, :, :].rearrange("p n a -> p (n a)"))], load=False)
        k.end_phase()
    k.stack = prev


def phase_B(k, g):
    prev = k.stack
    with ExitStack() as bs:
        k.stack = bs
        g.HF = k.sbuf([64, T], BF16, "HF")
        g.WSD = k.sbuf([64, 2, 8, 2, 64], BF16, "WSD")
        phase_BM(k, g)
        phase_B1(k, g)
        k.end_phase()
    k.stack = prev
    phase_B0(k, g)
    phase_B2(k, g)


def phase_C(k, g):
    prev = k.stack
    with ExitStack() as ps_:
        k.stack = ps_
        KT = k.sbuf([128, 2, NKEY], BF16, "KT")
        Vs = k.sbuf([128, NKT, 258], BF16, "Vs")
        wao = k.sbuf([128, 8, 1024], BF16, "wao")
        who = k.sbuf([128, 4, 1024], BF16, "who")
        wo = k.sbuf([128, 8, 1024], BF16, "wo")
        bg1 = k.sbuf([128, 1024], F32, "bg1")
        k.dma("sp", KT, g.Kd, [(KT[:, g_, :], g.Kd.t[:, g_, :]) for g_ in range(2)])
        k.dma("sp", Vs, g.Vd, [(Vs[:, t0:t0 + 22, :], g.Vd.t[t0:t0 + 22, :, :].rearrange("t p c -> p t c")) for t0 in (0, 22, 44)])
        k.dma("pool", wao, g.d_w_attn_out, [(wao[:, :, :], g.d_w_attn_out.t.rearrange("(kc p) n -> p kc n", p=128))])
        k.dma("pool", who, g.d_w_hy_out, [(who[:, :, :], g.d_w_hy_out.t.rearrange("(kc p) n -> p kc n", p=128))])
        k.dma("pool", wo, g.d_w_o, [(wo[:, :, :], g.d_w_o.t.rearrange("(kc p) n -> p kc n", p=128))])
        k.dma("sp", bg1, g.d_b_o, [(bg1[:, :], g.d_b_o[0:1024].partition_broadcast(128))])
        k.tt("dve", bg1[:, :], bg1[:, :], g.g1row[:, :], ALU.mult, [bg1, g.g1row], [bg1])
        Qts = [k.sbuf([128, 8, 128], BF16, f"Qt{i}") for i in range(2)]
        PTs = [k.sbuf([128, 512], BF16, f"PT{i}") for i in range(8)]
        recs = [k.sbuf([128, 512], F32, f"rec{i}") for i in range(2)]
        ssums = [k.sbuf([128, 512], F32, f"ssum{i}") for i in range(2)]
        ss2 = k.sbuf([128, 2], F32, "ss2")
        aoTs = [k.sbuf([128, 8, 128], BF16, f"aoT{i}") for i in range(2)]
        hyT = [k.sbuf([128, 4, 128], BF16, f"hyT{i}") for i in range(2)]
        sgs = [k.sbuf([128, 16, 128], BF16, f"sg{i}") for i in range(2)]
        xqs = [k.sbuf([128, 1024], F32, f"xq{i}") for i in range(2)]
        t1 = k.sbuf([128, 512], F32, "t1")
        t2 = k.sbuf([128, 512], F32, "t2")
        mixT = k.sbuf([128, 8, 128], BF16, "mixT")
        xnew = [k.sbuf([128, 1024], F32, f"xnew{i}") for i in range(2)]
        xn2 = k.sbuf([128, 1024], F32, "xn2")
        junk = k.sbuf([128, 1024], BF16, "junkc")
        ss = k.sbuf([128, 1], F32, "ssc")
        rs = k.sbuf([128, 1], F32, "rsc")
        h2T = [k.sbuf([128, 8, 128], BF16, f"h2T{i}") for i in range(2)]
        psb = [k.psum([128, 512], F32, f"ps{i}") for i in range(3)]
        pacc = [k.psum([128, 512], F32, f"pacc{i}") for i in range(2)]
        psum_s = k.psum([128, 512], F32, "psum_s")
        pm = [k.psum([128, 512], F32, f"pm{i}") for i in range(2)]
        cnt = {"s": 0}
        scale = 128.0 ** -0.5
        H2v = g.H2d.t.rearrange("kc p t -> p kc t")
        HYv = g.HYd.t.rearrange("(j p) t -> p j t", p=128)
        SGv = g.SGd.t.rearrange("(j p) t -> p j t", p=128)

        def loadQ(qi):
            Qt = Qts[qi % 2]
            k.dma("sp", Qt, g.Qd, [(Qt[:, :, :].rearrange("p h q -> p (h q)"), g.Qd.t[qi, :, :])])

        def loadM(qi):
            k.dma("sp", hyT[qi % 2], g.HYd, [(hyT[qi % 2][:, :, :], HYv[:, :, qi * 128:(qi + 1) * 128])])
            k.dma("sp", sgs[qi % 2], g.SGd, [(sgs[qi % 2][:, :, :], SGv[:, :, qi * 128:(qi + 1) * 128])])
            k.dma("sp", xqs[qi % 2], g.d_xl, [(xqs[qi % 2][:, :], g.d_xl.t[qi * 128:(qi + 1) * 128, :])])

        pend = {"q": [], "prevPT": None, "npair": 0}
        onesb = k.sbuf([128, 128], BF16, "onesb")
        k.op("dve", lambda e: e.memset(onesb[:, :], 1.0), [], [onesb])
        pairs = [k.sbuf([128, 512], BF16, f"pair{i}") for i in range(6)]

        def emit_pv(p):
            qi, g_, kt, PT = p
            pa_ = pacc[g_]
            k.mm(pa_[:, :], Vs[:, kt, g_ * 129:g_ * 129 + 128], PT[:, :], kt == 0, kt == NKT - 1, [PT, Vs], [pa_])
            if kt % 2 == 0:
                pend["prevPT"] = PT
            else:
                PTa = pend["prevPT"]
                pb = pairs[pend["npair"] % 6]
                pend["npair"] += 1
                k.tt("dve", pb[:, :], PTa[:, :], PT[:, :], ALU.add, [PTa, PT], [pb])
                if kt == NKT - 1:
                    sumq.append((pb, kt))
                elif kt % 4 == 1:
                    pend["prevpair"] = pb
                else:
                    pa2 = pend["prevpair"]
                    k.tt("dve", pb[:, :], pa2[:, :], pb[:, :], ALU.add, [pa2, pb], [pb])
                    sumq.append((pb, kt))
            flush_sums(keep=0 if kt == NKT - 1 else 2)
            if kt == NKT - 1:
                gens.append(norm_gen(qi, g_))

        sumq = []

        def flush_sums(keep):
            while len(sumq) > keep:
                pb, kt = sumq.pop(0)
                k.mm(psum_s[:, :], onesb[:, :], pb[:, :], kt == 3, kt == NKT - 1, [onesb, pb], [psum_s])

        def flush_pv(keep=0):
            while len(pend["q"]) > keep:
                emit_pv(pend["q"].pop(0))

        gens = []
        ncnt = {"n": 0}

        def norm_gen(qi, g_):
            sm = ssums[ncnt["n"] % 2]
            rc = recs[ncnt["n"] % 2]
            ncnt["n"] += 1
            pa_ = pacc[g_]
            aT = aoTs[qi % 2]
            k.copy("dve", sm[:, :], psum_s[:, :], [psum_s], [sm])
            yield
            for j in range(4):
                sl = slice(j * 128, (j + 1) * 128)
                k.op("dve", lambda e, sl=sl: e.reciprocal(out=rc[:, sl], in_=sm[:, sl]), [sm], [rc])
                yield
                k.tt("dve", aT[:, 4 * g_ + j, :], pa_[:, sl], rc[:, sl], ALU.mult, [pa_, rc], [aT])
                yield

        def pump():
            for gen in list(gens):
                try:
                    next(gen)
                except StopIteration:
                    gens.remove(gen)

        def drain():
            while gens:
                pump()

        def attn_group(qi, g_):
            Qt = Qts[qi % 2]
            rhsq = Qt[:, 4 * g_:4 * g_ + 4, :].rearrange("p h q -> p (h q)")
            for kt in range(NKT):
                p_ = psb[cnt["s"] % 3]
                PT = PTs[cnt["s"] % 8]
                cnt["s"] += 1
                k.mm(p_[:, :], KT[:, g_, kt * 128:(kt + 1) * 128], rhsq, True, True, [KT, Qt], [p_])
                k.act(PT[:, :], p_[:, :], AF.Exp, [p_], [PT], scale=scale)
                flush_pv(keep=1)
                pend["q"].append((qi, g_, kt, PT))
                if g_ == 0 and kt == 12 and qi >= 1:
                    gens.append(merge_gen(qi - 1, qi + 1 if qi + 1 < NQT else None))
                pump()

        def merge_gen(qi, next_load=None):
            aoT = aoTs[qi % 2]
            sg = sgs[qi % 2]
            hy = hyT[qi % 2]
            xq = xqs[qi % 2]
            xnw = xnew[qi % 2]
            for r in range(2):
                pA = pm[0][:, :].rearrange("p (j q) -> p j q", q=128)
                pH = pm[1][:, :].rearrange("p (j q) -> p j q", q=128)
                for i in range(4):
                    j = 4 * r + i
                    for kc in range(8):
                        k.mm(pA[:, i, :], wao[:, kc, j * 128:(j + 1) * 128], aoT[:, kc, :], kc == 0, kc == 7, [wao, aoT], [pm[0]])
                    yield
                for i in range(4):
                    j = 4 * r + i
                    for kc in range(4):
                        k.mm(pH[:, i, :], who[:, kc, j * 128:(j + 1) * 128], hy[:, kc, :], kc == 0, kc == 3, [who, hy], [pm[1]])
                    yield
                k.tt("dve", t1[:, :].rearrange("p (j q) -> p j q", q=128), pA, sg[:, 4 * r:4 * r + 4, :], ALU.mult, [pm[0], sg], [t1])
                yield
                k.tt("dve", t2[:, :].rearrange("p (j q) -> p j q", q=128), pH, sg[:, 8 + 4 * r:12 + 4 * r, :], ALU.mult, [pm[1], sg], [t2])
                yield
                k.tt("dve", mixT[:, 4 * r:4 * r + 4, :], t1[:, :].rearrange("p (j q) -> p j q", q=128),
                     t2[:, :].rearrange("p (j q) -> p j q", q=128), ALU.add, [t1, t2], [mixT])
                yield
            k.tt("pool", xq[:, :], xq[:, :], bg1[:, :], ALU.add, [xq, bg1], [xq])
            for half in range(2):
                for j in range(8):
                    k.mm(pm[half][:, :], mixT[:, j, :], wo[:, j, half * 512:(half + 1) * 512], j == 0, j == 7, [mixT, wo], [pm[half]])
                    if j % 2 == 1:
                        yield
            for half in range(2):
                sl = slice(half * 512, (half + 1) * 512)
                k.tt("dve", t1[:, :], pm[half][:, :], g.g1row[:, sl], ALU.mult, [pm[half], g.g1row], [t1])
                yield
                k.tt("dve", xnw[:, sl], t1[:, :], xq[:, sl], ALU.add, [t1, xq], [xnw])
                yield
            k.dma("pool", xnw, g.XNd, [(g.XNd.t[qi * 128:(qi + 1) * 128, :], xnw[:, :])], load=False)
            for half in range(2):
                sl = slice(half * 512, (half + 1) * 512)
                k.tt("pool", xn2[:, sl], xnw[:, sl], xnw[:, sl], ALU.mult, [xnw], [xn2])
                yield
                k.op("dve", lambda e, sl=sl, half=half: e.tensor_reduce(out=ss2[:, half:half + 1], in_=xn2[:, sl], op=ALU.add, axis=AX.X),
                     [xn2], [ss2])
                yield
            k.tt("dve", ss[:, :], ss2[:, 0:1], ss2[:, 1:2], ALU.add, [ss2], [ss])
            yield
            k.act(rs[:, :], ss[:, :], AF.Ln, [ss, g.epsc], [rs], scale=1.0 / D, bias=g.epsc[:, 0:1])
            k.act(rs[:, :], rs[:, :], AF.Exp, [rs], [rs], scale=-0.5)
            yield
            k.act(xn2[:, :], xnw[:, :], AF.Identity, [xnw, rs], [xn2], scale=rs[:, 0:1])
            yield
            hT_ = h2T[qi % 2]
            for r in range(2):
                pv = pm[r][:, :].rearrange("p (h q) -> p h q", q=128)
                for i in range(4):
                    kc = 4 * r + i
                    k.tr(pv[:, i, :], xn2[:, kc * 128:(kc + 1) * 128], g.identf[:, :], [xn2, g.identf], [pm[r]])
                yield
                for i in range(4):
                    kc = 4 * r + i
                    k.ts("dve", hT_[:, kc, :], pv[:, i, :], g.mods[:, 4, kc:kc + 1], g.mods[:, 5, kc:kc + 1], ALU.mult, ALU.add,
                         [pm[r], g.mods], [hT_])
                yield
            k.dma("pool", hT_, g.H2d, [(H2v[:, :, qi * 128:(qi + 1) * 128], hT_[:, :, :])], load=False)
            if next_load is not None:
                loadM(next_load)

        loadQ(0)
        loadM(0)
        loadM(1)
        for qi in range(NQT):
            if qi + 1 < NQT:
                loadQ(qi + 1)
            attn_group(qi, 0)
            attn_group(qi, 1)
            drain()
        flush_pv()
        drain()
        gens.append(merge_gen(NQT - 1))
        drain()
        k.end_phase()
    k.stack = prev


def phase_D(k, g):
    prev = k.stack
    with ExitStack() as ps_:
        k.stack = ps_
        wup = k.sbuf([128, 8, 2 * DFF], BF16, "wup")
        wdn = k.sbuf([128, 22, 1024], BF16, "wdn")
        upv = g.d_w_up.t.rearrange("(kc p) n -> p kc n", p=128)
        for q4 in range(4):
            lo, hi = q4 * 704, (q4 + 1) * 704
            k.dma("pool", wup, g.d_w_up, [(wup[:, :, lo:hi], upv[:, :, lo:hi]),
                                          (wup[:, :, DFF + lo:DFF + hi], upv[:, :, DFF + lo:DFF + hi])])
        k.dma("pool", wdn, g.d_w_down, [(wdn[:, :, :], g.d_w_down.t.rearrange("(j p) n -> p j n", p=128))])
        bg2 = k.sbuf([128, 1024], F32, "bg2")
        fnw = k.sbuf([128, 1024], F32, "fnw")
        k.dma("sp", bg2, g.d_b_down, [(bg2[:, :], g.d_b_down[0:1024].partition_broadcast(128))])
        k.dma("sp", fnw, g.d_final_norm_w, [(fnw[:, :], g.d_final_norm_w[0:1024].partition_broadcast(128))])
        k.tt("dve", bg2[:, :], bg2[:, :], g.g2row[:, :], ALU.mult, [bg2, g.g2row], [bg2])
        h2b = [k.sbuf([128, 8, 514], BF16, f"h2b{i}") for i in range(1)]
        uas = [k.sbuf([128, 514], F32, f"ua{i}") for i in range(2)]
        ugs = [k.sbuf([128, 514], F32, f"ug{i}") for i in range(2)]
        cas = [k.sbuf([128, 512], F32, f"ca{i}") for i in range(2)]
        cgs = [k.sbuf([128, 512], F32, f"cg{i}") for i in range(2)]
        actT = k.sbuf([128, 22, 512], BF16, "actT")
        xns = [k.sbuf([128, 1024], F32, f"xnd{i}") for i in range(1)]
        ys = [k.sbuf([128, 1024], F32, f"y{i}") for i in range(1)]
        t1 = k.sbuf([128, 512], F32, "t1d")
        ss = k.sbuf([128, 1], F32, "ssd")
        rs = k.sbuf([128, 1], F32, "rsd")
        pa = [k.psum([128, 512], F32, f"pa{i}") for i in range(2)]
        pg_ = [k.psum([128, 512], F32, f"pgd{i}") for i in range(2)]
        phs = [k.psum([128, 512], F32, f"ph{i}") for i in range(2)]
        cen = k.sbuf([128, 44], F32, "cen")
        k.tt("dve", cen[:, :], g.vec[:, vcol("fcw1"):vcol("fcw1") + 44], g.vec[:, vcol("b_up"):vcol("b_up") + 44], ALU.mult, [g.vec], [cen])
        k.tt("dve", cen[:, :], cen[:, :], g.vec[:, vcol("fcb"):vcol("fcb") + 44], ALU.add, [cen, g.vec], [cen])
        pd = [k.psum([128, 512], F32, f"pd{i}") for i in range(2)]
        H2v = g.H2d.t.rearrange("kc p t -> p kc t")
        NB = 8
        tcount = [0]
        for bi in range(NB):
            T0 = bi * 512
            hb = h2b[0]
            if bi == 0:
                k.dma("sp", hb, g.H2d, [(hb[:, :, 1:514], H2v[:, :, 0:513])])
                k.op("pool", lambda e, hb=hb: e.memset(hb[:, :, 0:1], 0.0), [], [hb])
            else:
                k.dma("sp", hb, g.H2d, [(hb[:, :, :], H2v[:, :, T0 - 1:T0 + 513])])
            for j in range(22):
                p_a, p_g = pa[j % 2], pg_[j % 2]
                ua, ug = uas[j % 2], ugs[j % 2]
                ca, cg = cas[j % 2], cgs[j % 2]
                for kc in range(8):
                    k.mm(p_a[:, :], wup[:, kc, j * 128:(j + 1) * 128], hb[:, kc, 1:513], kc == 0, kc == 7, [wup, hb], [p_a])
                for kc in range(8):
                    k.mm(p_g[:, :], wup[:, kc, DFF + j * 128:DFF + (j + 1) * 128], hb[:, kc, 1:513], kc == 0, kc == 7, [wup, hb], [p_g])
                ph = phs[j % 2]
                po = (j % 2) * 8
                for kc in range(8):
                    k.mm(ph[:, po:po + 2], wup[:, kc, j * 128:(j + 1) * 128], hb[:, kc, 0:514:513], kc == 0, kc == 7, [wup, hb], [ph])
                for kc in range(8):
                    k.mm(ph[:, po + 2:po + 4], wup[:, kc, DFF + j * 128:DFF + (j + 1) * 128], hb[:, kc, 0:514:513], kc == 0, kc == 7, [wup, hb], [ph])
                ba = g.vec[:, vcol("b_up", j):vcol("b_up", j) + 1]
                bg_ = g.vec[:, vcol("b_up", 22 + j):vcol("b_up", 22 + j) + 1]
                k.act(ua[:, 1:513], p_a[:, :], AF.Identity, [p_a, g.vec], [ua], bias=ba)
                k.ts("dve", ua[:, 0:514:513], ph[:, po:po + 2], ba, None, ALU.add, ALU.bypass, [ph, g.vec], [ua])
                k.act(ug[:, 1:513], p_g[:, :], AF.Identity, [p_g, g.vec], [ug], bias=bg_)
                k.ts("dve", ug[:, 0:514:513], ph[:, po + 2:po + 4], bg_, None, ALU.add, ALU.bypass, [ph, g.vec], [ug])
                if bi == 0:
                    k.op("dve", lambda e, ua=ua: e.memset(ua[:, 0:1], 0.0), [], [ua])
                    k.op("dve", lambda e, ug=ug: e.memset(ug[:, 0:1], 0.0), [], [ug])
                for (u_, c_, jj, pp_) in ((ua, ca, j, p_a), (ug, cg, 22 + j, p_g)):
                    w0 = g.vec[:, vcol("fcw0", jj):vcol("fcw0", jj) + 1]
                    w1 = g.vec[:, vcol("fcw1", jj):vcol("fcw1", jj) + 1]
                    w2 = g.vec[:, vcol("fcw2", jj):vcol("fcw2", jj) + 1]
                    k.act(c_[:, :], pp_[:, :], AF.Identity, [pp_, g.vec, cen], [c_], scale=w1, bias=cen[:, jj:jj + 1])
                    k.stt(c_[:, :], u_[:, 0:512], w0, c_[:, :], ALU.mult, ALU.add, [u_, g.vec, c_], [c_])
                    k.stt(c_[:, :], u_[:, 2:514], w2, c_[:, :], ALU.mult, ALU.add, [u_, g.vec, c_], [c_])
                k.act(ca[:, :], ca[:, :], AF.Gelu_apprx_tanh, [ca], [ca])
                k.tt("dve", actT[:, j, :], ca[:, :], cg[:, :], ALU.mult, [ca, cg], [actT])
            for tt_ in range(4):
                qi = bi * 4 + tt_
                xn_ = xns[0]
                y = ys[0]
                tcount[0] += 1
                k.dma("sp", xn_, g.XNd, [(xn_[:, :], g.XNd.t[qi * 128:(qi + 1) * 128, :])])
                k.tt("pool", xn_[:, :], xn_[:, :], bg2[:, :], ALU.add, [xn_, bg2], [xn_])
                for half in range(2):
                    for j in range(22):
                        k.mm(pd[half][:, :], actT[:, j, tt_ * 128:(tt_ + 1) * 128], wdn[:, j, half * 512:(half + 1) * 512],
                             j == 0, j == 21, [actT, wdn], [pd[half]])
                for half in range(2):
                    sl = slice(half * 512, (half + 1) * 512)
                    k.tt("dve", t1[:, :], pd[half][:, :], g.g2row[:, sl], ALU.mult, [pd[half], g.g2row], [t1])
                    k.tt("dve", y[:, sl], t1[:, :], xn_[:, sl], ALU.add, [t1, xn_], [y])
                k.act(t1[:, :].bitcast(BF16), y[:, :], AF.Square, [y], [t1, ss], accum_out=ss[:, 0:1])
                k.act(rs[:, :], ss[:, :], AF.Sqrt, [ss, g.epsc], [rs], scale=1.0 / D, bias=g.epsc[:, 0:1])
                k.op("dve", lambda e: e.reciprocal(out=rs[:, :], in_=rs[:, :]), [rs], [rs])
                k.act(xn_[:, :], y[:, :], AF.Identity, [y, rs], [xn_], scale=rs[:, 0:1])
                k.tt("pool", y[:, :], xn_[:, :], fnw[:, :], ALU.mult, [xn_, fnw], [y])
                k.dma("sp", y, g.out, [(g.out.t[qi * 128:(qi + 1) * 128, :], y[:, :])], load=False)
        k.end_phase()
    k.stack = prev


INPUT_SPECS = [
    ("xl", [T, D], F32), ("ctx", [NCTX, D], F32), ("cvecT", [128, 8, 2], F32),
    ("w_mod", [D, 6 * D], F32), ("b_mod", [6 * D], F32), ("vec", [128, NV], F32),
    ("w_in", [D, 5120], F32), ("b_in", [5120], F32), ("qkw", [1280], F32),
    ("rope", [T, 2, 64], F32), ("ident", [128, 128], BF16),
    ("zT", [33, T], F32), ("filt_w1", [33, 64], F32), ("filt_wi", [64, 2, 64], F32), ("filt_vec", [64, 8], F32),
    ("filt_w_out", [64, 2048], F32), ("skipT", [64, 2, 8], F32), ("decay", [64, 128, 512], F32),
    ("C1", [64, 256], BF16), ("F2T", [128, 128, 192], BF16), ("F2D", [128, 128, 384], BF16), ("R12", [128, 2, 256], BF16), ("I2T", [128, 128, 128], BF16),
    ("identf", [128, 128], F32), ("w_attn_out", [D, D], F32), ("w_hy_out", [512, D], F32), ("w_o", [D, D], F32),
    ("b_o", [D], F32), ("w_up", [D, 2 * DFF], F32), ("w_down", [DFF, D], F32), ("b_down", [D], F32),
    ("final_norm_w", [D], F32),
]


def build(debug=None):
    nc = bass.Bass("TRN2", target_bir_lowering=False)
    with ExitStack() as st:
        k = KB(nc, st)
        g = G()
        for name, shape, dt in INPUT_SPECS:
            setattr(g, "d_" + name, k.dram(name, shape, dt, kind="ExternalInput"))
        dk = "ExternalOutput" if debug == "A" else None
        dkb = "ExternalOutput" if debug == "B" else None
        g.HCd = k.dram("HCd", [1536, T], BF16, kind=dkb)
        g.KRd = k.dram("KRd", [2, 8, 128, 128, 64], BF16, kind=dkb)
        g.KId = k.dram("KId", [2, 8, 128, 128, 64], BF16, kind=dkb)
        g.HYd = k.dram("HYd", [512, NQ], BF16, kind=dkb)
        dkc = "ExternalOutput" if debug == "C" else None
        g.XNd = k.dram("XNd", [NQ, D], F32, kind=dkc)
        g.H2d = k.dram("H2d", [8, 128, NQ], BF16, kind=dkc)
        g.out = k.dram("out", [4096, D], F32, kind="ExternalOutput")
        g.identf = k.sbuf([128, 128], F32, "identf")
        k.dma("sp", g.identf, g.d_identf, [(g.identf[:, :], g.d_identf[:, :])])
        g.Kd = k.dram("Kd", [128, 2, NKEY], BF16, kind=dk)
        g.Vd = k.dram("Vd", [NKT, 128, 258], BF16, kind=dk)
        g.Qd = k.dram("Qd", [NQT, 128, 1024], BF16, kind=dk)
        g.HSd = k.dram("HSd", [1536, T], BF16, kind=dk)
        g.SGd = k.dram("SGd", [2048, NQ], BF16, kind=dk)
        g.epsc = k.sbuf([128, 1], F32, "epsc")
        k.op("dve", lambda e: e.memset(g.epsc[:, :], EPS), [], [g.epsc])
        phase_0(k, g)
        phase_A(k, g)
        phase_B(k, g)
        phase_C(k, g)
        phase_D(k, g)
        k.barrier()
        print("ninst", k.ninst, {e: c for e, c in k.cnt.items()})
    return nc


def _fm(v):
    v = np.asarray(v, np.float32)
    return np.ascontiguousarray(v.reshape(-1, 128).T)


def rope_tables():
    rows = T // 64
    row = np.repeat(np.arange(rows), 64).astype(np.float32)
    col = np.tile(np.arange(64), rows).astype(np.float32)
    freqs = (10000.0 ** (-np.arange(0, 64, 2, dtype=np.float32) / 64)).astype(np.float32)
    ang = np.concatenate([row[:, None] * freqs, col[:, None] * freqs], axis=-1).astype(np.float32)
    return np.stack([np.cos(ang), np.sin(ang)], axis=1).astype(np.float32)


_CONST = {}


def const_tables():
    if _CONST:
        return _CONST
    bf = ml_dtypes.bfloat16
    N = 2 * T
    n1 = np.arange(64, dtype=np.float64)
    n2 = np.arange(128, dtype=np.float64)
    k1 = np.arange(128, dtype=np.float64)
    k2 = np.arange(64, dtype=np.float64)
    phi = 2 * np.pi * np.outer(n1, k1 + 0.5) / 128
    _CONST["C1"] = np.concatenate([np.cos(phi), -np.sin(phi)], 1).astype(bf)
    psi = 2 * np.pi * (n2[:, None, None] * (k1[None, :, None] + 0.5) / N + n2[:, None, None] * k2[None, None, :] / 128)
    Hr, Hi = np.cos(psi), -np.sin(psi)
    _CONST["F2T"] = np.ascontiguousarray(np.concatenate([-Hi, Hr, Hi], 2).astype(bf))
    _CONST["F2D"] = np.ascontiguousarray(np.concatenate([-Hi, -Hi, Hr, Hr, Hi, Hi], 2).astype(bf))
    th = 2 * np.pi * np.outer(k2, n2) / 128
    Cr, Ci = np.cos(th), np.sin(th)
    S1 = np.concatenate([Cr, -Ci], 0); S2 = np.concatenate([-Ci, -Cr], 0); S3 = np.concatenate([Ci, Cr], 0)
    _CONST["R12"] = np.ascontiguousarray(np.stack([np.concatenate([S1, S3], 1), np.concatenate([S2, S1], 1)], 1).astype(bf))
    gg = 2 * np.pi * (n2[None, :, None] * (k1[:, None, None] + 0.5) / N + n1[None, None, :] * (k1[:, None, None] + 0.5) / 128)
    _CONST["I2T"] = np.ascontiguousarray(np.concatenate([np.cos(gg) * 2 / N, -np.sin(gg) * 2 / N], 2).astype(bf))
    L = T
    t_idx = np.arange(L, dtype=np.float32)
    tt = t_idx / np.float32(L - 1)
    f = np.linspace(1e-4, 15, 16, dtype=np.float32)
    wpos = (np.float32(2.0 * math.pi) * t_idx / np.float32(L)).astype(np.float32)
    z = np.concatenate([tt[:, None], np.cos(wpos[:, None] * f), -np.sin(wpos[:, None] * f)], -1).astype(np.float32)
    order = (np.arange(64)[None, :] * 128 + np.arange(128)[:, None]).reshape(-1)
    _CONST["zT"] = np.ascontiguousarray(z[order].T)
    min_decay = math.log(1e-2) / 1.5
    max_decay = math.log(1e-2) / 0.3
    deltas = np.linspace(min_decay, max_decay, 512, dtype=np.float32)
    dec = np.exp(-tt[:, None] * np.abs(deltas)).astype(np.float32)
    _CONST["decay"] = np.ascontiguousarray(dec.reshape(64, 128, 512))
    return _CONST


def host_inputs(inp, core):
    b, hf = core // 2, core % 2
    rev = hf == 1
    f32 = lambda a: np.ascontiguousarray(np.asarray(a, np.float32))
    x = np.asarray(inp["x"][b], np.float32)
    rope = rope_tables()
    hcw = np.asarray(inp["hy_conv_w"][0], np.float32)
    fcw = np.asarray(inp["ffn_conv_w"][0], np.float32)
    if rev:
        x = x[::-1]
        rope = rope[::-1]
        hcw = hcw[::-1]
        fcw = fcw[::-1]
    cvec = np.stack([np.asarray(inp["c"][b], np.float32), np.asarray(inp["c_ctx"], np.float32)], 0)
    cvecT = np.ascontiguousarray(cvec.reshape(2, 8, 128).transpose(2, 1, 0))
    vec = np.zeros((128, NV), np.float32)

    def put(name, v):
        o, w = VEC_COLS[name]
        vec[:, o:o + w] = _fm(v)
    put("norm1_w", inp["norm1_w"][0]); put("norm2_w", inp["norm2_w"][0]); put("b_in", inp["b_in"][0])
    put("hcw0", hcw[0]); put("hcw1", hcw[1]); put("hcw2", hcw[2]); put("hcb", inp["hy_conv_b"][0])
    put("b_up", inp["b_up"][0]); put("fcw0", fcw[0]); put("fcw1", fcw[1]); put("fcw2", fcw[2])
    put("fcb", inp["ffn_conv_b"][0]); put("b_mod", inp["b_mod"][0])
    qkw = np.concatenate([np.tile(np.asarray(inp["q_norm_w"][0], np.float32), 8),
                          np.tile(np.asarray(inp["k_norm_w"][0], np.float32), 2)])
    m = {
        "xl": f32(x), "ctx": f32(inp["ctx"][b]), "cvecT": f32(cvecT),
        "w_mod": f32(inp["w_mod"][0]), "b_mod": f32(inp["b_mod"][0]), "vec": vec,
        "w_in": f32(inp["w_in"][0]), "b_in": f32(inp["b_in"][0]), "qkw": f32(qkw),
        "rope": f32(rope), "ident": np.eye(128, dtype=np.float32).astype(ml_dtypes.bfloat16),
    }
    m["identf"] = np.eye(128, dtype=np.float32)
    for nm_ in ("w_attn_out", "w_hy_out", "w_o", "b_o", "w_up", "w_down", "b_down"):
        m[nm_] = f32(inp[nm_][0])
    m["final_norm_w"] = f32(inp["final_norm_w"])
    ct = const_tables()
    for nm in ("zT", "decay", "C1", "F2T", "F2D", "R12", "I2T"):
        m[nm] = ct[nm]
    wout = np.asarray(inp["filt_w_out"][0], np.float32)
    if rev:
        wout = np.concatenate([wout[:, 1024:], wout[:, :1024]], 1)
    fvec = np.zeros((64, 8), np.float32)
    fvec[:, 0] = inp["filt_freq"][0]; fvec[:, 1] = inp["filt_b1"][0]
    fvec[:, 2] = inp["filt_b_inner"][0][0]; fvec[:, 3] = inp["filt_b_inner"][0][1]
    m["filt_w1"] = f32(inp["filt_w1"][0])
    m["filt_wi"] = f32(np.asarray(inp["filt_w_inner"][0], np.float32).transpose(1, 0, 2))
    m["filt_vec"] = fvec
    m["filt_w_out"] = f32(wout)
    m["skipT"] = f32(np.asarray(inp["hy_skip"][0], np.float32).reshape(2, 8, 64).transpose(2, 0, 1))
    return m


def kernel(**inputs):
    nc = build()
    in_maps = [host_inputs(inputs, c) for c in range(8)]
    res = run_bass_kernel_spmd(nc, in_maps, core_ids=list(range(8)))
    out = np.zeros((4, T, D), np.float32)
    for c in range(8):
        b, hf = c // 2, c % 2
        y = np.asarray(res.results[c]["out"], np.float32)
        if hf == 0:
            out[b, :4096] = y
        else:
            out[b, 4096:] = y[::-1]
    return out
```

```python
import math
from contextlib import ExitStack
import numpy as np
import ml_dtypes
import concourse.bass as bass
import concourse.mybir as mybir
from concourse.bass_utils import run_bass_kernel_spmd

F32 = mybir.dt.float32
BF16 = mybir.dt.bfloat16
AF = mybir.ActivationFunctionType
ALU = mybir.AluOpType
AX = mybir.AxisListType

D = 1024
T = 8192
NCTX = 256
NKEY = T + NCTX
NKT = NKEY // 128
NQT = 33
NQ = NQT * 128
DFF = 2816
EPS = 1e-6


class Buf:
    def __init__(self, t, name):
        self.t = t
        self.name = name
        self.writer = {}
        self.readers = {}
        self.dsem = None
        self.dcount = 0

    def __getitem__(self, idx):
        return self.t[idx]


def _merge(d, toks):
    for k_, v in toks.items():
        if d.get(k_, 0) < v:
            d[k_] = v


class KB:
    def __init__(self, nc, stack):
        self.nc = nc
        self.gstack = stack
        self.stack = stack
        self.eng = {"pe": nc.tensor, "act": nc.scalar, "dve": nc.vector, "pool": nc.gpsimd, "sp": nc.sync}
        self.sems = {}
        self.cnt = {}
        for e in ("pe", "act", "dve", "pool"):
            self.sems[e] = stack.enter_context(nc.semaphore("s_" + e))
            self.cnt[e] = 0
        self.waited = {e: {} for e in self.eng}
        self.nbuf = 0
        self.ninst = 0
        self.dsem_live = {}

    def sbuf(self, shape, dtype, name):
        self.nbuf += 1
        name = f"{name}_{self.nbuf}"
        t = self.stack.enter_context(self.nc.sbuf_tensor(name, list(shape), dtype))
        return Buf(t, name)

    def view(self, buf, name):
        self.nbuf += 1
        return Buf(buf.t, f"{name}_{self.nbuf}")

    def psum(self, shape, dtype, name):
        self.nbuf += 1
        name = f"{name}_{self.nbuf}"
        t = self.stack.enter_context(self.nc.psum_tensor(name, list(shape), dtype))
        return Buf(t, name)

    def dram(self, name, shape, dtype, kind=None):
        if kind is None:
            t = self.nc.dram_tensor(name, list(shape), dtype)
        else:
            t = self.nc.dram_tensor(name, list(shape), dtype, kind=kind)
        return Buf(t.ap(), name)

    def _wait(self, e, deps):
        w = self.waited[e]
        for k_, v in deps.items():
            if w.get(k_, 0) >= v:
                continue
            self.eng[e].wait_ge(self.sems[k_], v)
            w[k_] = v
            self.ninst += 1

    def op(self, e, fn, reads=(), writes=()):
        deps = {}
        for b in reads:
            _merge(deps, b.writer)
        for b in writes:
            for src in (b.writer, b.readers):
                for k_, v in src.items():
                    if deps.get(k_, 0) < v:
                        deps[k_] = v
        if e == "pe":
            deps.pop("pe", None)
        self._wait(e, deps)
        inst = fn(self.eng[e])
        self.cnt[e] += 1
        inst.then_inc(self.sems[e], 1)
        self.ninst += 1
        tok = {e: self.cnt[e]}
        for b in reads:
            _merge(b.readers, tok)
        for b in writes:
            b.writer = dict(tok)
            b.readers = {}
        return tok

    def dma(self, q, sb, dr, parts, load=True):
        if sb.dsem is None:
            key = "d_" + sb.name
            self.sems[key] = self.stack.enter_context(self.nc.semaphore(key))
            sb.dsem = key
            self.dsem_live[key] = sb
        deps = {}
        if load:
            _merge(deps, dr.writer)
            _merge(deps, sb.writer)
            _merge(deps, sb.readers)
        else:
            _merge(deps, sb.writer)
            _merge(deps, dr.readers)
        if sb.dcount:
            _merge(deps, {sb.dsem: sb.dcount})
        self._wait(q, deps)
        for (o, i) in parts:
            self.eng[q].dma_start(out=o, in_=i).then_inc(self.sems[sb.dsem], 16)
            sb.dcount += 16
            self.ninst += 1
        tok = {sb.dsem: sb.dcount}
        if load:
            sb.writer = dict(tok)
            sb.readers = {}
            _merge(dr.readers, tok)
        else:
            _merge(sb.readers, tok)
            _merge(dr.writer, tok)
        return tok

    def barrier(self):
        allc = {e: c for e, c in self.cnt.items() if c}
        for key, sb in self.dsem_live.items():
            if sb.dcount:
                allc[key] = sb.dcount
        for e in self.eng:
            self._wait(e, dict(allc))

    def end_phase(self, drams=()):
        self.barrier()
        for d in drams:
            d.writer = {}
            d.readers = {}
        self.dsem_live = {}

    def mm(self, out, lhsT, rhs, start, stop, reads, writes):
        return self.op("pe", lambda e: e.matmul(out, lhsT, rhs, start=start, stop=stop), reads, writes)

    def tr(self, out, in_, ident_ap, reads, writes):
        return self.op("pe", lambda e: e.transpose(out, in_, ident_ap), reads, writes)

    def act(self, out, in_, func, reads, writes, **kw):
        return self.op("act", lambda e: e.activation(out=out, in_=in_, func=func, **kw), reads, writes)

    def tt(self, eng, out, in0, in1, op, reads, writes):
        return self.op(eng, lambda e: e.tensor_tensor(out=out, in0=in0, in1=in1, op=op), reads, writes)

    def ts(self, eng, out, in0, s1, s2, op0, op1, reads, writes):
        return self.op(eng, lambda e: e.tensor_scalar(out=out, in0=in0, scalar1=s1, scalar2=s2, op0=op0, op1=op1),
                       reads, writes)

    def stt(self, out, in0, scalar, in1, op0, op1, reads, writes):
        return self.op("dve", lambda e: e.scalar_tensor_tensor(out=out, in0=in0, scalar=scalar, in1=in1,
                                                                op0=op0, op1=op1), reads, writes)

    def copy(self, eng, out, in_, reads, writes):
        if eng == "act":
            return self.act(out, in_, AF.Identity, reads, writes)
        return self.op(eng, lambda e: e.tensor_copy(out=out, in_=in_), reads, writes)


class G:
    pass


VEC_COLS = {}
_off = 0
for _n, _w in [("norm1_w", 8), ("norm2_w", 8), ("b_in", 40), ("hcw0", 12), ("hcw1", 12), ("hcw2", 12),
               ("hcb", 12), ("b_up", 44), ("fcw0", 44), ("fcw1", 44), ("fcw2", 44), ("fcb", 44), ("b_mod", 48)]:
    VEC_COLS[_n] = (_off, _w)
    _off += _w
NV = _off


def vcol(name, j=0):
    return VEC_COLS[name][0] + j


def phase_0(k, g):
    nc = k.nc
    k.stack = k.gstack
    g.ident = k.sbuf([128, 128], BF16, "ident")
    g.vec = k.sbuf([128, NV], F32, "vec")
    g.mods = k.sbuf([128, 6, 8], F32, "mods")
    g.g1row = k.sbuf([128, 1024], F32, "g1row")
    g.g2row = k.sbuf([128, 1024], F32, "g2row")
    k.dma("sp", g.ident, g.d_ident, [(g.ident[:, :], g.d_ident[:, :])])
    k.dma("sp", g.vec, g.d_vec, [(g.vec[:, :], g.d_vec[:, :])])
    with ExitStack() as ps:
        k.stack = ps
        cv = k.sbuf([128, 8, 2], F32, "cv")
        sc = k.sbuf([128, 8, 2], BF16, "sc")
        screp = k.sbuf([128, 8, 128], BF16, "screp")
        wm = [k.sbuf([128, 8, 1024], BF16, f"wm{i}") for i in range(2)]
        mT = k.sbuf([128, 48, 2], F32, "mT")
        brow = k.sbuf([128, 1024], F32, "brow")
        pm = [k.psum([128, 512], F32, f"pm{i}") for i in range(2)]
        pr = [k.psum([128, 512], F32, f"pr{i}") for i in range(2)]
        k.dma("sp", cv, g.d_cvecT, [(cv[:, :, :], g.d_cvecT[:, :, :])])
        k.act(sc[:, :, :], cv[:, :, :], AF.Silu, [cv], [sc])
        k.copy("dve", screp[:, :, :], sc[:, :, 0:1].to_broadcast([128, 8, 128]), [sc], [screp])
        wv = g.d_w_mod.t.rearrange("(kc p) n -> p kc n", p=128)
        for piece in range(6):
            w = wm[piece % 2]
            k.dma("pool", w, g.d_w_mod, [(w[:, :, :], wv[:, :, piece * 1024:(piece + 1) * 1024])])
            if piece in (0, 1, 3, 4):
                p_ = pm[piece % 2]
                for j in range(8):
                    for kc in range(8):
                        k.mm(p_[:, j * 2:j * 2 + 2], w[:, kc, j * 128:(j + 1) * 128], sc[:, kc, :],
                             kc == 0, kc == 7, [w, sc], [p_])
                bm = g.vec[:, vcol("b_mod", piece * 8):vcol("b_mod", piece * 8) + 8]
                k.tt("dve", mT[:, piece * 8:(piece + 1) * 8, :],
                     p_[:, 0:16].rearrange("p (j two) -> p j two", two=2),
                     bm.unsqueeze(2).to_broadcast([128, 8, 2]), ALU.add, [p_, g.vec], [mT])
            else:
                dst = g.g1row if piece == 2 else g.g2row
                k.dma("sp", brow, g.d_b_mod, [(brow[:, :], g.d_b_mod[piece * 1024:(piece + 1) * 1024].partition_broadcast(128))])
                for half in range(2):
                    p_ = pr[half]
                    for kc in range(8):
                        k.mm(p_[:, :], screp[:, kc, :], w[:, kc, half * 512:(half + 1) * 512],
                             kc == 0, kc == 7, [w, screp], [p_])
                    k.tt("dve", dst[:, half * 512:(half + 1) * 512], p_[:, :], brow[:, half * 512:(half + 1) * 512],
                         ALU.add, [p_, brow], [dst])
        n1 = g.vec[:, vcol("norm1_w"):vcol("norm1_w") + 8]
        n2 = g.vec[:, vcol("norm2_w"):vcol("norm2_w") + 8]
        k.stt(g.mods[:, 0, :], mT[:, 8:16, 0], 1.0, n1, ALU.add, ALU.mult, [mT, g.vec], [g.mods])
        k.copy("dve", g.mods[:, 1, :], mT[:, 0:8, 0], [mT], [g.mods])
        k.stt(g.mods[:, 2, :], mT[:, 8:16, 1], 1.0, n1, ALU.add, ALU.mult, [mT, g.vec], [g.mods])
        k.copy("dve", g.mods[:, 3, :], mT[:, 0:8, 1], [mT], [g.mods])
        k.stt(g.mods[:, 4, :], mT[:, 32:40, 0], 1.0, n2, ALU.add, ALU.mult, [mT, g.vec], [g.mods])
        k.copy("dve", g.mods[:, 5, :], mT[:, 24:32, 0], [mT], [g.mods])
        k.end_phase()
    k.stack = k.gstack


def norm_tile(k, g, xt, xn, ss, rs, junk):
    k.act(junk[:, :], xt[:, :], AF.Square, [xt], [junk, ss], accum_out=ss[:, 0:1])
    k.act(rs[:, :], ss[:, :], AF.Sqrt, [ss], [rs], scale=1.0 / D, bias=g.epsc[:, 0:1])
    k.op("dve", lambda e: e.reciprocal(out=rs[:, :], in_=rs[:, :]), [rs], [rs])
    k.act(xn[:, :], xt[:, :], AF.Identity, [xt, rs], [xn], scale=rs[:, 0:1])


def phase_A(k, g):
    with ExitStack() as ps:
        k.stack = ps
        win = k.sbuf([128, 8, 5120], BF16, "win")
        wv = g.d_w_in.t.rearrange("(kc p) n -> p kc n", p=128)
        for lo, hi in [(1024, 2048), (2048, 3072), (0, 1024), (3072, 4096), (4096, 5120)]:
            k.dma("pool", win, g.d_w_in, [(win[:, :, lo:hi], wv[:, :, lo:hi])])
        brow = k.sbuf([128, 1536], F32, "brow")
        wrow = k.sbuf([128, 1280], F32, "wrow")
        k.dma("sp", brow, g.d_b_in, [(brow[:, :], g.d_b_in[0:1536].partition_broadcast(128))])
        k.dma("sp", wrow, g.d_qkw, [(wrow[:, :], g.d_qkw[0:1280].partition_broadcast(128))])
        NX = 3
        xts = [k.sbuf([128, 1024], F32, f"xt{i}") for i in range(NX)]
        xns = [k.sbuf([128, 1024], BF16, f"xn{i}") for i in range(2)]
        junk = k.sbuf([128, 1024], BF16, "junk")
        sss = [k.sbuf([128, 1], F32, f"ss{i}") for i in range(2)]
        rss = [k.sbuf([128, 1], F32, f"rs{i}") for i in range(2)]
        hTbig = k.sbuf([128, 2, 8, 512], BF16, "hT")
        hTt = [[k.view(hTbig, f"hTt{s}_{j}") for j in range(4)] for s in range(2)]
        ropes = [k.sbuf([128, 2, 64], F32, f"rope{i}") for i in range(5)]
        qks = [k.sbuf([128, 1280], F32, f"qk{i}") for i in range(2)]
        sq = k.sbuf([128, 1280], F32, "sq")
        ss10 = k.sbuf([128, 10], F32, "ss10")
        rs10 = k.sbuf([128, 10], F32, "rs10")
        tmpa = k.sbuf([128, 10, 64], F32, "tmpa")
        tmpb = k.sbuf([128, 10, 64], F32, "tmpb")
        qrs = [k.sbuf([128, 1280], BF16, f"qr{i}") for i in range(2)]
        vts = [k.sbuf([128, 2, 129], BF16, f"vt{i}") for i in range(2)]
        qTs = [k.sbuf([128, 8, 128], BF16, f"qT{i}") for i in range(2)]
        kTs = [k.sbuf([128, 2, 128], BF16, f"kT{i}") for i in range(2)]
        fms = [k.sbuf([128, 512], BF16, f"fm{i}") for i in range(4)]
        ptr = k.psum([128, 8, 128], BF16, "ptr")
        pq = [k.psum([128, 512], F32, f"pq{i}") for i in range(2)]
        pkv = k.psum([128, 512], F32, "pkv")
        pfm = [k.psum([128, 512], F32, f"pfm{i}") for i in range(2)]
        pqT = k.psum([128, 8, 128], BF16, "pqT")
        pkT = k.psum([128, 2, 128], BF16, "pkT")
        for v in vts:
            k.op("dve", lambda e, v=v: e.memset(v[:, :, 128:129], 1.0), [], [v])

        NT = NKT
        fm_count = [0]

        def is_q(ti):
            return 2 <= ti < 2 + NQT

        def stage_load(ti):
            xt = xts[ti % NX]
            if ti < 2:
                src, dbuf = g.d_ctx.t[ti * 128:(ti + 1) * 128, :], g.d_ctx
            else:
                src, dbuf = g.d_xl.t[(ti - 2) * 128:(ti - 1) * 128, :], g.d_xl
            k.dma("sp", xt, dbuf, [(xt[:, :], src)])
            if ti >= 2:
                rp = ropes[ti % 5]
                k.dma("sp", rp, g.d_rope, [(rp[:, :, :], g.d_rope.t[(ti - 2) * 128:(ti - 1) * 128, :, :])])

        def slot(ti):
            if ti < 2:
                return 1, ti
            t = ti - 2
            return (t // 4) % 2, t % 4

        def stage_norm(ti):
            xt = xts[ti % NX]
            xn = xns[ti % 2]
            norm_tile(k, g, xt, xn, sss[ti % 2], rss[ti % 2], junk)

        def stage_front(ti):
            xn = xns[ti % 2]
            for kc in range(8):
                k.tr(ptr[:, kc, :], xn[:, kc * 128:(kc + 1) * 128], g.ident[:, :], [xn, g.ident], [ptr])
            s, j = slot(ti)
            hb = hTt[s][j]
            mi = 2 if ti < 2 else 0
            for kc in range(8):
                k.act(hTbig[:, s, kc, j * 128:(j + 1) * 128], ptr[:, kc, :], AF.Identity, [ptr, g.mods], [hb],
                      scale=g.mods[:, mi, kc:kc + 1], bias=g.mods[:, mi + 1, kc:kc + 1])

        def stage_front2(ti):
            s, j = slot(ti)
            hb = hTt[s][j]
            if is_q(ti):
                for half in range(2):
                    for kc in range(8):
                        k.mm(pq[half][:, :], hTbig[:, s, kc, j * 128:(j + 1) * 128],
                             win[:, kc, half * 512:(half + 1) * 512], kc == 0, kc == 7, [hb, win], [pq[half]])
            for kc in range(8):
                k.mm(pkv[:, :], hTbig[:, s, kc, j * 128:(j + 1) * 128], win[:, kc, 1024:1536],
                     kc == 0, kc == 7, [hb, win], [pkv])
            qk = qks[ti % 2]
            vt = vts[ti % 2]
            if is_q(ti):
                for half in range(2):
                    k.tt("dve", qk[:, half * 512:(half + 1) * 512], pq[half][:, :], brow[:, half * 512:(half + 1) * 512],
                         ALU.add, [pq[half], brow], [qk])
            k.tt("dve", qk[:, 1024:1280], pkv[:, 0:256], brow[:, 1024:1280], ALU.add, [pkv, brow], [qk])
            k.tt("dve", vt[:, :, 0:128], pkv[:, 256:512].rearrange("p (g d) -> p g d", d=128),
                 brow[:, 1280:1536].rearrange("p (g d) -> p g d", d=128), ALU.add, [pkv, brow], [vt])
            k.dma("pool", vt, g.Vd, [(g.Vd.t[ti, :, :], vt[:, :, :].rearrange("p g c -> p (g c)"))], load=False)

        def stage_back(ti):
            qk = qks[ti % 2]
            qr = qrs[ti % 2]
            lo = 0 if is_q(ti) else 1024
            nh = (1280 - lo) // 128
            h0 = lo // 128
            k.tt("dve", sq[:, lo:1280], qk[:, lo:1280], qk[:, lo:1280], ALU.mult, [qk], [sq])
            k.op("dve", lambda e: e.tensor_reduce(out=ss10[:, h0:10], in_=sq[:, lo:1280].rearrange("p (h d) -> p h d", d=128),
                                                  op=ALU.add, axis=AX.X), [sq], [ss10])
            k.act(rs10[:, h0:10], ss10[:, h0:10], AF.Sqrt, [ss10], [rs10], scale=1.0 / 128, bias=g.epsc[:, 0:1])
            k.op("dve", lambda e: e.reciprocal(out=rs10[:, h0:10], in_=rs10[:, h0:10]), [rs10], [rs10])
            qv = qk[:, lo:1280].rearrange("p (h d) -> p h d", d=128)
            k.tt("dve", qv, qv, rs10[:, h0:10].unsqueeze(2).to_broadcast([128, nh, 128]), ALU.mult, [qk, rs10], [qk])
            if ti < 2:
                k.tt("dve", qr[:, lo:1280], qk[:, lo:1280], wrow[:, lo:1280], ALU.mult, [qk, wrow], [qr])
            else:
                k.tt("dve", qk[:, lo:1280], qk[:, lo:1280], wrow[:, lo:1280], ALU.mult, [qk, wrow], [qk])
                rp = ropes[ti % 5]
                q4 = qk[:, lo:1280].rearrange("p (h i two) -> p h i two", i=64, two=2)
                o4 = qr[:, lo:1280].rearrange("p (h i two) -> p h i two", i=64, two=2)
                x1, x2 = q4[:, :, :, 0], q4[:, :, :, 1]
                cs = rp[:, 0, :].unsqueeze(1).to_broadcast([128, nh, 64])
                sn = rp[:, 1, :].unsqueeze(1).to_broadcast([128, nh, 64])
                ta, tb = tmpa[:, h0:10, :], tmpb[:, h0:10, :]
                k.tt("dve", ta, x1, cs, ALU.mult, [qk, rp], [tmpa])
                k.tt("dve", tb, x2, sn, ALU.mult, [qk, rp], [tmpb])
                k.tt("dve", o4[:, :, :, 0], ta, tb, ALU.subtract, [tmpa, tmpb], [qr])
                k.tt("dve", ta, x1, sn, ALU.mult, [qk, rp], [tmpa])
                k.tt("dve", tb, x2, cs, ALU.mult, [qk, rp], [tmpb])
                k.tt("dve", o4[:, :, :, 1], ta, tb, ALU.add, [tmpa, tmpb], [qr])

        def stage_back2(ti):
            qr = qrs[ti % 2]
            kT = kTs[ti % 2]
            for gi in range(2):
                k.tr(pkT[:, gi, :], qr[:, 1024 + gi * 128:1024 + (gi + 1) * 128], g.ident[:, :], [qr, g.ident], [pkT])
            k.copy("act", kT[:, :, :], pkT[:, :, :], [pkT], [kT])
            k.dma("pool", kT, g.Kd, [(g.Kd.t[:, :, ti * 128:(ti + 1) * 128], kT[:, :, :])], load=False)
            if is_q(ti):
                qT = qTs[ti % 2]
                for h in range(8):
                    k.tr(pqT[:, h, :], qr[:, h * 128:(h + 1) * 128], g.ident[:, :], [qr, g.ident], [pqT])
                k.copy("act", qT[:, :, :], pqT[:, :, :], [pqT], [qT])
                k.dma("pool", qT, g.Qd, [(g.Qd.t[ti - 2, :, :], qT[:, :, :].rearrange("p h q -> p (h q)"))], load=False)

        def stage_fm(st):
            s = st % 2
            hbs = hTt[s]
            nq = max(0, min(4, NQT - st * 4))
            for j in range(12):
                pf = pfm[fm_count[0] % 2]
                fb = fms[fm_count[0] % 4]
                fm_count[0] += 1
                for kc in range(8):
                    k.mm(pf[:, :], win[:, kc, 1536 + j * 128:1536 + (j + 1) * 128], hTbig[:, s, kc, :],
                         kc == 0, kc == 7, hbs + [win], [pf])
                bc = vcol("b_in", 12 + j)
                k.act(fb[:, :], pf[:, :], AF.Identity, [pf, g.vec], [fb], bias=g.vec[:, bc:bc + 1])
                k.dma("pool", fb, g.HSd, [(g.HSd.t[j * 128:(j + 1) * 128, st * 512:(st + 1) * 512], fb[:, :])], load=False)
                yield
            if nq:
                n = nq * 128
                for j in range(16):
                    pf = pfm[fm_count[0] % 2]
                    fb = fms[fm_count[0] % 4]
                    fm_count[0] += 1
                    for kc in range(8):
                        k.mm(pf[:, 0:n], win[:, kc, 3072 + j * 128:3072 + (j + 1) * 128], hTbig[:, s, kc, 0:n],
                             kc == 0, kc == 7, hbs[:nq] + [win], [pf])
                    bc = vcol("b_in", 24 + j)
                    k.act(fb[:, 0:n], pf[:, 0:n], AF.Sigmoid, [pf, g.vec], [fb], bias=g.vec[:, bc:bc + 1])
                    k.dma("pool", fb, g.SGd, [(g.SGd.t[j * 128:(j + 1) * 128, st * 512:st * 512 + n], fb[:, 0:n])], load=False)
                    yield

        fm_gens = []

        def pump_fm(n):
            while n > 0 and fm_gens:
                try:
                    next(fm_gens[0])
                    n -= 1
                except StopIteration:
                    fm_gens.pop(0)

        stage_load(0)
        for it in range(NT + 4):
            if it + 1 < NT:
                stage_load(it + 1)
            if it < NT:
                stage_norm(it)
            if 3 <= it <= NT + 2:
                stage_back(it - 3)
            if 1 <= it <= NT:
                stage_front(it - 1)
            if 2 <= it <= NT + 1:
                stage_front2(it - 2)
                t = it - 2 - 2
                if t >= 0 and t % 4 == 3:
                    fm_gens.append(stage_fm(t // 4))
            pump_fm(8)
            if 3 <= it <= NT + 2:
                stage_back2(it - 3)
        pump_fm(10 ** 6)
        k.end_phase()
    k.stack = k.gstack


WARM_N = 0
TWO_PI = 2.0 * math.pi
MAGIC = 12582912.0
PI_LO = 3.1415925


def phase_B0(k, g):
    prev = k.stack
    with ExitStack() as ps:
        k.stack = ps
        us = [k.sbuf([128, T + 2], BF16, f"u{i}") for i in range(2)]
        accs = [k.sbuf([128, T], F32, f"acc{i}") for i in range(2)]
        outs = [k.sbuf([128, T], BF16, f"o{i}") for i in range(2)]
        dgs = [k.sbuf([128, 3, 128], BF16, f"dg{i}") for i in range(2)]
        pcs = [k.psum([128, 512], F32, f"pc{i}") for i in range(4)]
        for u in us:
            k.op("dve", lambda e, u=u: e.memset(u[:, 0:1], 0.0), [], [u])
            k.op("dve", lambda e, u=u: e.memset(u[:, T + 1:T + 2], 0.0), [], [u])
        n = 0
        o_views = [(o_, k.view(o_, "oB")) for o_ in outs]

        def load_u(j):
            u = us[j % 2]
            k.dma("sp", u, g.HSd, [(u[:, 1:T + 1], g.HSd.t[j * 128:(j + 1) * 128, :])])

        load_u(0)
        for j in range(12):
            u = us[j % 2]
            o, oB = o_views[j % 2]
            acc = accs[j % 2]
            dg = dgs[j % 2]
            if j + 1 < 12:
                load_u(j + 1)
            for tap, nm in enumerate(("hcw0", "hcw1", "hcw2")):
                wc = g.vec[:, vcol(nm, j):vcol(nm, j) + 1]
                k.act(dg[:, tap, :], g.ident[:, :], AF.Identity, [g.ident, g.vec], [dg], scale=wc)
            cb = g.vec[:, vcol("hcb", j):vcol("hcb", j) + 1]
            for m in range(T // 512):
                p_ = pcs[n % 4]
                n += 1
                for tap in range(3):
                    k.mm(p_[:, :], dg[:, tap, :], u[:, m * 512 + tap:m * 512 + tap + 512], tap == 0, tap == 2, [dg, u], [p_])
                k.act(acc[:, m * 512:(m + 1) * 512], p_[:, :], AF.Identity, [p_, g.vec], [acc], bias=cb)
            ov = o[:, :].rearrange("p (n2 n1) -> p n2 n1", n1=64)
            av = acc[:, :].rearrange("p (n1 n2) -> p n2 n1", n2=128)
            k.copy("dve", ov[:, 0:96, :], av[:, 0:96, :], [acc], [o])
            k.copy("pool", ov[:, 96:128, :], av[:, 96:128, :], [acc], [oB])
            k.dma("pool", o, g.HCd, [(g.HCd.t[j * 128:(j + 1) * 128, 0:96 * 64], o[:, 0:96 * 64])], load=False)
            k.dma("pool", oB, g.HCd, [(g.HCd.t[j * 128:(j + 1) * 128, 96 * 64:T], o[:, 96 * 64:T])], load=False)
        k.end_phase()
    k.stack = prev


def phase_BM(k, g):
    prev = k.stack
    with ExitStack() as ps:
        k.stack = ps
        w1 = k.sbuf([33, 64], F32, "fw1")
        wi = k.sbuf([64, 2, 64], F32, "fwi")
        fv = k.sbuf([64, 8], F32, "fv")
        wo = k.sbuf([64, 2048], F32, "fwo")
        k.dma("sp", w1, g.d_filt_w1, [(w1[:, :], g.d_filt_w1[:, :])])
        k.dma("sp", wi, g.d_filt_wi, [(wi[:, :, :], g.d_filt_wi[:, :, :])])
        k.dma("sp", fv, g.d_filt_vec, [(fv[:, :], g.d_filt_vec[:, :])])
        k.dma("sp", wo, g.d_filt_w_out, [(wo[:, :], g.d_filt_w_out[:, :])])
        k.ts("dve", fv[:, 4:7], fv[:, 1:4], fv[:, 0:1], None, ALU.mult, ALU.bypass, [fv], [fv])
        for o in range(2):
            f_ = wo[:, o * 512:(o + 1) * 512].rearrange("p (cb c) -> p cb c", c=64)
            b_ = wo[:, 1024 + o * 512:1024 + (o + 1) * 512].rearrange("p (cb c) -> p cb c", c=64)
            k.tt("dve", g.WSD[:, o, :, 0, :], f_, b_, ALU.add, [wo], [g.WSD])
            k.tt("dve", g.WSD[:, o, :, 1, :], f_, b_, ALU.subtract, [wo], [g.WSD])
        zs = [k.sbuf([33, 512], F32, f"z{i}") for i in range(4)]
        pres = [k.sbuf([64, 512], F32, f"pre{i}") for i in range(2)]
        rrs = [k.sbuf([64, 512], F32, f"rr{i}") for i in range(2)]
        hss = [[k.sbuf([64, 512], F32, f"h{p}_{i}") for i in range(2)] for p in range(2)]
        pps = [[k.psum([64, 512], F32, f"pp{p}_{i}") for i in range(2)] for p in range(2)]
        for cp in range(8):
            for par in range(2):
                ch = 2 * cp + par
                z = zs[ch % 4]
                k.dma("sp", z, g.d_zT, [(z[:, :], g.d_zT[:, ch * 512:(ch + 1) * 512])])
            for layer in range(3):
                for par in range(2):
                    ch = 2 * cp + par
                    z = zs[ch % 4]
                    pre, rr, hs, pp = pres[par], rrs[par], hss[par], pps[par]
                    p_ = pp[layer % 2]
                    if layer == 0:
                        k.mm(p_[:, :], w1[:, :], z[:, :], True, True, [w1, z], [p_])
                    else:
                        hin = hs[(layer - 1) % 2]
                        k.mm(p_[:, :], wi[:, layer - 1, :], hin[:, :], True, True, [wi, hin], [p_])
                    k.ts("dve", pre[:, :], p_[:, :], fv[:, 0:1], fv[:, 4 + layer:5 + layer], ALU.mult, ALU.add, [p_, fv], [pre])
                    k.ts("dve", rr[:, :], pre[:, :], 1.0 / TWO_PI, MAGIC, ALU.mult, ALU.add, [pre], [rr])
                    k.ts("dve", rr[:, :], rr[:, :], -MAGIC, None, ALU.add, ALU.bypass, [rr], [rr])
                    k.stt(pre[:, :], rr[:, :], -TWO_PI, pre[:, :], ALU.mult, ALU.add, [rr, pre], [pre])
                    k.ts("dve", pre[:, :], pre[:, :], -PI_LO, PI_LO, ALU.max, ALU.min, [pre], [pre])
                    if layer < 2:
                        ho = hs[layer % 2]
                        k.act(ho[:, :], pre[:, :], AF.Sin, [pre], [ho])
                    else:
                        k.act(g.HF[:, ch * 512:(ch + 1) * 512], pre[:, :], AF.Sin, [pre], [g.HF])
        k.end_phase()
    k.stack = prev


def phase_B1(k, g):
    prev = k.stack
    with ExitStack() as ps:
        k.stack = ps
        F2D = k.sbuf([128, 128, 384], BF16, "F2D")
        C1 = k.sbuf([64, 256], BF16, "C1")
        k.dma("sp", C1, g.d_C1, [(C1[:, :], g.d_C1[:, :])])
        k.dma("sp", F2D, g.d_F2D, [(F2D[:, q4 * 16:(q4 + 1) * 16, :], g.d_F2D.t[:, q4 * 16:(q4 + 1) * 16, :]) for q4 in range(8)])
        hTf = k.sbuf([64, 2, 128, 64], BF16, "hTf")
        Af = k.sbuf([128, 64, 256], BF16, "Af")
        dcs = [k.sbuf([64, 8, 64], F32, f"dc{i}") for i in range(2)]
        stg = [k.sbuf([128, 8, 64], BF16, f"stg{i}") for i in range(4)]
        pf = [k.psum([64, 4, 128], F32, f"pf{i}") for i in range(2)]
        pa = [k.psum([128, 2, 256], F32, f"pa{i}") for i in range(2)]
        px = [k.psum([128, 8, 64], F32, f"px{i}") for i in range(2)]
        cnt = {"pf": 0, "pa": 0, "px": 0, "dc": 0, "stg": 0, "ev": 0}

        def evac(out, in_, reads, writes):
            e = "act" if cnt["ev"] % 2 == 0 else "dve"
            cnt["ev"] += 1
            k.copy(e, out, in_, reads, writes)

        for o in range(2):
            for cb in range(8):
                c0 = cb * 64
                for dg in range(16):
                    dc = dcs[cnt["dc"] % 2]
                    cnt["dc"] += 1
                    k.dma("sp", dc, g.d_decay, [(dc[:, :, :], g.d_decay.t[:, dg * 8:(dg + 1) * 8, c0:c0 + 64])])
                    for half in range(2):
                        p_ = pf[cnt["pf"] % 2]
                        cnt["pf"] += 1
                        for i in range(4):
                            n2 = dg * 8 + half * 4 + i
                            k.mm(p_[:, i, :], g.HF[:, n2 * 64:(n2 + 1) * 64],
                                 g.WSD[:, o, cb, :, :].rearrange("p s c -> p (s c)"), True, True, [g.HF, g.WSD], [p_])
                        n0 = dg * 8 + half * 4
                        k.tt("dve", hTf[:, :, n0:n0 + 4, :],
                             p_[:, :, :].rearrange("p n (s c) -> p s n c", s=2),
                             dc[:, half * 4:half * 4 + 4, :].unsqueeze(1).to_broadcast([64, 2, 4, 64]),
                             ALU.mult, [p_, dc], [hTf])
                for sd in range(2):
                    for cp in range(32):
                        p_ = pa[cnt["pa"] % 2]
                        cnt["pa"] += 1
                        for i in range(2):
                            k.mm(p_[:, i, :], hTf[:, sd, :, 2 * cp + i], C1[:, :], True, True, [hTf, C1], [p_])
                        evac(Af[:, 2 * cp:2 * cp + 2, :], p_[:, :, :], [p_], [Af])
                    dst = g.KRd if sd == 0 else g.KId
                    for kg in range(16):
                        p_ = px[cnt["px"] % 2]
                        cnt["px"] += 1
                        for i in range(8):
                            k1 = kg * 8 + i
                            ar = Af[:, :, k1]
                            ai = Af[:, :, 128 + k1]
                            if sd == 0:
                                la, lb = F2D[:, k1, 128:256], F2D[:, k1, 0:128]
                            else:
                                la, lb = F2D[:, k1, 256:384], F2D[:, k1, 128:256]
                            k.mm(p_[:, i, :], la, ar, True, False, [F2D, Af], [p_])
                            k.mm(p_[:, i, :], lb, ai, False, True, [F2D, Af], [p_])
                        sb_ = stg[cnt["stg"] % 4]
                        cnt["stg"] += 1
                        evac(sb_[:, :, :], p_[:, :, :], [p_], [sb_])
                        k.dma("pool", sb_, dst, [(dst.t[o, cb, :, kg * 8:(kg + 1) * 8, :], sb_[:, :, :])], load=False)
        k.end_phase()
    k.stack = prev


def phase_B2(k, g):
    prev = k.stack
    with ExitStack() as ps:
        k.stack = ps
        F2T = k.sbuf([128, 128, 192], BF16, "F2T")
        C1 = k.sbuf([64, 256], BF16, "C1")
        R12 = k.sbuf([128, 2, 256], BF16, "R12")
        skp = k.sbuf([64, 2, 8], F32, "skp")
        k.dma("sp", C1, g.d_C1, [(C1[:, :], g.d_C1[:, :])])
        k.dma("sp", R12, g.d_R12, [(R12[:, :, :], g.d_R12[:, :, :])])
        k.dma("sp", skp, g.d_skipT, [(skp[:, :, :], g.d_skipT[:, :, :])])
        for q4 in range(4):
            k.dma("sp", F2T, g.d_F2T, [(F2T[:, q4 * 32:(q4 + 1) * 32, :], g.d_F2T.t[:, q4 * 32:(q4 + 1) * 32, :])])
        R1s = [k.sbuf([64, T], BF16, f"R1_{i}") for i in range(2)]
        BIG1 = k.sbuf([128, 2 * T], BF16, "BIG1")
        BIG2 = k.sbuf([128, 2 * T], BF16, "BIG2")
        uT = BIG1[0:64, 0:T].rearrange("p (n c) -> p n c", c=64)
        P1 = BIG1[:, 0:T].rearrange("p (k c) -> p k c", c=64)
        P2 = BIG1[:, T:2 * T].rearrange("p (k c) -> p k c", c=64)
        A = BIG2[:, :].rearrange("p (c r) -> p c r", r=256)
        B = BIG2[:, :].rearrange("p (c r n) -> p c r n", r=2, n=128)
        HY = k.sbuf([64, NQT, 128], BF16, "HY")
        krs = [k.sbuf([128, 8, 64], BF16, f"kr{i}") for i in range(2)]
        kis = [k.sbuf([128, 8, 64], BF16, f"ki{i}") for i in range(2)]
        i2s = [k.sbuf([128, 8, 128], BF16, f"i2{i}") for i in range(2)]
        xgs = [k.sbuf([64, 8, 64], BF16, f"xg{i}") for i in range(2)]
        tmp = k.sbuf([64, 8, 64], F32, "tmp")
        ptb = [k.psum([64, 16, 64], BF16, f"ptb{i}") for i in range(2)]
        pg = [k.psum([128, 512], F32, f"pg{i}") for i in range(6)]
        cnt = {"pg": 0, "ev": 0, "ld": 0}

        def bank():
            p_ = pg[cnt["pg"] % 6]
            cnt["pg"] += 1
            return p_

        def evac(out, in_, reads, writes):
            e = "act" if cnt["ev"] % 2 == 0 else "dve"
            cnt["ev"] += 1
            k.copy(e, out, in_, reads, writes)

        k.dma("sp", R1s[0], g.HCd, [(R1s[0][:, :], g.HCd.t[1024:1024 + 64, :])])
        for cb in range(8):
            c0 = cb * 64
            R1 = R1s[cb % 2]
            for o in range(2):
                if o == 1 and cb + 1 < 8:
                    Rn = R1s[(cb + 1) % 2]
                    k.dma("sp", Rn, g.HCd, [(Rn[:, :], g.HCd.t[1024 + c0 + 64:1024 + c0 + 128, :])])
                NN = 64 if o == 0 else NQT
                for ng in range(8):
                    p_ = ptb[ng % 2]
                    for i in range(16):
                        n2 = ng * 16 + i
                        k.tr(p_[:, i, :], R1[:, n2 * 64:(n2 + 1) * 64], g.ident[0:64, 0:64], [R1, g.ident], [p_])
                    evac(uT[:, ng * 16:(ng + 1) * 16, :], p_[:, :, :], [p_], [BIG1])
                for cp in range(32):
                    p_ = bank()
                    pv = p_[:, :].rearrange("p (i r) -> p i r", r=256)
                    for i in range(2):
                        k.mm(pv[:, i, :], uT[:, :, 2 * cp + i], C1[:, :], True, True, [BIG1, C1], [p_])
                    evac(A[:, 2 * cp:2 * cp + 2, :], pv, [p_], [BIG2])
                for kg in range(16):
                    kr = krs[kg % 2]
                    ki = kis[kg % 2]
                    k.dma("sp", kr, g.KRd, [(kr[:, :, :], g.KRd.t[o, cb, :, kg * 8:(kg + 1) * 8, :])])
                    k.dma("sp", ki, g.KId, [(ki[:, :, :], g.KId.t[o, cb, :, kg * 8:(kg + 1) * 8, :])])
                    p_ = bank()
                    pv = p_[:, :].rearrange("p (i c) -> p i c", c=64)
                    for i in range(8):
                        k1 = kg * 8 + i
                        k.mm(pv[:, i, :], F2T[:, k1, 64:192], A[:, :, k1], True, False, [F2T, BIG2], [p_])
                        k.mm(pv[:, i, :], F2T[:, k1, 0:128], A[:, :, 128 + k1], False, True, [F2T, BIG2], [p_])
                    k.tt("dve", P1[:, kg * 8:(kg + 1) * 8, :], pv, kr[:, :, :], ALU.mult, [p_, kr], [BIG1])
                    k.tt("dve", P2[:, kg * 8:(kg + 1) * 8, :], pv, ki[:, :, :], ALU.mult, [p_, ki], [BIG1])
                for cp in range(32):
                    p_ = bank()
                    pv = p_[:, :].rearrange("p (i r) -> p i r", r=256)
                    for i in range(2):
                        c = 2 * cp + i
                        k.mm(pv[:, i, :], P1[:, :, c], R12[:, 0, :], True, False, [BIG1, R12], [p_])
                        k.mm(pv[:, i, :], P2[:, :, c], R12[:, 1, :], False, True, [BIG1, R12], [p_])
                    evac(B[:, 2 * cp:2 * cp + 2, :, :], p_[:, :].rearrange("p (i r n) -> p i r n", r=2, n=128), [p_], [BIG2])
                xrow = (0 if o == 0 else 512) + c0
                for ng in range(16):
                    i2 = i2s[ng % 2]
                    xg = xgs[ng % 2]
                    k.dma("sp", i2, g.d_I2T, [(i2[:, :, :], g.d_I2T.t[:, ng * 8:(ng + 1) * 8, :])])
                    k.dma("sp", xg, g.HCd, [(xg[:, :, :].rearrange("p a b -> p (a b)"), g.HCd.t[xrow:xrow + 64, ng * 512:(ng + 1) * 512])])
                    p_ = bank()
                    pv = p_[0:64, :].rearrange("p (i n) -> p i n", n=64)
                    for i in range(8):
                        n2 = ng * 8 + i
                        k.mm(pv[:, i, 0:NN], B[:, :, 0, n2], i2[:, i, 0:NN], True, False, [BIG2, i2], [p_])
                        k.mm(pv[:, i, 0:NN], B[:, :, 1, n2], i2[:, i, 64:64 + NN], False, True, [BIG2, i2], [p_])
                    zin = R1[:, ng * 512:(ng + 1) * 512].rearrange("p (a b) -> p a b", b=64)
                    k.stt(tmp[:, :, 0:NN], zin[:, :, 0:NN], skp[:, o, cb:cb + 1], pv[:, :, 0:NN], ALU.mult, ALU.add,
                          [R1, skp, p_], [tmp])
                    if o == 0:
                        k.tt("dve", zin, tmp[:, :, :], xg[:, :, :], ALU.mult, [tmp, xg], [R1])
                    else:
                        k.tt("dve", HY[:, :, ng * 8:(ng + 1) * 8].rearrange("p n a -> p a n"), tmp[:, :, 0:NN], xg[:, :, 0:NN],
                             ALU.mult, [tmp, xg], [HY])
            k.dma("pool", HY, g.HYd, [(g.HYd.t[c0:c0 + 64, :], HY[:, :, :].rearrange("p n a -> p (n a)"))], load=False)
        k.end_phase()
    k.stack = prev


def phase_B(k, g):
    prev = k.stack
    with ExitStack() as bs:
        k.stack = bs
        g.HF = k.sbuf([64, T], BF16, "HF")
        g.WSD = k.sbuf([64, 2, 8, 2, 64], BF16, "WSD")
        phase_BM(k, g)
        phase_B1(k, g)
        k.end_phase()
    k.stack = prev
    phase_B0(k, g)
    phase_B2(k, g)


def phase_C(k, g):
    prev = k.stack
    with ExitStack() as ps_:
        k.stack = ps_
        KT = k.sbuf([128, 2, NKEY], BF16, "KT")
        Vs = k.sbuf([128, NKT, 258], BF16, "Vs")
        wao = k.sbuf([128, 8, 1024], BF16, "wao")
        who = k.sbuf([128, 4, 1024], BF16, "who")
        wo = k.sbuf([128, 8, 1024], BF16, "wo")
        bg1 = k.sbuf([128, 1024], F32, "bg1")
        k.dma("sp", KT, g.Kd, [(KT[:, g_, :], g.Kd.t[:, g_, :]) for g_ in range(2)])
        k.dma("sp", Vs, g.Vd, [(Vs[:, t0:t0 + 22, :], g.Vd.t[t0:t0 + 22, :, :].rearrange("t p c -> p t c")) for t0 in (0, 22, 44)])
        k.dma("pool", wao, g.d_w_attn_out, [(wao[:, :, :], g.d_w_attn_out.t.rearrange("(kc p) n -> p kc n", p=128))])
        k.dma("pool", who, g.d_w_hy_out, [(who[:, :, :], g.d_w_hy_out.t.rearrange("(kc p) n -> p kc n", p=128))])
        k.dma("pool", wo, g.d_w_o, [(wo[:, :, :], g.d_w_o.t.rearrange("(kc p) n -> p kc n", p=128))])
        k.dma("sp", bg1, g.d_b_o, [(bg1[:, :], g.d_b_o[0:1024].partition_broadcast(128))])
        k.tt("dve", bg1[:, :], bg1[:, :], g.g1row[:, :], ALU.mult, [bg1, g.g1row], [bg1])
        Qts = [k.sbuf([128, 8, 128], BF16, f"Qt{i}") for i in range(2)]
        PTs = [k.sbuf([128, 512], BF16, f"PT{i}") for i in range(8)]
        recs = [k.sbuf([128, 512], F32, f"rec{i}") for i in range(2)]
        ssums = [k.sbuf([128, 512], F32, f"ssum{i}") for i in range(2)]
        ss2 = k.sbuf([128, 2], F32, "ss2")
        aoTs = [k.sbuf([128, 8, 128], BF16, f"aoT{i}") for i in range(2)]
        hyT = [k.sbuf([128, 4, 128], BF16, f"hyT{i}") for i in range(2)]
        sgs = [k.sbuf([128, 16, 128], BF16, f"sg{i}") for i in range(2)]
        xqs = [k.sbuf([128, 1024], F32, f"xq{i}") for i in range(2)]
        t1 = k.sbuf([128, 512], F32, "t1")
        t2 = k.sbuf([128, 512], F32, "t2")
        mixT = k.sbuf([128, 8, 128], BF16, "mixT")
        xnew = [k.sbuf([128, 1024], F32, f"xnew{i}") for i in range(2)]
        xn2 = k.sbuf([128, 1024], F32, "xn2")
        junk = k.sbuf([128, 1024], BF16, "junkc")
        ss = k.sbuf([128, 1], F32, "ssc")
        rs = k.sbuf([128, 1], F32, "rsc")
        h2T = [k.sbuf([128, 8, 128], BF16, f"h2T{i}") for i in range(2)]
        psb = [k.psum([128, 512], F32, f"ps{i}") for i in range(3)]
        pacc = [k.psum([128, 512], F32, f"pacc{i}") for i in range(2)]
        psum_s = k.psum([128, 512], F32, "psum_s")
        pm = [k.psum([128, 512], F32, f"pm{i}") for i in range(2)]
        cnt = {"s": 0}
        scale = 128.0 ** -0.5
        H2v = g.H2d.t.rearrange("kc p t -> p kc t")
        HYv = g.HYd.t.rearrange("(j p) t -> p j t", p=128)
        SGv = g.SGd.t.rearrange("(j p) t -> p j t", p=128)

        def loadQ(qi):
            Qt = Qts[qi % 2]
            k.dma("sp", Qt, g.Qd, [(Qt[:, :, :].rearrange("p h q -> p (h q)"), g.Qd.t[qi, :, :])])

        def loadM(qi):
            k.dma("sp", hyT[qi % 2], g.HYd, [(hyT[qi % 2][:, :, :], HYv[:, :, qi * 128:(qi + 1) * 128])])
            k.dma("sp", sgs[qi % 2], g.SGd, [(sgs[qi % 2][:, :, :], SGv[:, :, qi * 128:(qi + 1) * 128])])
            k.dma("sp", xqs[qi % 2], g.d_xl, [(xqs[qi % 2][:, :], g.d_xl.t[qi * 128:(qi + 1) * 128, :])])

        pend = {"q": [], "prevPT": None, "npair": 0}
        onesb = k.sbuf([128, 128], BF16, "onesb")
        k.op("dve", lambda e: e.memset(onesb[:, :], 1.0), [], [onesb])
        pairs = [k.sbuf([128, 512], BF16, f"pair{i}") for i in range(6)]

        def emit_pv(p):
            qi, g_, kt, PT = p
            pa_ = pacc[g_]
            k.mm(pa_[:, :], Vs[:, kt, g_ * 129:g_ * 129 + 128], PT[:, :], kt == 0, kt == NKT - 1, [PT, Vs], [pa_])
            if kt % 2 == 0:
                pend["prevPT"] = PT
            else:
                PTa = pend["prevPT"]
                pb = pairs[pend["npair"] % 6]
                pend["npair"] += 1
                k.tt("dve", pb[:, :], PTa[:, :], PT[:, :], ALU.add, [PTa, PT], [pb])
                if kt == NKT - 1:
                    sumq.append((pb, kt))
                elif kt % 4 == 1:
                    pend["prevpair"] = pb
                else:
                    pa2 = pend["prevpair"]
                    k.tt("dve", pb[:, :], pa2[:, :], pb[:, :], ALU.add, [pa2, pb], [pb])
                    sumq.append((pb, kt))
            flush_sums(keep=0 if kt == NKT - 1 else 2)
            if kt == NKT - 1:
                gens.append(norm_gen(qi, g_))

        sumq = []

        def flush_sums(keep):
            while len(sumq) > keep:
                pb, kt = sumq.pop(0)
                k.mm(psum_s[:, :], onesb[:, :], pb[:, :], kt == 3, kt == NKT - 1, [onesb, pb], [psum_s])

        def flush_pv(keep=0):
            while len(pend["q"]) > keep:
                emit_pv(pend["q"].pop(0))

        gens = []
        ncnt = {"n": 0}

        def norm_gen(qi, g_):
            sm = ssums[ncnt["n"] % 2]
            rc = recs[ncnt["n"] % 2]
            ncnt["n"] += 1
            pa_ = pacc[g_]
            aT = aoTs[qi % 2]
            k.copy("dve", sm[:, :], psum_s[:, :], [psum_s], [sm])
            yield
            for j in range(4):
                sl = slice(j * 128, (j + 1) * 128)
                k.op("dve", lambda e, sl=sl: e.reciprocal(out=rc[:, sl], in_=sm[:, sl]), [sm], [rc])
                yield
                k.tt("dve", aT[:, 4 * g_ + j, :], pa_[:, sl], rc[:, sl], ALU.mult, [pa_, rc], [aT])
                yield

        def pump():
            for gen in list(gens):
                try:
                    next(gen)
                except StopIteration:
                    gens.remove(gen)

        def drain():
            while gens:
                pump()

        def attn_group(qi, g_):
            Qt = Qts[qi % 2]
            rhsq = Qt[:, 4 * g_:4 * g_ + 4, :].rearrange("p h q -> p (h q)")
            for kt in range(NKT):
                p_ = psb[cnt["s"] % 3]
                PT = PTs[cnt["s"] % 8]
                cnt["s"] += 1
                k.mm(p_[:, :], KT[:, g_, kt * 128:(kt + 1) * 128], rhsq, True, True, [KT, Qt], [p_])
                k.act(PT[:, :], p_[:, :], AF.Exp, [p_], [PT], scale=scale)
                flush_pv(keep=1)
                pend["q"].append((qi, g_, kt, PT))
                if g_ == 0 and kt == 12 and qi >= 1:
                    gens.append(merge_gen(qi - 1, qi + 1 if qi + 1 < NQT else None))
                pump()

        def merge_gen(qi, next_load=None):
            aoT = aoTs[qi % 2]
            sg = sgs[qi % 2]
            hy = hyT[qi % 2]
            xq = xqs[qi % 2]
            xnw = xnew[qi % 2]
            for r in range(2):
                pA = pm[0][:, :].rearrange("p (j q) -> p j q", q=128)
                pH = pm[1][:, :].rearrange("p (j q) -> p j q", q=128)
                for i in range(4):
                    j = 4 * r + i
                    for kc in range(8):
                        k.mm(pA[:, i, :], wao[:, kc, j * 128:(j + 1) * 128], aoT[:, kc, :], kc == 0, kc == 7, [wao, aoT], [pm[0]])
                    yield
                for i in range(4):
                    j = 4 * r + i
                    for kc in range(4):
                        k.mm(pH[:, i, :], who[:, kc, j * 128:(j + 1) * 128], hy[:, kc, :], kc == 0, kc == 3, [who, hy], [pm[1]])
                    yield
                k.tt("dve", t1[:, :].rearrange("p (j q) -> p j q", q=128), pA, sg[:, 4 * r:4 * r + 4, :], ALU.mult, [pm[0], sg], [t1])
                yield
                k.tt("dve", t2[:, :].rearrange("p (j q) -> p j q", q=128), pH, sg[:, 8 + 4 * r:12 + 4 * r, :], ALU.mult, [pm[1], sg], [t2])
                yield
                k.tt("dve", mixT[:, 4 * r:4 * r + 4, :], t1[:, :].rearrange("p (j q) -> p j q", q=128),
                     t2[:, :].rearrange("p (j q) -> p j q", q=128), ALU.add, [t1, t2], [mixT])
                yield
            k.tt("pool", xq[:, :], xq[:, :], bg1[:, :], ALU.add, [xq, bg1], [xq])
            for half in range(2):
                for j in range(8):
                    k.mm(pm[half][:, :], mixT[:, j, :], wo[:, j, half * 512:(half + 1) * 512], j == 0, j == 7, [mixT, wo], [pm[half]])
                    if j % 2 == 1:
                        yield
            for half in range(2):
                sl = slice(half * 512, (half + 1) * 512)
                k.tt("dve", t1[:, :], pm[half][:, :], g.g1row[:, sl], ALU.mult, [pm[half], g.g1row], [t1])
                yield
                k.tt("dve", xnw[:, sl], t1[:, :], xq[:, sl], ALU.add, [t1, xq], [xnw])
                yield
            k.dma("pool", xnw, g.XNd, [(g.XNd.t[qi * 128:(qi + 1) * 128, :], xnw[:, :])], load=False)
            for half in range(2):
                sl = slice(half * 512, (half + 1) * 512)
                k.tt("pool", xn2[:, sl], xnw[:, sl], xnw[:, sl], ALU.mult, [xnw], [xn2])
                yield
                k.op("dve", lambda e, sl=sl, half=half: e.tensor_reduce(out=ss2[:, half:half + 1], in_=xn2[:, sl], op=ALU.add, axis=AX.X),
                     [xn2], [ss2])
                yield
            k.tt("dve", ss[:, :], ss2[:, 0:1], ss2[:, 1:2], ALU.add, [ss2], [ss])
            yield
            k.act(rs[:, :], ss[:, :], AF.Ln, [ss, g.epsc], [rs], scale=1.0 / D, bias=g.epsc[:, 0:1])
            k.act(rs[:, :], rs[:, :], AF.Exp, [rs], [rs], scale=-0.5)
            yield
            k.act(xn2[:, :], xnw[:, :], AF.Identity, [xnw, rs], [xn2], scale=rs[:, 0:1])
            yield
            hT_ = h2T[qi % 2]
            for r in range(2):
                pv = pm[r][:, :].rearrange("p (h q) -> p h q", q=128)
                for i in range(4):
                    kc = 4 * r + i
                    k.tr(pv[:, i, :], xn2[:, kc * 128:(kc + 1) * 128], g.identf[:, :], [xn2, g.identf], [pm[r]])
                yield
                for i in range(4):
                    kc = 4 * r + i
                    k.ts("dve", hT_[:, kc, :], pv[:, i, :], g.mods[:, 4, kc:kc + 1], g.mods[:, 5, kc:kc + 1], ALU.mult, ALU.add,
                         [pm[r], g.mods], [hT_])
                yield
            k.dma("pool", hT_, g.H2d, [(H2v[:, :, qi * 128:(qi + 1) * 128], hT_[:, :, :])], load=False)
            if next_load is not None:
                loadM(next_load)

        loadQ(0)
        loadM(0)
        loadM(1)
        for qi in range(NQT):
            if qi + 1 < NQT:
                loadQ(qi + 1)
            attn_group(qi, 0)
            attn_group(qi, 1)
            drain()
        flush_pv()
        drain()
        gens.append(merge_gen(NQT - 1))
        drain()
        k.end_phase()
    k.stack = prev


def phase_D(k, g):
    prev = k.stack
    with ExitStack() as ps_:
        k.stack = ps_
        wup = k.sbuf([128, 8, 2 * DFF], BF16, "wup")
        wdn = k.sbuf([128, 22, 1024], BF16, "wdn")
        upv = g.d_w_up.t.rearrange("(kc p) n -> p kc n", p=128)
        for q4 in range(4):
            lo, hi = q4 * 704, (q4 + 1) * 704
            k.dma("pool", wup, g.d_w_up, [(wup[:, :, lo:hi], upv[:, :, lo:hi]),
                                          (wup[:, :, DFF + lo:DFF + hi], upv[:, :, DFF + lo:DFF + hi])])
        k.dma("pool", wdn, g.d_w_down, [(wdn[:, :, :], g.d_w_down.t.rearrange("(j p) n -> p j n", p=128))])
        bg2 = k.sbuf([128, 1024], F32, "bg2")
        fnw = k.sbuf([128, 1024], F32, "fnw")
        k.dma("sp", bg2, g.d_b_down, [(bg2[:, :], g.d_b_down[0:1024].partition_broadcast(128))])
        k.dma("sp", fnw, g.d_final_norm_w, [(fnw[:, :], g.d_final_norm_w[0:1024].partition_broadcast(128))])
        k.tt("dve", bg2[:, :], bg2[:, :], g.g2row[:, :], ALU.mult, [bg2, g.g2row], [bg2])
        h2b = [k.sbuf([128, 8, 514], BF16, f"h2b{i}") for i in range(1)]
        uas = [k.sbuf([128, 514], F32, f"ua{i}") for i in range(2)]
        ugs = [k.sbuf([128, 514], F32, f"ug{i}") for i in range(2)]
        cas = [k.sbuf([128, 512], F32, f"ca{i}") for i in range(2)]
        cgs = [k.sbuf([128, 512], F32, f"cg{i}") for i in range(2)]
        actT = k.sbuf([128, 22, 512], BF16, "actT")
        xns = [k.sbuf([128, 1024], F32, f"xnd{i}") for i in range(1)]
        ys = [k.sbuf([128, 1024], F32, f"y{i}") for i in range(1)]
        t1 = k.sbuf([128, 512], F32, "t1d")
        ss = k.sbuf([128, 1], F32, "ssd")
        rs = k.sbuf([128, 1], F32, "rsd")
        pa = [k.psum([128, 512], F32, f"pa{i}") for i in range(2)]
        pg_ = [k.psum([128, 512], F32, f"pgd{i}") for i in range(2)]
        phs = [k.psum([128, 512], F32, f"ph{i}") for i in range(2)]
        cen = k.sbuf([128, 44], F32, "cen")
        k.tt("dve", cen[:, :], g.vec[:, vcol("fcw1"):vcol("fcw1") + 44], g.vec[:, vcol("b_up"):vcol("b_up") + 44], ALU.mult, [g.vec], [cen])
        k.tt("dve", cen[:, :], cen[:, :], g.vec[:, vcol("fcb"):vcol("fcb") + 44], ALU.add, [cen, g.vec], [cen])
        pd = [k.psum([128, 512], F32, f"pd{i}") for i in range(2)]
        H2v = g.H2d.t.rearrange("kc p t -> p kc t")
        NB = 8
        tcount = [0]
        for bi in range(NB):
            T0 = bi * 512
            hb = h2b[0]
            if bi == 0:
                k.dma("sp", hb, g.H2d, [(hb[:, :, 1:514], H2v[:, :, 0:513])])
                k.op("pool", lambda e, hb=hb: e.memset(hb[:, :, 0:1], 0.0), [], [hb])
            else:
                k.dma("sp", hb, g.H2d, [(hb[:, :, :], H2v[:, :, T0 - 1:T0 + 513])])
            for j in range(22):
                p_a, p_g = pa[j % 2], pg_[j % 2]
                ua, ug = uas[j % 2], ugs[j % 2]
                ca, cg = cas[j % 2], cgs[j % 2]
                for kc in range(8):
                    k.mm(p_a[:, :], wup[:, kc, j * 128:(j + 1) * 128], hb[:, kc, 1:513], kc == 0, kc == 7, [wup, hb], [p_a])
                for kc in range(8):
                    k.mm(p_g[:, :], wup[:, kc, DFF + j * 128:DFF + (j + 1) * 128], hb[:, kc, 1:513], kc == 0, kc == 7, [wup, hb], [p_g])
                ph = phs[j % 2]
                po = (j % 2) * 8
                for kc in range(8):
                    k.mm(ph[:, po:po + 2], wup[:, kc, j * 128:(j + 1) * 128], hb[:, kc, 0:514:513], kc == 0, kc == 7, [wup, hb], [ph])
                for kc in range(8):
                    k.mm(ph[:, po + 2:po + 4], wup[:, kc, DFF + j * 128:DFF + (j + 1) * 128], hb[:, kc, 0:514:513], kc == 0, kc == 7, [wup, hb], [ph])
                ba = g.vec[:, vcol("b_up", j):vcol("b_up", j) + 1]
                bg_ = g.vec[:, vcol("b_up", 22 + j):vcol("b_up", 22 + j) + 1]
                k.act(ua[:, 1:513], p_a[:, :], AF.Identity, [p_a, g.vec], [ua], bias=ba)
                k.ts("dve", ua[:, 0:514:513], ph[:, po:po + 2], ba, None, ALU.add, ALU.bypass, [ph, g.vec], [ua])
                k.act(ug[:, 1:513], p_g[:, :], AF.Identity, [p_g, g.vec], [ug], bias=bg_)
                k.ts("dve", ug[:, 0:514:513], ph[:, po + 2:po + 4], bg_, None, ALU.add, ALU.bypass, [ph, g.vec], [ug])
                if bi == 0:
                    k.op("dve", lambda e, ua=ua: e.memset(ua[:, 0:1], 0.0), [], [ua])
                    k.op("dve", lambda e, ug=ug: e.memset(ug[:, 0:1], 0.0), [], [ug])
                for (u_, c_, jj, pp_) in ((ua, ca, j, p_a), (ug, cg, 22 + j, p_g)):
                    w0 = g.vec[:, vcol("fcw0", jj):vcol("fcw0", jj) + 1]
                    w1 = g.vec[:, vcol("fcw1", jj):vcol("fcw1", jj) + 1]
                    w2 = g.vec[:, vcol("fcw2", jj):vcol("fcw2", jj) + 1]
                    k.act(c_[:, :], pp_[:, :], AF.Identity, [pp_, g.vec, cen], [c_], scale=w1, bias=cen[:, jj:jj + 1])
                    k.stt(c_[:, :], u_[:, 0:512], w0, c_[:, :], ALU.mult, ALU.add, [u_, g.vec, c_], [c_])
                    k.stt(c_[:, :], u_[:, 2:514], w2, c_[:, :], ALU.mult, ALU.add, [u_, g.vec, c_], [c_])
                k.act(ca[:, :], ca[:, :], AF.Gelu_apprx_tanh, [ca], [ca])
                k.tt("dve", actT[:, j, :], ca[:, :], cg[:, :], ALU.mult, [ca, cg], [actT])
            for tt_ in range(4):
                qi = bi * 4 + tt_
                xn_ = xns[0]
                y = ys[0]
                tcount[0] += 1
                k.dma("sp", xn_, g.XNd, [(xn_[:, :], g.XNd.t[qi * 128:(qi + 1) * 128, :])])
                k.tt("pool", xn_[:, :], xn_[:, :], bg2[:, :], ALU.add, [xn_, bg2], [xn_])
                for half in range(2):
                    for j in range(22):
                        k.mm(pd[half][:, :], actT[:, j, tt_ * 128:(tt_ + 1) * 128], wdn[:, j, half * 512:(half + 1) * 512],
                             j == 0, j == 21, [actT, wdn], [pd[half]])
                for half in range(2):
                    sl = slice(half * 512, (half + 1) * 512)
                    k.tt("dve", t1[:, :], pd[half][:, :], g.g2row[:, sl], ALU.mult, [pd[half], g.g2row], [t1])
                    k.tt("dve", y[:, sl], t1[:, :], xn_[:, sl], ALU.add, [t1, xn_], [y])
                k.act(t1[:, :].bitcast(BF16), y[:, :], AF.Square, [y], [t1, ss], accum_out=ss[:, 0:1])
                k.act(rs[:, :], ss[:, :], AF.Sqrt, [ss, g.epsc], [rs], scale=1.0 / D, bias=g.epsc[:, 0:1])
                k.op("dve", lambda e: e.reciprocal(out=rs[:, :], in_=rs[:, :]), [rs], [rs])
                k.act(xn_[:, :], y[:, :], AF.Identity, [y, rs], [xn_], scale=rs[:, 0:1])
                k.tt("pool", y[:, :], xn_[:, :], fnw[:, :], ALU.mult, [xn_, fnw], [y])
                k.dma("sp", y, g.out, [(g.out.t[qi * 128:(qi + 1) * 128, :], y[:, :])], load=False)
        k.end_phase()
    k.stack = prev


INPUT_SPECS = [
    ("xl", [T, D], F32), ("ctx", [NCTX, D], F32), ("cvecT", [128, 8, 2], F32),
    ("w_mod", [D, 6 * D], F32), ("b_mod", [6 * D], F32), ("vec", [128, NV], F32),
    ("w_in", [D, 5120], F32), ("b_in", [5120], F32), ("qkw", [1280], F32),
    ("rope", [T, 2, 64], F32), ("ident", [128, 128], BF16),
    ("zT", [33, T], F32), ("filt_w1", [33, 64], F32), ("filt_wi", [64, 2, 64], F32), ("filt_vec", [64, 8], F32),
    ("filt_w_out", [64, 2048], F32), ("skipT", [64, 2, 8], F32), ("decay", [64, 128, 512], F32),
    ("C1", [64, 256], BF16), ("F2T", [128, 128, 192], BF16), ("F2D", [128, 128, 384], BF16), ("R12", [128, 2, 256], BF16), ("I2T", [128, 128, 128], BF16),
    ("identf", [128, 128], F32), ("w_attn_out", [D, D], F32), ("w_hy_out", [512, D], F32), ("w_o", [D, D], F32),
    ("b_o", [D], F32), ("w_up", [D, 2 * DFF], F32), ("w_down", [DFF, D], F32), ("b_down", [D], F32),
    ("final_norm_w", [D], F32),
]


def build(debug=None):
    nc = bass.Bass("TRN2", target_bir_lowering=False)
    with ExitStack() as st:
        k = KB(nc, st)
        g = G()
        for name, shape, dt in INPUT_SPECS:
            setattr(g, "d_" + name, k.dram(name, shape, dt, kind="ExternalInput"))
        dk = "ExternalOutput" if debug == "A" else None
        dkb = "ExternalOutput" if debug == "B" else None
        g.HCd = k.dram("HCd", [1536, T], BF16, kind=dkb)
        g.KRd = k.dram("KRd", [2, 8, 128, 128, 64], BF16, kind=dkb)
        g.KId = k.dram("KId", [2, 8, 128, 128, 64], BF16, kind=dkb)
        g.HYd = k.dram("HYd", [512, NQ], BF16, kind=dkb)
        dkc = "ExternalOutput" if debug == "C" else None
        g.XNd = k.dram("XNd", [NQ, D], F32, kind=dkc)
        g.H2d = k.dram("H2d", [8, 128, NQ], BF16, kind=dkc)
        g.out = k.dram("out", [4096, D], F32, kind="ExternalOutput")
        g.identf = k.sbuf([128, 128], F32, "identf")
        k.dma("sp", g.identf, g.d_identf, [(g.identf[:, :], g.d_identf[:, :])])
        g.Kd = k.dram("Kd", [128, 2, NKEY], BF16, kind=dk)
        g.Vd = k.dram("Vd", [NKT, 128, 258], BF16, kind=dk)
        g.Qd = k.dram("Qd", [NQT, 128, 1024], BF16, kind=dk)
        g.HSd = k.dram("HSd", [1536, T], BF16, kind=dk)
        g.SGd = k.dram("SGd", [2048, NQ], BF16, kind=dk)
        g.epsc = k.sbuf([128, 1], F32, "epsc")
        k.op("dve", lambda e: e.memset(g.epsc[:, :], EPS), [], [g.epsc])
        phase_0(k, g)
        phase_A(k, g)
        phase_B(k, g)
        phase_C(k, g)
        phase_D(k, g)
        k.barrier()
        print("ninst", k.ninst, {e: c for e, c in k.cnt.items()})
    return nc


def _fm(v):
    v = np.asarray(v, np.float32)
    return np.ascontiguousarray(v.reshape(-1, 128).T)


def rope_tables():
    rows = T // 64
    row = np.repeat(np.arange(rows), 64).astype(np.float32)
    col = np.tile(np.arange(64), rows).astype(np.float32)
    freqs = (10000.0 ** (-np.arange(0, 64, 2, dtype=np.float32) / 64)).astype(np.float32)
    ang = np.concatenate([row[:, None] * freqs, col[:, None] * freqs], axis=-1).astype(np.float32)
    return np.stack([np.cos(ang), np.sin(ang)], axis=1).astype(np.float32)


_CONST = {}


def const_tables():
    if _CONST:
        return _CONST
    bf = ml_dtypes.bfloat16
    N = 2 * T
    n1 = np.arange(64, dtype=np.float64)
    n2 = np.arange(128, dtype=np.float64)
    k1 = np.arange(128, dtype=np.float64)
    k2 = np.arange(64, dtype=np.float64)
    phi = 2 * np.pi * np.outer(n1, k1 + 0.5) / 128
    _CONST["C1"] = np.concatenate([np.cos(phi), -np.sin(phi)], 1).astype(bf)
    psi = 2 * np.pi * (n2[:, None, None] * (k1[None, :, None] + 0.5) / N + n2[:, None, None] * k2[None, None, :] / 128)
    Hr, Hi = np.cos(psi), -np.sin(psi)
    _CONST["F2T"] = np.ascontiguousarray(np.concatenate([-Hi, Hr, Hi], 2).astype(bf))
    _CONST["F2D"] = np.ascontiguousarray(np.concatenate([-Hi, -Hi, Hr, Hr, Hi, Hi], 2).astype(bf))
    th = 2 * np.pi * np.outer(k2, n2) / 128
    Cr, Ci = np.cos(th), np.sin(th)
    S1 = np.concatenate([Cr, -Ci], 0); S2 = np.concatenate([-Ci, -Cr], 0); S3 = np.concatenate([Ci, Cr], 0)
    _CONST["R12"] = np.ascontiguousarray(np.stack([np.concatenate([S1, S3], 1), np.concatenate([S2, S1], 1)], 1).astype(bf))
    gg = 2 * np.pi * (n2[None, :, None] * (k1[:, None, None] + 0.5) / N + n1[None, None, :] * (k1[:, None, None] + 0.5) / 128)
    _CONST["I2T"] = np.ascontiguousarray(np.concatenate([np.cos(gg) * 2 / N, -np.sin(gg) * 2 / N], 2).astype(bf))
    L = T
    t_idx = np.arange(L, dtype=np.float32)
    tt = t_idx / np.float32(L - 1)
    f = np.linspace(1e-4, 15, 16, dtype=np.float32)
    wpos = (np.float32(2.0 * math.pi) * t_idx / np.float32(L)).astype(np.float32)
    z = np.concatenate([tt[:, None], np.cos(wpos[:, None] * f), -np.sin(wpos[:, None] * f)], -1).astype(np.float32)
    order = (np.arange(64)[None, :] * 128 + np.arange(128)[:, None]).reshape(-1)
    _CONST["zT"] = np.ascontiguousarray(z[order].T)
    min_decay = math.log(1e-2) / 1.5
    max_decay = math.log(1e-2) / 0.3
    deltas = np.linspace(min_decay, max_decay, 512, dtype=np.float32)
    dec = np.exp(-tt[:, None] * np.abs(deltas)).astype(np.float32)
    _CONST["decay"] = np.ascontiguousarray(dec.reshape(64, 128, 512))
    return _CONST


def host_inputs(inp, core):
    b, hf = core // 2, core % 2
    rev = hf == 1
    f32 = lambda a: np.ascontiguousarray(np.asarray(a, np.float32))
    x = np.asarray(inp["x"][b], np.float32)
    rope = rope_tables()
    hcw = np.asarray(inp["hy_conv_w"][0], np.float32)
    fcw = np.asarray(inp["ffn_conv_w"][0], np.float32)
    if rev:
        x = x[::-1]
        rope = rope[::-1]
        hcw = hcw[::-1]
        fcw = fcw[::-1]
    cvec = np.stack([np.asarray(inp["c"][b], np.float32), np.asarray(inp["c_ctx"], np.float32)], 0)
    cvecT = np.ascontiguousarray(cvec.reshape(2, 8, 128).transpose(2, 1, 0))
    vec = np.zeros((128, NV), np.float32)

    def put(name, v):
        o, w = VEC_COLS[name]
        vec[:, o:o + w] = _fm(v)
    put("norm1_w", inp["norm1_w"][0]); put("norm2_w", inp["norm2_w"][0]); put("b_in", inp["b_in"][0])
    put("hcw0", hcw[0]); put("hcw1", hcw[1]); put("hcw2", hcw[2]); put("hcb", inp["hy_conv_b"][0])
    put("b_up", inp["b_up"][0]); put("fcw0", fcw[0]); put("fcw1", fcw[1]); put("fcw2", fcw[2])
    put("fcb", inp["ffn_conv_b"][0]); put("b_mod", inp["b_mod"][0])
    qkw = np.concatenate([np.tile(np.asarray(inp["q_norm_w"][0], np.float32), 8),
                          np.tile(np.asarray(inp["k_norm_w"][0], np.float32), 2)])
    m = {
        "xl": f32(x), "ctx": f32(inp["ctx"][b]), "cvecT": f32(cvecT),
        "w_mod": f32(inp["w_mod"][0]), "b_mod": f32(inp["b_mod"][0]), "vec": vec,
        "w_in": f32(inp["w_in"][0]), "b_in": f32(inp["b_in"][0]), "qkw": f32(qkw),
        "rope": f32(rope), "ident": np.eye(128, dtype=np.float32).astype(ml_dtypes.bfloat16),
    }
    m["identf"] = np.eye(128, dtype=np.float32)
    for nm_ in ("w_attn_out", "w_hy_out", "w_o", "b_o", "w_up", "w_down", "b_down"):
        m[nm_] = f32(inp[nm_][0])
    m["final_norm_w"] = f32(inp["final_norm_w"])
    ct = const_tables()
    for nm in ("zT", "decay", "C1", "F2T", "F2D", "R12", "I2T"):
        m[nm] = ct[nm]
    wout = np.asarray(inp["filt_w_out"][0], np.float32)
    if rev:
        wout = np.concatenate([wout[:, 1024:], wout[:, :1024]], 1)
    fvec = np.zeros((64, 8), np.float32)
    fvec[:, 0] = inp["filt_freq"][0]; fvec[:, 1] = inp["filt_b1"][0]
    fvec[:, 2] = inp["filt_b_inner"][0][0]; fvec[:, 3] = inp["filt_b_inner"][0][1]
    m["filt_w1"] = f32(inp["filt_w1"][0])
    m["filt_wi"] = f32(np.asarray(inp["filt_w_inner"][0], np.float32).transpose(1, 0, 2))
    m["filt_vec"] = fvec
    m["filt_w_out"] = f32(wout)
    m["skipT"] = f32(np.asarray(inp["hy_skip"][0], np.float32).reshape(2, 8, 64).transpose(2, 0, 1))
    return m


def kernel(**inputs):
    nc = build()
    in_maps = [host_inputs(inputs, c) for c in range(8)]
    res = run_bass_kernel_spmd(nc, in_maps, core_ids=list(range(8)))
    out = np.zeros((4, T, D), np.float32)
    for c in range(8):
        b, hf = c // 2, c % 2
        y = np.asarray(res.results[c]["out"], np.float32)
        if hf == 0:
            out[b, :4096] = y
        else:
            out[b, 4096:] = y[::-1]
    return out
```
